# Optimizing a Trainium2 kernel written in Bass

```python
import math
import jax, jax.numpy as jnp
from jax import lax
import numpy as np

D_MODEL = 1024
BATCH = 16
SEQ = 2048
DEPTH = 4

GRID_W = 64
CTX_LEN = 256

ATT_HEADS = 8
ATT_HEAD_DIM = 64
ATT_QK_DIM = ATT_HEADS * 2 * ATT_HEAD_DIM
ATT_V_DIM = ATT_HEADS * 2 * ATT_HEAD_DIM
Q_BLOCK = 128
ROPE_BASE = 10000.0
ROPE_PAIRS = ATT_HEAD_DIM // 4

SSD_D_INNER = 2 * D_MODEL
SSD_HEAD_DIM = 64
SSD_HEADS = SSD_D_INNER // SSD_HEAD_DIM
SSD_GROUPS = 8
SSD_STATE = 128
SSD_CONV = 5
SSD_CHUNK = 128
SSD_CONV_DIM = SSD_D_INNER + 2 * SSD_GROUPS * SSD_STATE

D_FF = 4 * D_MODEL

N_BRANCH = 2
N_MOD = 6
IN_DIM = 2 * ATT_QK_DIM + ATT_V_DIM + SSD_D_INNER + SSD_CONV_DIM + 2 * SSD_HEADS + N_BRANCH * D_MODEL

kernel_name = 'hybrid_diffattn_ssd_dit_trunk'


def rmsnorm(x, g, eps=1e-6):
    xf = x.astype(jnp.float32)
    y = xf * lax.rsqrt(jnp.mean(xf * xf, axis=-1, keepdims=True) + eps)
    return (y * g.astype(jnp.float32)).astype(x.dtype)


def modulate(x, g, shift, scale):
    return rmsnorm(x, g) * (1 + scale) + shift


def in_proj_split(t):
    sizes = (ATT_QK_DIM, ATT_QK_DIM, ATT_V_DIM, SSD_D_INNER, SSD_CONV_DIM,
             SSD_HEADS, SSD_HEADS, D_MODEL, D_MODEL)
    idx, acc = [], 0
    for s in sizes[:-1]:
        acc += s
        idx.append(acc)
    return jnp.split(t, idx, axis=-1)


def axial_rope(n_tok):
    rows = n_tok // GRID_W
    row = jnp.broadcast_to(jnp.arange(rows)[:, None], (rows, GRID_W)).reshape(-1).astype(jnp.float32)
    col = jnp.broadcast_to(jnp.arange(GRID_W)[None, :], (rows, GRID_W)).reshape(-1).astype(jnp.float32)
    inv = jnp.float32(ROPE_BASE) ** (-jnp.arange(ROPE_PAIRS, dtype=jnp.float32) / ROPE_PAIRS)
    ang = jnp.concatenate([row[:, None] * inv, col[:, None] * inv], axis=-1)
    return jnp.cos(ang), jnp.sin(ang)


def apply_rope(x, cos, sin):
    half = x.shape[-1] // 2
    x1 = x[..., :half].astype(jnp.float32)
    x2 = x[..., half:].astype(jnp.float32)
    c = cos[:, None, None, :]
    s = sin[:, None, None, :]
    return jnp.concatenate([x1 * c - x2 * s, x1 * s + x2 * c], axis=-1).astype(x.dtype)


def diff_attend(q, k, v, lam):
    s = jnp.einsum('bqhcd,bkhcd->bchqk', q, k).astype(jnp.float32) * (ATT_HEAD_DIM ** -0.5)
    p = jax.nn.softmax(s, axis=-1)
    a = p[:, 0] - lam * p[:, 1]
    return jnp.einsum('bhqk,bkhe->bqhe', a.astype(v.dtype), v)


def diff_attention(q_l, k_l, v_l, q_c, k_c, v_c, lam, lam_init, subln_g, with_ctx_out):
    b, n, h = q_l.shape[:3]
    k_all = jnp.concatenate([k_l, k_c], axis=1)
    v_all = jnp.concatenate([v_l, v_c], axis=1)
    nb = n // Q_BLOCK
    qb = jnp.moveaxis(q_l.reshape(b, nb, Q_BLOCK, h, 2, ATT_HEAD_DIM), 1, 0)
    o_l = lax.map(lambda qq: diff_attend(qq, k_all, v_all, lam), qb)
    o_l = jnp.moveaxis(o_l, 0, 1).reshape(b, n, h, 2 * ATT_HEAD_DIM)

    def post(o):
        return (rmsnorm(o, subln_g) * (1.0 - lam_init)).reshape(o.shape[0], o.shape[1], ATT_V_DIM)

    o_c = post(diff_attend(q_c, k_c, v_c, lam)) if with_ctx_out else None
    return post(o_l), o_c


def dwconv_silu(x, w, bias):
    kw = w.shape[0]
    y = lax.conv_general_dilated(x, w.astype(x.dtype)[:, None, :], window_strides=(1,),
                                 padding=[(kw // 2, kw // 2)],
                                 dimension_numbers=('NWC', 'WIO', 'NWC'),
                                 feature_group_count=x.shape[-1])
    return jax.nn.silu(y + bias)


def ssd_chunked(x, dt, a, bm, cm, h0, need_y):
    f32 = jnp.float32
    out_dtype = x.dtype
    b, n, h, p = x.shape
    g, ns = bm.shape[2], bm.shape[3]
    r = h // g
    nc = n // SSD_CHUNK
    dt = dt.astype(f32)
    xc = (x.astype(f32) * dt[..., None]).reshape(b, nc, SSD_CHUNK, g, r, p)
    cs = jnp.cumsum((dt * a.astype(f32)).reshape(b, nc, SSD_CHUNK, g, r), axis=2)
    bc = bm.astype(f32).reshape(b, nc, SSD_CHUNK, g, ns)
    cc = cm.astype(f32).reshape(b, nc, SSD_CHUNK, g, ns)
    decay_to_end = jnp.exp(cs[:, :, -1:] - cs)
    chunk_states = jnp.einsum('bclgn,bclgr,bclgrp->bcgrpn', bc, decay_to_end, xc)
    chunk_decay = jnp.exp(cs[:, :, -1])

    def step(hc, inp):
        st, dec = inp
        return hc * dec[..., None, None] + st, hc

    h_last, h_in = lax.scan(step, h0.astype(f32),
                            (jnp.moveaxis(chunk_states, 1, 0), jnp.moveaxis(chunk_decay, 1, 0)))
    if not need_y:
        return None, h_last
    h_in = jnp.moveaxis(h_in, 0, 1)
    seg = cs[:, :, :, None] - cs[:, :, None, :]
    lower = jnp.tril(jnp.ones((SSD_CHUNK, SSD_CHUNK), dtype=bool))[:, :, None, None]
    lmat = jnp.exp(jnp.where(lower, seg, -jnp.inf))
    cb = jnp.einsum('bclgn,bcsgn->bclsg', cc, bc)
    y = (jnp.einsum('bclsg,bclsgr,bcsgrp->bclgrp', cb, lmat, xc)
         + jnp.einsum('bclgn,bcgrpn,bclgr->bclgrp', cc, h_in, jnp.exp(cs)))
    return y.reshape(b, n, h, p).astype(out_dtype), h_last


def ssd_inputs(xbc_raw, dtf_raw, dtb_raw, conv_w, conv_b, dt_bias_f, dt_bias_b):
    xbc = dwconv_silu(xbc_raw, conv_w, conv_b)
    xs, bm, cm = jnp.split(xbc, [SSD_D_INNER, SSD_D_INNER + SSD_GROUPS * SSD_STATE], axis=-1)
    b, n = xs.shape[:2]
    xs = xs.reshape(b, n, SSD_HEADS, SSD_HEAD_DIM)
    bm = bm.reshape(b, n, SSD_GROUPS, SSD_STATE)
    cm = cm.reshape(b, n, SSD_GROUPS, SSD_STATE)
    dt_f = jax.nn.softplus((dtf_raw + dt_bias_f).astype(jnp.float32))
    dt_b = jax.nn.softplus((dtb_raw + dt_bias_b).astype(jnp.float32))
    return xs, bm, cm, dt_f, dt_b


def ssd_output(yf, yb, xs, z, ssd_d, ssd_norm_g, w_ssd_o):
    b, n = xs.shape[:2]
    y = (yf + yb + ssd_d[:, None] * xs).reshape(b, n, SSD_D_INNER)
    return rmsnorm(y * jax.nn.silu(z), ssd_norm_g) @ w_ssd_o


def hybrid_mixer(h_l, h_c, cos, sin, w_in, conv_w, conv_b, dt_bias_f, dt_bias_b, a_log_f, a_log_b,
                 ssd_d, ssd_norm_g, lam, lam_init, subln_g, w_attn_o, w_ssd_o, w_out, with_ctx_out):
    ql, kl, vl, zl, xbcl, dtfl, dtbl, gal, gsl = in_proj_split(h_l @ w_in)
    qc, kc, vc, zc, xbcc, dtfc, dtbc, gac, gsc = in_proj_split(h_c @ w_in)

    def qk(t):
        return t.reshape(t.shape[0], t.shape[1], ATT_HEADS, 2, ATT_HEAD_DIM)

    def vv(t):
        return t.reshape(t.shape[0], t.shape[1], ATT_HEADS, 2 * ATT_HEAD_DIM)

    att_l, att_c = diff_attention(apply_rope(qk(ql), cos, sin), apply_rope(qk(kl), cos, sin), vv(vl),
                                  qk(qc), qk(kc), vv(vc), lam, lam_init, subln_g, with_ctx_out)

    a_f = -jnp.exp(a_log_f.astype(jnp.float32))
    a_b = -jnp.exp(a_log_b.astype(jnp.float32))
    rev = lambda t: jnp.flip(t, axis=1)
    xs_c, bm_c, cm_c, dtf_c, dtb_c = ssd_inputs(xbcc, dtfc, dtbc, conv_w, conv_b, dt_bias_f, dt_bias_b)
    xs_l, bm_l, cm_l, dtf_l, dtb_l = ssd_inputs(xbcl, dtfl, dtbl, conv_w, conv_b, dt_bias_f, dt_bias_b)
    b = h_l.shape[0]
    h0 = jnp.zeros((b, SSD_GROUPS, SSD_HEADS // SSD_GROUPS, SSD_HEAD_DIM, SSD_STATE), jnp.float32)
    yc_f, st_f = ssd_chunked(xs_c, dtf_c, a_f, bm_c, cm_c, h0, with_ctx_out)
    yc_b, st_b = ssd_chunked(rev(xs_c), rev(dtb_c), a_b, rev(bm_c), rev(cm_c), h0, with_ctx_out)
    yl_f, _ = ssd_chunked(xs_l, dtf_l, a_f, bm_l, cm_l, st_f, True)
    yl_b, _ = ssd_chunked(rev(xs_l), rev(dtb_l), a_b, rev(bm_l), rev(cm_l), st_b, True)
    ssd_l = ssd_output(yl_f, rev(yl_b), xs_l, zl, ssd_d, ssd_norm_g, w_ssd_o)

    def merge(att, ssd, ga, gs):
        return (jax.nn.sigmoid(ga) * (att @ w_attn_o) + jax.nn.sigmoid(gs) * ssd) @ w_out

    o_l = merge(att_l, ssd_l, gal, gsl)
    o_c = None
    if with_ctx_out:
        ssd_c = ssd_output(yc_f, rev(yc_b), xs_c, zc, ssd_d, ssd_norm_g, w_ssd_o)
        o_c = merge(att_c, ssd_c, gac, gsc)
    return o_l, o_c


def sq_relu_mlp(h, w1, w2):
    return jnp.square(jax.nn.relu(h @ w1)) @ w2


def setup_inputs(seed: int = 0) -> dict:
    key = jax.random.key(seed)
    ks = jax.random.split(key, 32)
    f32 = jnp.float32

    def nrm(k, shape, scale):
        return jax.random.normal(k, shape, f32) * scale

    def gain(k, shape):
        return 1.0 + 0.05 * jax.random.normal(k, shape, f32)

    dt = jnp.exp(jax.random.uniform(ks[10], (DEPTH, 2, SSD_HEADS), f32,
                                    math.log(1e-3), math.log(1e-1)))
    dt_bias = dt + jnp.log(-jnp.expm1(-dt))
    a_log = jnp.log(jax.random.uniform(ks[11], (DEPTH, 2, SSD_HEADS), f32, 1.0, 16.0))
    return {
        'x': nrm(ks[0], (BATCH, SEQ, D_MODEL), 1.0),
        'c': nrm(ks[1], (BATCH, D_MODEL), 1.0),
        'ctx': nrm(ks[2], (BATCH, CTX_LEN, D_MODEL), 1.0),
        'c_ctx': nrm(ks[3], (D_MODEL,), 1.0),
        'ada_w': nrm(ks[4], (DEPTH, D_MODEL, N_MOD * D_MODEL), 0.5 * D_MODEL ** -0.5),
        'ada_b': nrm(ks[5], (DEPTH, N_MOD * D_MODEL), 0.02),
        'norm_mix_g': gain(ks[6], (DEPTH, D_MODEL)),
        'w_in': nrm(ks[7], (DEPTH, D_MODEL, IN_DIM), D_MODEL ** -0.5),
        'conv_w': nrm(ks[8], (DEPTH, SSD_CONV, SSD_CONV_DIM), SSD_CONV ** -0.5),
        'conv_b': nrm(ks[9], (DEPTH, SSD_CONV_DIM), 0.02),
        'dt_bias_f': dt_bias[:, 0],
        'dt_bias_b': dt_bias[:, 1],
        'a_log_f': a_log[:, 0],
        'a_log_b': a_log[:, 1],
        'ssd_d': gain(ks[12], (DEPTH, SSD_HEADS)),
        'ssd_norm_g': gain(ks[13], (DEPTH, SSD_D_INNER)),
        'lambda_q1': nrm(ks[14], (DEPTH, ATT_HEAD_DIM), 0.1),
        'lambda_k1': nrm(ks[15], (DEPTH, ATT_HEAD_DIM), 0.1),
        'lambda_q2': nrm(ks[16], (DEPTH, ATT_HEAD_DIM), 0.1),
        'lambda_k2': nrm(ks[17], (DEPTH, ATT_HEAD_DIM), 0.1),
        'attn_subln_g': gain(ks[18], (DEPTH, 2 * ATT_HEAD_DIM)),
        'w_attn_o': nrm(ks[19], (DEPTH, ATT_V_DIM, D_MODEL), ATT_V_DIM ** -0.5),
        'w_ssd_o': nrm(ks[20], (DEPTH, SSD_D_INNER, D_MODEL), SSD_D_INNER ** -0.5),
        'w_out': nrm(ks[21], (DEPTH, D_MODEL, D_MODEL), D_MODEL ** -0.5),
        'norm_mlp_g': gain(ks[22], (DEPTH, D_MODEL)),
        'w_mlp1': nrm(ks[23], (DEPTH, D_MODEL, D_FF), D_MODEL ** -0.5),
        'w_mlp2': nrm(ks[24], (DEPTH, D_FF, D_MODEL), D_FF ** -0.5),
        'final_norm_g': gain(ks[25], (D_MODEL,)),
    }


def reference(x, c, ctx, c_ctx, ada_w, ada_b, norm_mix_g, w_in, conv_w, conv_b, dt_bias_f, dt_bias_b,
              a_log_f, a_log_b, ssd_d, ssd_norm_g, lambda_q1, lambda_k1, lambda_q2, lambda_k2,
              attn_subln_g, w_attn_o, w_ssd_o, w_out, norm_mlp_g, w_mlp1, w_mlp2, final_norm_g):
    n_lat = x.shape[1]
    cos, sin = axial_rope(n_lat)
    c_act = jax.nn.silu(c)
    cc_act = jax.nn.silu(c_ctx)
    xl, xc = x, ctx
    for l in range(DEPTH):
        last = l == DEPTH - 1
        mod_l = [m[:, None, :] for m in jnp.split(c_act @ ada_w[l] + ada_b[l], N_MOD, axis=-1)]
        mod_c = jnp.split(cc_act @ ada_w[l] + ada_b[l], N_MOD, axis=-1)
        lam_init = 0.8 - 0.6 * math.exp(-0.3 * l)
        lam = (jnp.exp(jnp.sum(lambda_q1[l].astype(jnp.float32) * lambda_k1[l].astype(jnp.float32)))
               - jnp.exp(jnp.sum(lambda_q2[l].astype(jnp.float32) * lambda_k2[l].astype(jnp.float32)))
               + lam_init)
        h_l = modulate(xl, norm_mix_g[l], mod_l[0], mod_l[1])
        h_c = modulate(xc, norm_mix_g[l], mod_c[0], mod_c[1])
        o_l, o_c = hybrid_mixer(h_l, h_c, cos, sin, w_in[l], conv_w[l], conv_b[l], dt_bias_f[l], dt_bias_b[l],
                                a_log_f[l], a_log_b[l], ssd_d[l], ssd_norm_g[l], lam, lam_init,
                                attn_subln_g[l], w_attn_o[l], w_ssd_o[l], w_out[l], not last)
        xl = xl + mod_l[2] * o_l
        xl = xl + mod_l[5] * sq_relu_mlp(modulate(xl, norm_mlp_g[l], mod_l[3], mod_l[4]), w_mlp1[l], w_mlp2[l])
        if not last:
            xc = xc + mod_c[2] * o_c
            xc = xc + mod_c[5] * sq_relu_mlp(modulate(xc, norm_mlp_g[l], mod_c[3], mod_c[4]), w_mlp1[l], w_mlp2[l])
    return rmsnorm(xl, final_norm_g)
```

```python
import math
import numpy as np
import concourse.bass as bass
import concourse.mybir as mybir
from concourse.bass_utils import run_bass_kernel_spmd
from contextlib import ExitStack

F32 = mybir.dt.float32
BF16 = mybir.dt.bfloat16
ALU = mybir.AluOpType
AF = mybir.ActivationFunctionType
AX = mybir.AxisListType

D = 1024
IN_DIM = 11328
NSEM = 80
ARENA_F32 = 50176

PV = {}
_o = 0
for _n, _w in (("gmix", 8), ("gmlp", 8), ("convw", 160), ("convb", 32), ("dtb", 64), ("alog", 64),
               ("ssdd", 16), ("ssdg", 16), ("subg", 128), ("lq1", 64), ("lk1", 64), ("lq2", 64),
               ("lk2", 64), ("adab", 48)):
    PV[_n] = (_o, _o + _w)
    _o += _w
NPV = _o


class Tile:
    def __init__(s, ap, name=""):
        s.ap = ap; s.w = None; s.r = {}; s.dsem = None; s.dcnt = 0; s.name = name

    def __getitem__(s, k):
        return V(s, s.ap[k])

    def v(s):
        return V(s, s.ap)


class V:
    def __init__(s, t, ap):
        s.t = t; s.ap = ap

    def __getitem__(s, k):
        return V(s.t, s.ap[k])

    def re(s, pat, **kw):
        return V(s.t, s.ap.rearrange(pat, **kw))

    def bc(s, shape):
        return V(s.t, s.ap.to_broadcast(list(shape)))

    def un(s, ax):
        return V(s.t, s.ap.unsqueeze(ax))


def _a(x):
    return x.ap if isinstance(x, V) else x


class KB:
    def __init__(s, nc, stack):
        s.nc = nc
        s.q = {e: [] for e in ("pe", "act", "dve", "pool", "sp")}
        s.esem = {e: stack.enter_context(nc.semaphore("e_" + e)) for e in ("pe", "act", "dve", "pool")}
        s.cnt = {e: 0 for e in s.esem}
        s.known = {e: {} for e in s.q}
        s.free_sems = [stack.enter_context(nc.semaphore("d%d" % i)) for i in range(NSEM)]
        s.semcnt = {sm: 0 for sm in s.free_sems}
        s.phase_sems = []
        s.persist = False
        s.arena = stack.enter_context(nc.sbuf_tensor("arena", [128, ARENA_F32], F32))
        s.off = 0
        s.base = 0
        s.psum = [Tile(stack.enter_context(nc.psum_tensor("ps%d" % i, [128, 512], F32))[:, :], "ps%d" % i)
                  for i in range(8)]
        s.rot = {}

    def alloc(s, free_shape, dt=F32, name=""):
        n = int(np.prod(free_shape))
        nf = (n + 1) // 2 if dt == BF16 else n
        nf = (nf + 7) // 8 * 8
        assert s.off + nf <= ARENA_F32, "SBUF arena overflow %s %d" % (name, s.off + nf)
        ap = s.arena[:, s.off:s.off + nf]
        s.off += nf
        if dt == BF16:
            ap = ap.bitcast(BF16)
        ap = ap[:, 0:n]
        if len(free_shape) == 2:
            ap = ap.rearrange("p (a b) -> p a b", a=free_shape[0])
        elif len(free_shape) == 3:
            ap = ap.rearrange("p (a b c) -> p a b c", a=free_shape[0], b=free_shape[1])
        return Tile(ap, name)

    def ring(s, key, n, free_shape, dt=F32):
        if key not in s.rot:
            s.rot[key] = [[s.alloc(free_shape, dt, key + str(i)) for i in range(n)], 0]
        r = s.rot[key]
        t = r[0][r[1] % n]
        r[1] += 1
        return t

    def ps(s, idxs, key):
        r = s.rot.setdefault("ps_" + key, [None, 0])
        t = s.psum[idxs[r[1] % len(idxs)]]
        r[1] += 1
        return t

    def _getsem(s):
        sm = s.free_sems.pop()
        if not s.persist:
            s.phase_sems.append(sm)
        return sm

    def _waits(s, eng, reads, writes, skip_dma_sem=None):
        w = {}

        def need(d):
            if d is None:
                return
            sem, val, src = d
            if src == "pe" and eng == "pe":
                return
            if w.get(sem, 0) < val:
                w[sem] = val

        for t in reads:
            need(t.w)
        for t in writes:
            if not (skip_dma_sem is not None and t.w is not None and t.w[2] == "dma" and t.w[0] is skip_dma_sem):
                need(t.w)
            for d in t.r.values():
                need(d)
        out = []
        kn = s.known[eng]
        for sem, val in w.items():
            if kn.get(sem, 0) < val:
                kn[sem] = val
                out.append((sem, val))
        return out

    def op(s, eng, fn, outs, ins):
        reads = []
        for v in ins:
            if isinstance(v, V) and v.t not in reads:
                reads.append(v.t)
        writes = []
        for v in outs:
            if isinstance(v, V) and v.t not in writes:
                writes.append(v.t)
        wl = s._waits(eng, reads, writes)
        s.cnt[eng] += 1
        me = (s.esem[eng], s.cnt[eng], eng)
        s.q[eng].append((wl, fn, (s.esem[eng], 1)))
        for t in reads:
            t.r[eng] = me
        for t in writes:
            t.w = me
            t.r = {}

    def dma(s, out, in_, queue="sp"):
        reads = [in_.t] if isinstance(in_, V) else []
        writes = [out.t] if isinstance(out, V) else []
        owner = writes[0] if writes else reads[0]
        if owner.dsem is None:
            owner.dsem = s._getsem()
            owner.dcnt = s.semcnt[owner.dsem]
        wl = s._waits(queue, reads, writes, skip_dma_sem=owner.dsem)
        owner.dcnt += 16
        s.semcnt[owner.dsem] = owner.dcnt
        dep = (owner.dsem, owner.dcnt, "dma")
        oap, iap = _a(out), _a(in_)
        s.q[queue].append((wl, lambda e: e.dma_start(out=oap, in_=iap), (owner.dsem, 16)))
        for t in reads:
            t.r[("dma", id(owner))] = dep
        for t in writes:
            t.w = dep
            t.r = {}

    def barrier(s):
        allsem = {}
        for e, sm in s.esem.items():
            allsem[sm] = s.cnt[e]
        for sm, c in s.semcnt.items():
            allsem[sm] = c
        for eng in s.q:
            kn = s.known[eng]
            wl = []
            for sm, c in allsem.items():
                if c > kn.get(sm, 0):
                    kn[sm] = c
                    wl.append((sm, c))
            s.q[eng].append((wl, None, None))

    def end_phase(s):
        s.barrier()
        s.free_sems.extend(s.phase_sems)
        s.phase_sems = []
        s.off = s.base
        s.rot = {}
        for t in s.psum:
            t.w = None; t.r = {}

    def emit(s):
        with s.nc.Block() as block:
            def rep(name):
                def f(e):
                    for wl, fn, inc in s.q[name]:
                        for sem, val in wl:
                            e.wait_ge(sem, val)
                        if fn is not None:
                            fn(e).then_inc(inc[0], inc[1])
                return f
            block.tensor(rep("pe"))
            block.scalar(rep("act"))
            block.vector(rep("dve"))
            block.gpsimd(rep("pool"))
            block.sync(rep("sp"))

    def mm(s, out, lhsT, rhs, start=True, stop=True, skip=False):
        o, l, r = _a(out), _a(lhsT), _a(rhs)
        if skip:
            s.op("pe", lambda e: e.matmul(o, lhsT=l, rhs=r, start=start, stop=stop, skip_group_check=True),
                 [out], [lhsT, rhs])
        else:
            s.op("pe", lambda e: e.matmul(o, lhsT=l, rhs=r, start=start, stop=stop), [out], [lhsT, rhs])

    def tr(s, out, in_, ident):
        o, i, d = _a(out), _a(in_), _a(ident)
        s.op("pe", lambda e: e.transpose(o, i, d), [out], [in_, ident])

    def act(s, out, in_, func, bias=0.0, scale=1.0, accum=None):
        o, i, b, sc, ac = _a(out), _a(in_), _a(bias), _a(scale), _a(accum)
        if ac is None:
            s.op("act", lambda e: e.activation(out=o, in_=i, func=func, bias=b, scale=sc), [out], [in_, bias, scale])
        else:
            s.op("act", lambda e: e.activation(out=o, in_=i, func=func, bias=b, scale=sc, accum_out=ac),
                 [out, accum], [in_, bias, scale])

    def tt(s, eng, out, a, b, op):
        o, x, y = _a(out), _a(a), _a(b)
        s.op(eng, lambda e: e.tensor_tensor(out=o, in0=x, in1=y, op=op), [out], [a, b])

    def ts(s, eng, out, a, s1, s2, op0, op1=None):
        o, x, c1, c2 = _a(out), _a(a), _a(s1), _a(s2)
        if op1 is None:
            s.op(eng, lambda e: e.tensor_scalar(out=o, in0=x, scalar1=c1, scalar2=None, op0=op0), [out], [a, s1])
        else:
            s.op(eng, lambda e: e.tensor_scalar(out=o, in0=x, scalar1=c1, scalar2=c2, op0=op0, op1=op1),
                 [out], [a, s1, s2])

    def stt(s, eng, out, a, sc, b, op0, op1):
        o, x, c, y = _a(out), _a(a), _a(sc), _a(b)
        s.op(eng, lambda e: e.scalar_tensor_tensor(out=o, in0=x, scalar=c, in1=y, op0=op0, op1=op1),
             [out], [a, sc, b])

    def cp(s, eng, out, a):
        o, x = _a(out), _a(a)
        if eng == "act":
            s.op("act", lambda e: e.activation(out=o, in_=x, func=AF.Copy), [out], [a])
        else:
            s.op(eng, lambda e: e.tensor_copy(out=o, in_=x), [out], [a])

    def memset(s, eng, out, val):
        o = _a(out)
        s.op(eng, lambda e: e.memset(o, val), [out], [])

    def recip(s, out, a):
        o, x = _a(out), _a(a)
        s.op("dve", lambda e: e.reciprocal(out=o, in_=x), [out], [a])

    def rsum(s, out, a):
        o, x = _a(out), _a(a)
        s.op("dve", lambda e: e.tensor_reduce(out=o, in_=x, axis=AX.X, op=ALU.add), [out], [a])


def build(NB=2, NL=2048, NCX=256, DEPTH=4, dbg=False):
    TS = NL + NCX
    T = NB * TS
    NC3 = NB + 1
    NKC = TS // 128
    nc = bass.Bass("TRN2", target_bir_lowering=False)

    def din(name, shape, dt=F32):
        return nc.dram_tensor(name, list(shape), dt, kind="ExternalInput").ap()

    skind = "ExternalOutput" if dbg else "Internal"

    def dscr(name, shape, dt):
        return nc.dram_tensor(name, list(shape), dt, kind=skind).ap()

    xin = din("xT", [D, T])
    cvec_in = din("cvec", [128, 8 * NC3])
    pvec_in = din("pvec", [DEPTH, 128, NPV])
    gvec_in = din("gvec", [128, 8])
    consts_in = din("consts", [128, 6 * 128])
    rope_in = din("rope", [128, 2 * NL])
    ada_w = din("ada_w", [DEPTH, D, 6 * D])
    w_in = din("w_in", [DEPTH, D, IN_DIM])
    w_ao = din("w_attn_o", [DEPTH, D, D])
    w_so = din("w_ssd_o", [DEPTH, 2 * D, D])
    w_o = din("w_out", [DEPTH, D, D])
    w_1 = din("w_mlp1", [DEPTH, D, 4 * D])
    w_2 = din("w_mlp2", [DEPTH, 4 * D, D])
    outT = nc.dram_tensor("outT", [D, NB * NL], F32, kind="ExternalOutput").ap()

    xres = dscr("xres", [D, T], F32)
    qT = dscr("qT", [D, T], BF16)
    kT = dscr("kT", [D, T], BF16)
    vaug = dscr("vaug", [T, 8, 129], BF16)
    szT = dscr("szT", [2 * D, T], BF16)
    xbcT = dscr("xbcT", [4 * D, T], BF16)
    xbtok = dscr("xbtok", [T, 3 * D], BF16)
    dtD = dscr("dtD", [T, 64], F32)
    sgT = dscr("sgT", [2 * D, T], BF16)
    attT = dscr("attT", [D, T], BF16)
    yT = dscr("yT", [2 * D, T], F32)
    gT = dscr("gT", [2 * D, T], BF16)
    uT = dscr("uT", [4 * D, T], BF16)

    def fm(ap):
        return ap.rearrange("(k p) t -> p k t", p=128)

    seqs = []
    for b in range(NB):
        seqs.append((b, 0, b * TS, NL))
        seqs.append((b, 1, b * TS + NL, NCX))
    tiles = []
    for si, (b, ic, so, sl) in enumerate(seqs):
        for o in range(0, sl, 512):
            n = min(512, sl - o)
            tiles.append((b, ic, so + o, n, o, si, o + n >= sl))

    with ExitStack() as stack:
        k = KB(nc, stack)
        k.persist = True
        cst_f = k.alloc([6, 128], F32, "cst_f")
        cst = k.alloc([6, 128], BF16, "cst")
        pv = k.alloc([DEPTH, NPV], F32, "pv")
        modT = k.alloc([DEPTH, 48, NC3], F32, "modT")
        gmA = k.alloc([DEPTH, 8, NC3], F32, "gmA")
        gmF = k.alloc([DEPTH, 8, NC3], F32, "gmF")
        aneg = k.alloc([DEPTH, 64], F32, "aneg")
        nlam = k.alloc([DEPTH, 1], F32, "nlam")
        subg = k.alloc([DEPTH, 128], F32, "subg")
        fg = k.alloc([8], F32, "fg")
        cact_f = k.alloc([8, NC3], F32, "cact_f")
        cact = k.alloc([8, NC3], BF16, "cact")
        sml = k.alloc([16], F32, "sml")
        k.dma(cst_f.v(), consts_in.rearrange("p (a b) -> p a b", a=6))
        k.dma(fg.v(), gvec_in)
        k.dma(cact_f.v(), cvec_in.rearrange("p (a b) -> p a b", a=8))
        for l in range(DEPTH):
            k.dma(pv[:, l, :], pvec_in[l])
        k.cp("dve", cst.v(), cst_f.v())
        k.act(cact_f.v(), cact_f.v(), AF.Silu)
        k.cp("dve", cact.v(), cact_f.v())
        IDN, LE, GT, GE, LT, ONE = (cst[:, i, :] for i in range(6))
        k.persist = False
        k.base = k.off

        def pvs(l, name):
            a, b = PV[name]
            return pv[:, l, a:b]

        for l in range(DEPTH):
            wk = [k.ring("adaw", 8, [6 * D], BF16) for _ in range(8)]
            for kk in range(8):
                k.dma(wk[kk].v(), ada_w[l, kk * 128:(kk + 1) * 128, :], queue="pool")
            ps = k.ps([0, 1], "m")
            for j in range(48):
                for kk in range(8):
                    k.mm(ps[:, j * NC3:(j + 1) * NC3], wk[kk][:, j * 128:(j + 1) * 128], cact[:, kk, :],
                         start=(kk == 0), stop=(kk == 7))
            k.tt("dve", modT[:, l, :, :], ps[:, 0:48 * NC3].re("p (j c) -> p j c", c=NC3),
                 pvs(l, "adab").un(2).bc([128, 48, NC3]), ALU.add)
            for m in (1, 4):
                k.ts("dve", modT[:, l, 8 * m:8 * m + 8, :], modT[:, l, 8 * m:8 * m + 8, :], 1.0, None, ALU.add)
            k.tt("dve", gmA[:, l, :, :], modT[:, l, 8:16, :], pvs(l, "gmix").un(2).bc([128, 8, NC3]), ALU.mult)
            k.tt("dve", gmF[:, l, :, :], modT[:, l, 32:40, :], pvs(l, "gmlp").un(2).bc([128, 8, NC3]), ALU.mult)
            k.act(aneg[:, l, :], pvs(l, "alog"), AF.Exp)
            k.ts("dve", aneg[:, l, :], aneg[:, l, :], -1.0, None, ALU.mult)
            lam_init = 0.8 - 0.6 * math.exp(-0.3 * l)
            tmpl = k.ring("lamt", 2, [64], F32)
            k.tt("dve", tmpl.v(), pvs(l, "lq1"), pvs(l, "lk1"), ALU.mult)
            k.rsum(sml[:, 0:1], tmpl.v())
            tmpl2 = k.ring("lamt", 2, [64], F32)
            k.tt("dve", tmpl2.v(), pvs(l, "lq2"), pvs(l, "lk2"), ALU.mult)
            k.rsum(sml[:, 1:2], tmpl2.v())
            k.act(sml[:, 2:4], sml[:, 0:2], AF.Exp)
            k.tt("dve", sml[:, 4:5], sml[:, 3:4], sml[:, 2:3], ALU.subtract)
            k.ts("dve", nlam[:, l, :], sml[:, 4:5], -lam_init, None, ALU.add)
            k.ts("dve", subg[:, l, :], pvs(l, "subg"), 1.0 - lam_init, None, ALU.mult)
        k.end_phase()

        def modulate(src, hts, gm, l, shift_m, tl):
            for ti, (b, ic, off, n, pos0, si, last) in enumerate(tl):
                col = NB if ic else b
                xt = k.ring("mod_x", 2, [8, 512], F32)
                k.dma(xt[:, :, :n], fm(src)[:, :, off:off + n])
                sq = k.ring("mod_sq", 2, [8, 512], BF16)
                k.act(sq[:, :, :n], xt[:, :, :n], AF.Square)
                ps = k.ps([0, 1], "mod")
                for j in range(8):
                    k.mm(ps[:, :n], ONE, sq[:, j, :n], start=(j == 0), stop=(j == 7))
                rstd = k.ring("mod_r", 2, [512], F32)
                k.ts("dve", rstd[:, :n], ps[:, :n], 1.0 / D, 1e-6, ALU.mult, ALU.add)
                k.act(rstd[:, :n], rstd[:, :n], AF.Sqrt)
                k.recip(rstd[:, :n], rstd[:, :n])
                xn = k.ring("mod_xn", 2, [8, 512], F32)
                k.tt("dve", xn[:, :, :n], xt[:, :, :n], rstd[:, :n].un(1).bc([128, 8, n]), ALU.mult)
                for j in range(8):
                    k.act(hts[ti][:, j, :n], xn[:, j, :n], AF.Identity,
                          bias=modT[:, l, 8 * shift_m + j, col:col + 1], scale=gm[:, l, j, col:col + 1])

        for l in range(DEPTH):
            last_layer = (l == DEPTH - 1)
            xsrc = xin if l == 0 else xres
            act_tiles = [t for t in tiles if not (last_layer and t[1])]
            hts = [k.alloc([8, tiles[i][3]], BF16, "hT%d" % i) for i in range(len(tiles))]
            mark = k.off
            modulate(xsrc, hts, gmA, l, 0, tiles)
            k.barrier()
            k.off = mark
            k.rot = {}
            rope = k.alloc([2, NL], F32, "rope")
            k.dma(rope.v(), rope_in.rearrange("p (a b) -> p a b", a=2))
            vst = [k.alloc([4, 129], BF16, "vst%d" % i) for i in range(3)]
            for t_ in vst:
                k.memset("pool", t_[:, :, 128:129], 1.0)
            vst_i = 0
            segs = (("q", 0, 1024), ("k", 1024, 2048), ("v", 2048, 3072), ("z", 3072, 5120),
                    ("xbc", 5120, 9216), ("dt", 9216, 9280), ("ga", 9280, 10304), ("gs", 10304, 11328))
            wfm = w_in[l].rearrange("(k p) c -> p k c", p=128)
            for sname, s0, s1 in segs:
                for c0 in range(s0, s1, 512):
                    cw_ = min(512, s1 - c0)
                    wb = k.ring("wb", 3, [8, 512], BF16)
                    k.dma(wb[:, :, :cw_], wfm[:, :, c0:c0 + cw_], queue="pool")
                    if sname == "v":
                        vb = (c0 - s0) // 512
                        for ti, (b, ic, off, n, pos0, si, last) in enumerate(tiles):
                            for tb in range(n // 128):
                                ps = k.ps([0, 1, 2, 3, 4, 5], "b")
                                for kk in range(8):
                                    k.mm(ps[:, :], hts[ti][:, kk, tb * 128:(tb + 1) * 128], wb[:, kk, :],
                                         start=(kk == 0), stop=(kk == 7))
                                vs_ = vst[vst_i % 3]; vst_i += 1
                                k.act(vs_[:, :, 0:128], ps[:, :].re("p (h e) -> p h e", h=4), AF.Copy)
                                t0 = off + tb * 128
                                k.dma(vaug[t0:t0 + 128, vb * 4:(vb + 1) * 4, :], vs_.v())
                        continue
                    if sname == "dt":
                        for ti, (b, ic, off, n, pos0, si, last) in enumerate(tiles):
                            nb_ = n // 128
                            ps = k.ps([0, 1, 2, 3, 4, 5], "b")
                            for tb in range(nb_):
                                for kk in range(8):
                                    k.mm(ps[:, tb * 64:(tb + 1) * 64], hts[ti][:, kk, tb * 128:(tb + 1) * 128],
                                         wb[:, kk, :64], start=(kk == 0), stop=(kk == 7))
                            d1 = k.ring("dt1", 2, [4, 64], F32)
                            k.tt("dve", d1[:, :nb_, :], ps[:, :nb_ * 64].re("p (a b) -> p a b", b=64),
                                 pvs(l, "dtb").un(1).bc([128, nb_, 64]), ALU.add)
                            k.act(d1[:, :nb_, :], d1[:, :nb_, :], AF.Exp)
                            d2 = k.ring("dt2", 2, [4, 64], F32)
                            k.act(d2[:, :nb_, :], d1[:, :nb_, :], AF.Ln, bias=1.0)
                            k.dma(dtD[off:off + n, :].rearrange("(a p) c -> p a c", p=128), d2[:, :nb_, :])
                        continue
                    for jc in range(cw_ // 128):
                        gcol = c0 + jc * 128
                        stage = None
                        for ti, (b, ic, off, n, pos0, si, last) in enumerate(tiles):
                            ps = k.ps([0, 1, 2, 3, 4, 5], "b")
                            for kk in range(8):
                                k.mm(ps[:, :n], wb[:, kk, jc * 128:(jc + 1) * 128], hts[ti][:, kk, :n],
                                     start=(kk == 0), stop=(kk == 7))
                            if sname in ("q", "k"):
                                h = (gcol - s0) // 128
                                dst = qT if sname == "q" else kT
                                scl = 0.125 if sname == "q" else 1.0
                                ob = k.ring("ob", 3, [512], BF16)
                                if ic:
                                    k.act(ob[:, :n], ps[:, :n], AF.Identity, scale=scl)
                                else:
                                    qf = k.ring("qf", 2, [512], F32)
                                    k.act(qf[:, :n], ps[:, :n], AF.Identity, scale=scl)
                                    A = k.ring("ropeA", 2, [512], F32)
                                    k.tt("dve", A[:, :n], qf[:, :n], rope[:, 0, pos0:pos0 + n], ALU.mult)
                                    Bt = k.ring("ropeB", 2, [512], F32)
                                    for g in range(4):
                                        gs_ = g ^ 1
                                        k.tt("dve", Bt[32 * g:32 * g + 32, :n], qf[32 * gs_:32 * gs_ + 32, :n],
                                             rope[32 * gs_:32 * gs_ + 32, 1, pos0:pos0 + n], ALU.mult)
                                    k.tt("pool", ob[:, :n], A[:, :n], Bt[:, :n], ALU.add)
                                k.dma(dst[h * 128:(h + 1) * 128, off:off + n], ob[:, :n])
                            elif sname == "z":
                                r0 = gcol - s0
                                ob = k.ring("ob", 3, [512], BF16)
                                k.act(ob[:, :n], ps[:, :n], AF.Silu)
                                k.dma(szT[r0:r0 + 128, off:off + n], ob[:, :n])
                            elif sname in ("ga", "gs"):
                                r0 = gcol - 9280
                                ob = k.ring("ob", 3, [512], BF16)
                                k.act(ob[:, :n], ps[:, :n], AF.Sigmoid)
                                k.dma(sgT[r0:r0 + 128, off:off + n], ob[:, :n])
                            else:
                                jx = (gcol - s0) // 128
                                sb_, sic, soff, slen = seqs[si]
                                if pos0 == 0:
                                    stage = k.ring("stage", 2, [NL + 4], F32)
                                    k.memset("pool", stage[:, 0:2], 0.0)
                                    k.memset("pool", stage[:, 2 + slen:4 + slen], 0.0)
                                k.act(stage[:, 2 + pos0:2 + pos0 + n], ps[:, :n], AF.Copy)
                                if last:
                                    a0, _ = PV["convw"]
                                    acc = k.ring("cacc", 1, [NL], F32)
                                    k.ts("dve", acc[:, :slen], stage[:, 0:slen], pv[:, l, a0 + jx * 5:a0 + jx * 5 + 1],
                                         None, ALU.mult)
                                    for kq in range(1, 5):
                                        k.stt("dve", acc[:, :slen], stage[:, kq:kq + slen],
                                              pv[:, l, a0 + jx * 5 + kq:a0 + jx * 5 + kq + 1], acc[:, :slen],
                                              ALU.mult, ALU.add)
                                    cvo = k.ring("cvo", 2, [NL], BF16)
                                    b0_, _ = PV["convb"]
                                    k.act(cvo[:, :slen], acc[:, :slen], AF.Silu, bias=pv[:, l, b0_ + jx:b0_ + jx + 1])
                                    k.dma(xbcT[jx * 128:(jx + 1) * 128, soff:soff + slen], cvo[:, :slen])
                                    if jx < 24:
                                        for t4 in range(0, slen // 128, 4):
                                            nq = min(4, slen // 128 - t4)
                                            pst = k.ps([6, 7], "bt")
                                            pv_ = V(pst, pst.ap.bitcast(BF16))
                                            for q_ in range(nq):
                                                k.tr(pv_[:, q_ * 128:(q_ + 1) * 128],
                                                     cvo[:, (t4 + q_) * 128:(t4 + q_ + 1) * 128], IDN)
                                            tst = k.ring("tst", 2, [512], BF16)
                                            k.cp("dve", tst[:, :nq * 128], pv_[:, :nq * 128])
                                            tk0 = soff + t4 * 128
                                            k.dma(xbtok[tk0:tk0 + nq * 128, jx * 128:(jx + 1) * 128]
                                                  .rearrange("(q p) c -> p q c", p=128),
                                                  tst[:, :nq * 128].re("p (q c) -> p q c", c=128))
            k.end_phase()

            for b in range(NB):
                kts = []
                for h in range(8):
                    t_ = k.ring("KT%d" % h, 1, [TS], BF16)
                    k.dma(t_.v(), kT[h * 128:(h + 1) * 128, b * TS:(b + 1) * TS])
                    kts.append(t_)
                va = k.ring("VA", 1, [NKC, 8, 129], BF16)
                k.dma(va.v(), vaug[b * TS:(b + 1) * TS].rearrange("(c p) h e -> p c h e", p=128))
                for h in range(8):
                    for (tb_, ic, off, n, pos0, si, last) in tiles:
                        if tb_ != b or (ic and last_layer):
                            continue
                        nqb = n // 128
                        qt = k.ring("QT", 2, [512], BF16)
                        k.dma(qt[:, :n], qT[h * 128:(h + 1) * 128, off:off + n])
                        kcs = list(range(NL // 128, NKC)) if ic else list(range(NKC))
                        obank = [k.psum[4], k.psum[5], k.psum[6]]

                        def oreg(c, qb):
                            r = c * 4 + qb
                            return obank[r // 3][:, (r % 3) * 129:(r % 3) * 129 + 129]
                        started = set()
                        for kc in kcs:
                            pts = []
                            for c in range(2):
                                pss = k.ps([0, 1, 2, 3], "s")
                                k.mm(pss[:, :n], kts[h][64 * c:64 * c + 64, kc * 128:(kc + 1) * 128],
                                     qt[64 * c:64 * c + 64, :n])
                                pt = k.ring("PT", 4, [512], BF16)
                                k.act(pt[:, :n], pss[:, :n], AF.Exp)
                                pts.append(pt)
                            for c in range(2):
                                for qb in range(nqb):
                                    r = c * 4 + qb
                                    st = (r // 3) not in started
                                    started.add(r // 3)
                                    k.mm(oreg(c, qb), pts[c][:, qb * 128:(qb + 1) * 128], va[:, kc, h, :],
                                         start=st, stop=(kc == kcs[-1]), skip=True)
                        rec = k.ring("rec", 2, [8], F32)
                        for c in range(2):
                            for qb in range(nqb):
                                k.recip(rec[:, c * 4 + qb:c * 4 + qb + 1], oreg(c, qb)[:, 128:129])
                        k.ts("dve", rec[:, 4:8], rec[:, 4:8], nlam[:, l, :], None, ALU.mult)
                        ssq = k.ring("ssq", 2, [4], F32)
                        k.memset("dve", ssq.v(), 0.0)
                        os_ = []
                        for qb in range(nqb):
                            t1 = k.ring("at1", 2, [128], F32)
                            k.ts("dve", t1.v(), oreg(0, qb)[:, 0:128], rec[:, qb:qb + 1], None, ALU.mult)
                            o_ = k.ring("ao", 8, [128], F32)
                            k.stt("dve", o_.v(), oreg(1, qb)[:, 0:128], rec[:, 4 + qb:5 + qb], t1.v(),
                                  ALU.mult, ALU.add)
                            junk = k.ring("ajunk", 2, [128], F32)
                            k.act(junk.v(), o_.v(), AF.Square, accum=ssq[:, qb:qb + 1])
                            os_.append(o_)
                        rs = k.ring("ars", 2, [4], F32)
                        k.ts("dve", rs.v(), ssq.v(), 1.0 / 128, 1e-6, ALU.mult, ALU.add)
                        k.act(rs.v(), rs.v(), AF.Sqrt)
                        k.recip(rs.v(), rs.v())
                        on = k.ring("aon", 2, [4, 128], BF16)
                        for qb in range(nqb):
                            k.stt("dve", on[:, qb, :], os_[qb].v(), rs[:, qb:qb + 1], subg[:, l, :],
                                  ALU.mult, ALU.mult)
                        pst = k.psum[7]
                        pv_ = V(pst, pst.ap.bitcast(BF16))
                        for qb in range(nqb):
                            k.tr(pv_[:, qb * 128:(qb + 1) * 128], on[:, qb, :], IDN)
                        ob = k.ring("ob", 3, [512], BF16)
                        k.cp("act", ob[:, :n], pv_[:, :n])
                        k.dma(attT[h * 128:(h + 1) * 128, off:off + n], ob[:, :n])
            k.end_phase()

            d0, _ = PV["ssdd"]
            for b in range(NB):
                for dr in range(2):
                    Ms, Um, Vm, Dm = (GT, LE, LE, GT) if dr == 0 else (LT, GE, GE, LT)
                    lastcol = 127 if dr == 0 else 0
                    STg = [k.ring("ST%d" % g, 1, [256], F32) for g in range(8)]
                    STb = [k.ring("STb%d" % g, 1, [4, 128], BF16) for g in range(8)]
                    for g in range(8):
                        k.memset("pool", STg[g].v(), 0.0)
                        k.memset("pool", STb[g].v(), 0.0)
                    cl = [(1, ci) for ci in range(NCX // 128)] + [(0, ci) for ci in range(NL // 128)]
                    if dr == 1:
                        cl = [(1, ci) for ci in reversed(range(NCX // 128))] + \
                             [(0, ci) for ci in reversed(range(NL // 128))]
                    for (ic, ci) in cl:
                        tok0 = b * TS + (NL if ic else 0) + ci * 128
                        xs_tok = k.ring("xs_tok", 2, [32, 64], BF16)
                        k.dma(xs_tok.v(), xbtok[tok0:tok0 + 128, 0:2048].rearrange("p (h e) -> p h e", e=64))
                        B_tok = k.ring("B_tok", 2, [1024], BF16)
                        k.dma(B_tok.v(), xbtok[tok0:tok0 + 128, 2048:3072])
                        BT = k.ring("BT", 2, [8, 128], BF16)
                        k.dma(BT.v(), fm(xbcT[2048:3072, :])[:, :, tok0:tok0 + 128])
                        CT = k.ring("CT", 2, [8, 128], BF16)
                        k.dma(CT.v(), fm(xbcT[3072:4096, :])[:, :, tok0:tok0 + 128])
                        dt = k.ring("dt", 2, [32], F32)
                        k.dma(dt.v(), dtD[tok0:tok0 + 128, dr * 32:(dr + 1) * 32])
                        if dr == 1:
                            yf = k.ring("yf", 2, [16, 128], F32)
                            k.dma(yf.v(), fm(yT)[:, :, tok0:tok0 + 128])
                            xsT = k.ring("xsT", 2, [16, 128], BF16)
                            k.dma(xsT.v(), fm(xbcT[0:2048, :])[:, :, tok0:tok0 + 128])
                            szt = k.ring("szt", 2, [16, 128], BF16)
                            k.dma(szt.v(), fm(szT)[:, :, tok0:tok0 + 128])
                        dta = k.ring("dta", 2, [32], F32)
                        k.tt("dve", dta.v(), dt.v(), aneg[:, l, dr * 32:(dr + 1) * 32], ALU.mult)
                        dhi = k.ring("dhi", 2, [32], BF16)
                        k.cp("dve", dhi.v(), dta.v())
                        dlf = k.ring("dlf", 2, [32], F32)
                        k.tt("dve", dlf.v(), dta.v(), dhi.v(), ALU.subtract)
                        dlo = k.ring("dlo", 2, [32], BF16)
                        k.cp("dve", dlo.v(), dlf.v())
                        psd = k.psum[7]
                        k.mm(psd[:, 0:32], Dm, dhi.v(), start=True, stop=False)
                        k.mm(psd[:, 0:32], Dm, dlo.v(), start=False, stop=True)
                        dte = k.ring("dte", 2, [32], F32)
                        k.act(dte.v(), psd[:, 0:32], AF.Exp)
                        xcp = k.ring("xcp", 2, [32, 128], BF16)
                        if k.rot["xcp"][1] <= 2:
                            k.memset("pool", xcp.v(), 0.0)
                        xcv = xcp.v().re("p (a two) c -> p a two c", two=2)
                        xsv = xs_tok.v().re("p (a two) e -> p a two e", two=2)
                        dtv = dt.v().re("p (a two) -> p a two", two=2)
                        for par in range(2):
                            k.tt("pool", xcv[:, :, par, par * 64:par * 64 + 64], xsv[:, :, par, :],
                                 dtv[:, :, par].un(2).bc([128, 16, 64]), ALU.mult)
                        xcd = k.ring("xcd", 2, [32, 64], BF16)
                        dev = dte.v().re("p (a two) -> p a two", two=2)
                        xdv = xcd.v().re("p (a two) e -> p a two e", two=2)
                        for par in range(2):
                            k.tt("pool", xdv[:, :, par, :], xcv[:, :, par, par * 64:par * 64 + 64],
                                 dev[:, :, par].un(2).bc([128, 16, 64]), ALU.mult)
                        rhi = k.ring("rhi", 2, [32, 128], BF16)
                        k.tt("dve", rhi.v(), dhi.v().un(2).bc([128, 32, 128]), Um.un(1).bc([128, 32, 128]), ALU.mult)
                        rlo = k.ring("rlo", 2, [32, 128], BF16)
                        k.tt("dve", rlo.v(), dlo.v().un(2).bc([128, 32, 128]), Um.un(1).bc([128, 32, 128]), ALU.mult)
                        for g in range(8):
                            r4h = rhi[:, 4 * g:4 * g + 4, :].re("p h l -> p (h l)")
                            r4l = rlo[:, 4 * g:4 * g + 4, :].re("p h l -> p (h l)")
                            pseg = k.ps([0, 1], "seg")
                            k.mm(pseg[:, :], Ms, r4h, start=True, stop=False)
                            k.mm(pseg[:, :], Ms, r4l, start=False, stop=True)
                            pcs = k.ps([2, 3], "cs")
                            k.mm(pcs[:, :], ONE, r4h, start=True, stop=False)
                            k.mm(pcs[:, :], ONE, r4l, start=False, stop=True)
                            E = k.ring("E", 2, [4, 128], BF16)
                            k.act(E.v().re("p h l -> p (h l)"), pseg[:, :], AF.Exp)
                            E0 = k.ring("E0", 2, [4, 128], F32)
                            k.act(E0.v().re("p h l -> p (h l)"), pcs[:, :], AF.Exp)
                            pcb = k.psum[4]
                            k.mm(pcb[:, 0:128], BT[:, g, :], CT[:, g, :])
                            cbm = k.ring("cbm", 2, [128], BF16)
                            k.tt("dve", cbm.v(), pcb[:, 0:128], Vm, ALU.mult)
                            MT = k.ring("MT", 2, [4, 128], BF16)
                            k.tt("dve", MT.v(), E.v(), cbm.v().un(1).bc([128, 4, 128]), ALU.mult)
                            MV = k.ring("MV", 2, [4, 128], BF16)
                            k.tt("pool", MV.v(), E0.v(), CT[:, g, :].un(1).bc([128, 4, 128]), ALU.mult)
                            py = k.psum[5]
                            for hp in range(2):
                                for hh in range(2):
                                    hl = 2 * hp + hh
                                    k.mm(py[:, hp * 128:(hp + 1) * 128], xcp[:, 4 * g + hl, :], MT[:, hl, :],
                                         start=(hh == 0), stop=False)
                                for hh in range(2):
                                    hl = 2 * hp + hh
                                    k.mm(py[:, hp * 128:(hp + 1) * 128], STb[g][:, hl, :], MV[:, hl, :],
                                         start=False, stop=(hh == 1))
                            if dr == 0:
                                yst = k.ring("yst", 2, [2, 128], F32)
                                k.cp("act", yst.v().re("p a t -> p (a t)"), py[:, 0:256])
                                k.dma(yT[2 * g * 128:(2 * g + 2) * 128, tok0:tok0 + 128]
                                      .rearrange("(a p) t -> p a t", p=128), yst.v())
                            else:
                                gs1 = k.ring("gs1", 2, [2, 128], F32)
                                k.tt("dve", gs1.v(), py[:, 0:256].re("p (a t) -> p a t", a=2),
                                     yf[:, 2 * g:2 * g + 2, :], ALU.add)
                                gs2 = k.ring("gs2", 2, [2, 128], F32)
                                for hp in range(2):
                                    k.stt("dve", gs2[:, hp, :], xsT[:, 2 * g + hp, :],
                                          pv[:, l, d0 + 2 * g + hp:d0 + 2 * g + hp + 1], gs1[:, hp, :],
                                          ALU.mult, ALU.add)
                                go = k.ring("go", 2, [2, 128], BF16)
                                k.tt("pool", go.v(), gs2.v(), szt[:, 2 * g:2 * g + 2, :], ALU.mult)
                                k.dma(gT[2 * g * 128:(2 * g + 2) * 128, tok0:tok0 + 128]
                                      .rearrange("(a p) t -> p a t", p=128), go.v())
                            pst_ = k.psum[6]
                            k.mm(pst_[:, 0:256], B_tok[:, g * 128:(g + 1) * 128],
                                 xcd[:, 4 * g:4 * g + 4, :].re("p h e -> p (h e)"))
                            tmp = k.ring("sttmp", 2, [4, 64], F32)
                            k.tt("dve", tmp.v(), STg[g].v().re("p (h e) -> p h e", e=64),
                                 E0[:, :, lastcol:lastcol + 1].bc([128, 4, 64]), ALU.mult)
                            k.tt("dve", STg[g].v(), tmp.v().re("p h e -> p (h e)"), pst_[:, 0:256], ALU.add)
                            sgv = STg[g].v().re("p (a two e) -> p a two e", two=2, e=64)
                            sbv = STb[g].v().re("p (a two) c -> p a two c", two=2)
                            for par in range(2):
                                k.cp("act", sbv[:, :, par, par * 64:par * 64 + 64], sgv[:, :, par, :])
            k.end_phase()

            wA = k.alloc([8, D], BF16, "wA")
            wS = k.alloc([16, D], BF16, "wS")
            wO = k.alloc([8, D], BF16, "wO")
            for kk in range(8):
                k.dma(wA[:, kk, :], w_ao[l, kk * 128:(kk + 1) * 128, :], queue="pool")
            for kk in range(16):
                k.dma(wS[:, kk, :], w_so[l, kk * 128:(kk + 1) * 128, :], queue="pool")
            for kk in range(8):
                k.dma(wO[:, kk, :], w_o[l, kk * 128:(kk + 1) * 128, :], queue="pool")
            g0, _ = PV["ssdg"]
            for kk in range(16):
                k.ts("dve", wS[:, kk, :], wS[:, kk, :], pv[:, l, g0 + kk:g0 + kk + 1], None, ALU.mult)
            for (b, ic, off, n, pos0, si, last) in act_tiles:
                col = NB if ic else b
                aT = k.ring("aT", 1, [8, 512], BF16)
                k.dma(aT[:, :, :n], fm(attT)[:, :, off:off + n])
                gt = k.ring("gTt", 1, [16, 512], BF16)
                k.dma(gt[:, :, :n], fm(gT)[:, :, off:off + n])
                sg = k.ring("sg", 1, [16, 512], BF16)
                k.dma(sg[:, :, :n], fm(sgT)[:, :, off:off + n])
                xt = k.ring("xt", 1, [8, 512], F32)
                k.dma(xt[:, :, :n], fm(xsrc)[:, :, off:off + n])
                sq = k.ring("sq", 1, [16, 512], BF16)
                k.act(sq[:, :, :n], gt[:, :, :n], AF.Square)
                pss = k.psum[0]
                for j in range(16):
                    k.mm(pss[:, :n], ONE, sq[:, j, :n], start=(j == 0), stop=(j == 15))
                rstd = k.ring("rstd", 2, [512], F32)
                k.ts("dve", rstd[:, :n], pss[:, :n], 1.0 / (2 * D), 1e-6, ALU.mult, ALU.add)
                k.act(rstd[:, :n], rstd[:, :n], AF.Sqrt)
                k.recip(rstd[:, :n], rstd[:, :n])
                mT = k.ring("mT", 1, [8, 512], BF16)
                for dj in range(8):
                    psA = k.ps([1, 2], "A")
                    for kk in range(8):
                        k.mm(psA[:, :n], wA[:, kk, dj * 128:(dj + 1) * 128], aT[:, kk, :n],
                             start=(kk == 0), stop=(kk == 7))
                    psS = k.ps([3, 4], "S")
                    for kk in range(16):
                        k.mm(psS[:, :n], wS[:, kk, dj * 128:(dj + 1) * 128], gt[:, kk, :n],
                             start=(kk == 0), stop=(kk == 15))
                    t1 = k.ring("et1", 2, [512], F32)
                    k.tt("dve", t1[:, :n], psA[:, :n], sg[:, dj, :n], ALU.mult)
                    t2 = k.ring("et2", 2, [512], F32)
                    k.tt("dve", t2[:, :n], psS[:, :n], rstd[:, :n], ALU.mult)
                    t3 = k.ring("et3", 2, [512], F32)
                    k.tt("pool", t3[:, :n], t2[:, :n], sg[:, 8 + dj, :n], ALU.mult)
                    k.tt("pool", mT[:, dj, :n], t1[:, :n], t3[:, :n], ALU.add)
                for dj in range(8):
                    psO = k.ps([5, 6], "O")
                    for kk in range(8):
                        k.mm(psO[:, :n], wO[:, kk, dj * 128:(dj + 1) * 128], mT[:, kk, :n],
                             start=(kk == 0), stop=(kk == 7))
                    k.stt("dve", xt[:, dj, :n], psO[:, :n], modT[:, l, 16 + dj, col:col + 1], xt[:, dj, :n],
                          ALU.mult, ALU.add)
                k.dma(fm(xres)[:, :, off:off + n], xt[:, :, :n])
            k.end_phase()

            hts = [k.alloc([8, act_tiles[i][3]], BF16, "h2T%d" % i) for i in range(len(act_tiles))]
            mark = k.off
            modulate(xres, hts, gmF, l, 3, act_tiles)
            k.barrier()
            k.off = mark
            k.rot = {}
            w1fm = w_1[l].rearrange("(k p) c -> p k c", p=128)
            for c0 in range(0, 4 * D, 512):
                wb = k.ring("wb", 3, [8, 512], BF16)
                k.dma(wb.v(), w1fm[:, :, c0:c0 + 512], queue="pool")
                for jc in range(4):
                    r0 = c0 + jc * 128
                    for ti, (b, ic, off, n, pos0, si, last) in enumerate(act_tiles):
                        ps = k.ps([0, 1, 2, 3, 4, 5, 6, 7], "f1")
                        for kk in range(8):
                            k.mm(ps[:, :n], wb[:, kk, jc * 128:(jc + 1) * 128], hts[ti][:, kk, :n],
                                 start=(kk == 0), stop=(kk == 7))
                        r_ = k.ring("relu", 3, [512], BF16)
                        k.act(r_[:, :n], ps[:, :n], AF.Relu)
                        ob = k.ring("ob", 3, [512], BF16)
                        k.tt("pool", ob[:, :n], r_[:, :n], r_[:, :n], ALU.mult)
                        k.dma(uT[r0:r0 + 128, off:off + n], ob[:, :n])
            k.end_phase()

            w2 = k.alloc([32, D], BF16, "w2")
            for kk in range(32):
                k.dma(w2[:, kk, :], w_2[l, kk * 128:(kk + 1) * 128, :], queue="pool")
            for (b, ic, off, n, pos0, si, last) in act_tiles:
                col = NB if ic else b
                ut = k.ring("ut", 2, [32, 512], BF16)
                k.dma(ut[:, :, :n], fm(uT)[:, :, off:off + n])
                xt = k.ring("xt", 2, [8, 512], F32)
                k.dma(xt[:, :, :n], fm(xres)[:, :, off:off + n])
                for dj in range(8):
                    ps = k.ps([0, 1, 2, 3], "f2")
                    for kk in range(32):
                        k.mm(ps[:, :n], w2[:, kk, dj * 128:(dj + 1) * 128], ut[:, kk, :n],
                             start=(kk == 0), stop=(kk == 31))
                    k.stt("dve", xt[:, dj, :n], ps[:, :n], modT[:, l, 40 + dj, col:col + 1], xt[:, dj, :n],
                          ALU.mult, ALU.add)
                k.dma(fm(xres)[:, :, off:off + n], xt[:, :, :n])
            k.end_phase()

        for (b, ic, off, n, pos0, si, last) in tiles:
            if ic:
                continue
            xt = k.ring("xt", 2, [8, 512], F32)
            k.dma(xt[:, :, :n], fm(xres)[:, :, off:off + n])
            sq = k.ring("sq", 2, [8, 512], BF16)
            k.act(sq[:, :, :n], xt[:, :, :n], AF.Square)
            ps = k.ps([0, 1], "fin")
            for j in range(8):
                k.mm(ps[:, :n], ONE, sq[:, j, :n], start=(j == 0), stop=(j == 7))
            rstd = k.ring("rstd", 2, [512], F32)
            k.ts("dve", rstd[:, :n], ps[:, :n], 1.0 / D, 1e-6, ALU.mult, ALU.add)
            k.act(rstd[:, :n], rstd[:, :n], AF.Sqrt)
            k.recip(rstd[:, :n], rstd[:, :n])
            xo = k.ring("xo", 2, [8, 512], F32)
            k.tt("dve", xo[:, :, :n], xt[:, :, :n], rstd[:, :n].un(1).bc([128, 8, n]), ALU.mult)
            for j in range(8):
                k.ts("dve", xo[:, j, :n], xo[:, j, :n], fg[:, j:j + 1], None, ALU.mult)
            o0 = b * NL + pos0
            k.dma(fm(outT)[:, :, o0:o0 + n], xo[:, :, :n])
        k.end_phase()
        k.emit()
    return nc


def _fmaj(v, nchunk):
    return np.ascontiguousarray(np.asarray(v, np.float32).reshape(nchunk, 128).T)


def _consts():
    r = np.arange(128)[:, None]
    c = np.arange(128)[None, :]
    mats = [(r == c), (r <= c), (r > c), (r >= c), (r < c), np.ones((128, 128), bool)]
    return np.ascontiguousarray(np.concatenate([m.astype(np.float32) for m in mats], axis=1))


def _rope(NL, grid_w=64):
    rows = NL // grid_w
    row = np.broadcast_to(np.arange(rows)[:, None], (rows, grid_w)).reshape(-1).astype(np.float32)
    col = np.broadcast_to(np.arange(grid_w)[None, :], (rows, grid_w)).reshape(-1).astype(np.float32)
    inv = (np.float32(10000.0) ** (-np.arange(16, dtype=np.float32) / np.float32(16))).astype(np.float32)
    ang = np.concatenate([row[:, None] * inv, col[:, None] * inv], axis=-1).astype(np.float32)
    cos = np.cos(ang).astype(np.float32).T
    sin = np.sin(ang).astype(np.float32).T
    p = np.arange(128)
    cosT = cos[p % 32]
    sgn = np.where((p % 64) < 32, 1.0, -1.0).astype(np.float32)[:, None]
    sinS = sin[p % 32] * sgn
    return np.ascontiguousarray(np.concatenate([cosT, sinS], axis=1).astype(np.float32))


def host_inputs(inp, NB, NL, NCX, DEPTH, ncores):
    f = lambda a: np.asarray(a, np.float32)
    x, c, ctx, c_ctx = f(inp["x"]), f(inp["c"]), f(inp["ctx"]), f(inp["c_ctx"])
    pvec = np.zeros((DEPTH, 128, NPV), np.float32)

    def put(l, name, arr):
        a, b = PV[name]
        pvec[l, :, a:b] = arr
    for l in range(DEPTH):
        put(l, "gmix", _fmaj(inp["norm_mix_g"][l], 8))
        put(l, "gmlp", _fmaj(inp["norm_mlp_g"][l], 8))
        cw = f(inp["conv_w"][l])
        put(l, "convw", cw.T.reshape(32, 128, 5).transpose(1, 0, 2).reshape(128, 160))
        put(l, "convb", _fmaj(inp["conv_b"][l], 32))
        put(l, "dtb", np.broadcast_to(np.concatenate([f(inp["dt_bias_f"][l]), f(inp["dt_bias_b"][l])])[None], (128, 64)))
        put(l, "alog", np.broadcast_to(np.concatenate([f(inp["a_log_f"][l]), f(inp["a_log_b"][l])])[None], (128, 64)))
        put(l, "ssdd", _fmaj(np.repeat(f(inp["ssd_d"][l]), 64), 16))
        put(l, "ssdg", _fmaj(inp["ssd_norm_g"][l], 16))
        put(l, "subg", np.broadcast_to(f(inp["attn_subln_g"][l])[None], (128, 128)))
        for nm, key in (("lq1", "lambda_q1"), ("lk1", "lambda_k1"), ("lq2", "lambda_q2"), ("lk2", "lambda_k2")):
            put(l, nm, np.broadcast_to(f(inp[key][l])[None], (128, 64)))
        put(l, "adab", _fmaj(inp["ada_b"][l], 48))
    shared = {
        "pvec": pvec, "gvec": _fmaj(inp["final_norm_g"], 8), "consts": _consts(), "rope": _rope(NL),
        "ada_w": f(inp["ada_w"]), "w_in": f(inp["w_in"]), "w_attn_o": f(inp["w_attn_o"]),
        "w_ssd_o": f(inp["w_ssd_o"]), "w_out": f(inp["w_out"]), "w_mlp1": f(inp["w_mlp1"]),
        "w_mlp2": f(inp["w_mlp2"]),
    }
    maps = []
    for ci in range(ncores):
        bs = list(range(ci * NB, (ci + 1) * NB))
        toks = np.concatenate([np.concatenate([x[b], ctx[b]], axis=0) for b in bs], axis=0)
        cv = np.stack([c[b] for b in bs] + [c_ctx], axis=1)
        cvec = cv.reshape(8, 128, NB + 1).transpose(1, 0, 2).reshape(128, 8 * (NB + 1))
        m = dict(shared)
        m["xT"] = np.ascontiguousarray(toks.T)
        m["cvec"] = np.ascontiguousarray(cvec)
        maps.append(m)
    return maps


_NC_CACHE = {}


def kernel(**inputs):
    x = np.asarray(inputs["x"])
    B, NL, _ = x.shape
    NCX = np.asarray(inputs["ctx"]).shape[1]
    DEPTH = np.asarray(inputs["w_in"]).shape[0]
    ncores = 8
    NB = B // ncores
    key = (NB, NL, NCX, DEPTH)
    if key not in _NC_CACHE:
        _NC_CACHE[key] = build(NB, NL, NCX, DEPTH)
    nc = _NC_CACHE[key]
    maps = host_inputs(inputs, NB, NL, NCX, DEPTH, ncores)
    res = run_bass_kernel_spmd(nc, maps, core_ids=list(range(ncores)))
    out = np.empty((B, NL, D), np.float32)
    for ci in range(ncores):
        oT = np.asarray(res.results[ci]["outT"])
        for j in range(NB):
            out[ci * NB + j] = oT[:, j * NL:(j + 1) * NL].T
    return out
```

```python
import math
import numpy as np
import concourse.bass as bass
import concourse.mybir as mybir
from concourse.bass_utils import run_bass_kernel_spmd
from contextlib import ExitStack

F32 = mybir.dt.float32
BF16 = mybir.dt.bfloat16
ALU = mybir.AluOpType
AF = mybir.ActivationFunctionType
AX = mybir.AxisListType

D = 1024
IN_DIM = 11328
NSEM = 80
ARENA_F32 = 50176

PV = {}
_o = 0
for _n, _w in (("gmix", 8), ("gmlp", 8), ("convw", 160), ("convb", 32), ("dtb", 64), ("alog", 64),
               ("ssdd", 16), ("ssdg", 16), ("subg", 128), ("lq1", 64), ("lk1", 64), ("lq2", 64),
               ("lk2", 64), ("adab", 48)):
    PV[_n] = (_o, _o + _w)
    _o += _w
NPV = _o


class Tile:
    def __init__(s, ap, name=""):
        s.ap = ap; s.w = None; s.r = {}; s.dsem = None; s.dcnt = 0; s.name = name

    def __getitem__(s, k):
        return V(s, s.ap[k])

    def v(s):
        return V(s, s.ap)


class V:
    def __init__(s, t, ap):
        s.t = t; s.ap = ap

    def __getitem__(s, k):
        return V(s.t, s.ap[k])

    def re(s, pat, **kw):
        return V(s.t, s.ap.rearrange(pat, **kw))

    def bc(s, shape):
        return V(s.t, s.ap.to_broadcast(list(shape)))

    def un(s, ax):
        return V(s.t, s.ap.unsqueeze(ax))


def _a(x):
    return x.ap if isinstance(x, V) else x


class KB:
    def __init__(s, nc, stack):
        s.nc = nc
        s.q = {e: [] for e in ("pe", "act", "dve", "pool", "sp")}
        s.esem = {e: stack.enter_context(nc.semaphore("e_" + e)) for e in ("pe", "act", "dve", "pool")}
        s.cnt = {e: 0 for e in s.esem}
        s.known = {e: {} for e in s.q}
        s.free_sems = [stack.enter_context(nc.semaphore("d%d" % i)) for i in range(NSEM)]
        s.semcnt = {sm: 0 for sm in s.free_sems}
        s.phase_sems = []
        s.persist = False
        s.arena = stack.enter_context(nc.sbuf_tensor("arena", [128, ARENA_F32], F32))
        s.off = 0
        s.base = 0
        s.psum = [Tile(stack.enter_context(nc.psum_tensor("ps%d" % i, [128, 512], F32))[:, :], "ps%d" % i)
                  for i in range(8)]
        s.rot = {}

    def alloc(s, free_shape, dt=F32, name=""):
        n = int(np.prod(free_shape))
        nf = (n + 1) // 2 if dt == BF16 else n
        nf = (nf + 7) // 8 * 8
        assert s.off + nf <= ARENA_F32, "SBUF arena overflow %s %d" % (name, s.off + nf)
        ap = s.arena[:, s.off:s.off + nf]
        s.off += nf
        if dt == BF16:
            ap = ap.bitcast(BF16)
        ap = ap[:, 0:n]
        if len(free_shape) == 2:
            ap = ap.rearrange("p (a b) -> p a b", a=free_shape[0])
        elif len(free_shape) == 3:
            ap = ap.rearrange("p (a b c) -> p a b c", a=free_shape[0], b=free_shape[1])
        return Tile(ap, name)

    def ring(s, key, n, free_shape, dt=F32):
        if key not in s.rot:
            s.rot[key] = [[s.alloc(free_shape, dt, key + str(i)) for i in range(n)], 0]
        r = s.rot[key]
        t = r[0][r[1] % n]
        r[1] += 1
        return t

    def ps(s, idxs, key):
        r = s.rot.setdefault("ps_" + key, [None, 0])
        t = s.psum[idxs[r[1] % len(idxs)]]
        r[1] += 1
        return t

    def _getsem(s):
        sm = s.free_sems.pop()
        if not s.persist:
            s.phase_sems.append(sm)
        return sm

    def _waits(s, eng, reads, writes, skip_dma_sem=None):
        w = {}

        def need(d):
            if d is None:
                return
            sem, val, src = d
            if src == "pe" and eng == "pe":
                return
            if w.get(sem, 0) < val:
                w[sem] = val

        for t in reads:
            need(t.w)
        for t in writes:
            if not (skip_dma_sem is not None and t.w is not None and t.w[2] == "dma" and t.w[0] is skip_dma_sem):
                need(t.w)
            for d in t.r.values():
                need(d)
        out = []
        kn = s.known[eng]
        for sem, val in w.items():
            if kn.get(sem, 0) < val:
                kn[sem] = val
                out.append((sem, val))
        return out

    def op(s, eng, fn, outs, ins):
        reads = []
        for v in ins:
            if isinstance(v, V) and v.t not in reads:
                reads.append(v.t)
        writes = []
        for v in outs:
            if isinstance(v, V) and v.t not in writes:
                writes.append(v.t)
        wl = s._waits(eng, reads, writes)
        s.cnt[eng] += 1
        me = (s.esem[eng], s.cnt[eng], eng)
        s.q[eng].append((wl, fn, (s.esem[eng], 1)))
        for t in reads:
            t.r[eng] = me
        for t in writes:
            t.w = me
            t.r = {}

    def dma(s, out, in_, queue="sp"):
        reads = [in_.t] if isinstance(in_, V) else []
        writes = [out.t] if isinstance(out, V) else []
        owner = writes[0] if writes else reads[0]
        if owner.dsem is None:
            owner.dsem = s._getsem()
            owner.dcnt = s.semcnt[owner.dsem]
        wl = s._waits(queue, reads, writes, skip_dma_sem=owner.dsem)
        owner.dcnt += 16
        s.semcnt[owner.dsem] = owner.dcnt
        dep = (owner.dsem, owner.dcnt, "dma")
        oap, iap = _a(out), _a(in_)
        s.q[queue].append((wl, lambda e: e.dma_start(out=oap, in_=iap), (owner.dsem, 16)))
        for t in reads:
            t.r[("dma", id(owner))] = dep
        for t in writes:
            t.w = dep
            t.r = {}

    def barrier(s):
        allsem = {}
        for e, sm in s.esem.items():
            allsem[sm] = s.cnt[e]
        for sm, c in s.semcnt.items():
            allsem[sm] = c
        for eng in s.q:
            kn = s.known[eng]
            wl = []
            for sm, c in allsem.items():
                if c > kn.get(sm, 0):
                    kn[sm] = c
                    wl.append((sm, c))
            s.q[eng].append((wl, None, None))

    def end_phase(s):
        s.barrier()
        s.free_sems.extend(s.phase_sems)
        s.phase_sems = []
        s.off = s.base
        s.rot = {}
        for t in s.psum:
            t.w = None; t.r = {}

    def emit(s):
        with s.nc.Block() as block:
            def rep(name):
                def f(e):
                    for wl, fn, inc in s.q[name]:
                        for sem, val in wl:
                            e.wait_ge(sem, val)
                        if fn is not None:
                            fn(e).then_inc(inc[0], inc[1])
                return f
            block.tensor(rep("pe"))
            block.scalar(rep("act"))
            block.vector(rep("dve"))
            block.gpsimd(rep("pool"))
            block.sync(rep("sp"))

    def mm(s, out, lhsT, rhs, start=True, stop=True, skip=False):
        o, l, r = _a(out), _a(lhsT), _a(rhs)
        if skip:
            s.op("pe", lambda e: e.matmul(o, lhsT=l, rhs=r, start=start, stop=stop, skip_group_check=True),
                 [out], [lhsT, rhs])
        else:
            s.op("pe", lambda e: e.matmul(o, lhsT=l, rhs=r, start=start, stop=stop), [out], [lhsT, rhs])

    def tr(s, out, in_, ident):
        o, i, d = _a(out), _a(in_), _a(ident)
        s.op("pe", lambda e: e.transpose(o, i, d), [out], [in_, ident])

    def act(s, out, in_, func, bias=0.0, scale=1.0, accum=None):
        o, i, b, sc, ac = _a(out), _a(in_), _a(bias), _a(scale), _a(accum)
        if ac is None:
            s.op("act", lambda e: e.activation(out=o, in_=i, func=func, bias=b, scale=sc), [out], [in_, bias, scale])
        else:
            s.op("act", lambda e: e.activation(out=o, in_=i, func=func, bias=b, scale=sc, accum_out=ac),
                 [out, accum], [in_, bias, scale])

    def tt(s, eng, out, a, b, op):
        o, x, y = _a(out), _a(a), _a(b)
        s.op(eng, lambda e: e.tensor_tensor(out=o, in0=x, in1=y, op=op), [out], [a, b])

    def ts(s, eng, out, a, s1, s2, op0, op1=None):
        o, x, c1, c2 = _a(out), _a(a), _a(s1), _a(s2)
        if op1 is None:
            s.op(eng, lambda e: e.tensor_scalar(out=o, in0=x, scalar1=c1, scalar2=None, op0=op0), [out], [a, s1])
        else:
            s.op(eng, lambda e: e.tensor_scalar(out=o, in0=x, scalar1=c1, scalar2=c2, op0=op0, op1=op1),
                 [out], [a, s1, s2])

    def stt(s, eng, out, a, sc, b, op0, op1):
        o, x, c, y = _a(out), _a(a), _a(sc), _a(b)
        s.op(eng, lambda e: e.scalar_tensor_tensor(out=o, in0=x, scalar=c, in1=y, op0=op0, op1=op1),
             [out], [a, sc, b])

    def cp(s, eng, out, a):
        o, x = _a(out), _a(a)
        if eng == "act":
            s.op("act", lambda e: e.activation(out=o, in_=x, func=AF.Copy), [out], [a])
        else:
            s.op(eng, lambda e: e.tensor_copy(out=o, in_=x), [out], [a])

    def memset(s, eng, out, val):
        o = _a(out)
        s.op(eng, lambda e: e.memset(o, val), [out], [])

    def recip(s, out, a):
        o, x = _a(out), _a(a)
        s.op("dve", lambda e: e.reciprocal(out=o, in_=x), [out], [a])

    def rsum(s, out, a):
        o, x = _a(out), _a(a)
        s.op("dve", lambda e: e.tensor_reduce(out=o, in_=x, axis=AX.X, op=ALU.add), [out], [a])


def build(NB=2, NL=2048, NCX=256, DEPTH=4, dbg=False):
    TS = NL + NCX
    T = NB * TS
    NC3 = NB + 1
    NKC = TS // 128
    nc = bass.Bass("TRN2", target_bir_lowering=False)

    def din(name, shape, dt=F32):
        return nc.dram_tensor(name, list(shape), dt, kind="ExternalInput").ap()

    skind = "ExternalOutput" if dbg else "Internal"

    def dscr(name, shape, dt):
        return nc.dram_tensor(name, list(shape), dt, kind=skind).ap()

    xin = din("xT", [D, T])
    cvec_in = din("cvec", [128, 8 * NC3])
    pvec_in = din("pvec", [DEPTH, 128, NPV])
    gvec_in = din("gvec", [128, 8])
    consts_in = din("consts", [128, 6 * 128])
    rope_in = din("rope", [128, 2 * NL])
    ada_w = din("ada_w", [DEPTH, D, 6 * D])
    w_in = din("w_in", [DEPTH, D, IN_DIM])
    w_ao = din("w_attn_o", [DEPTH, D, D])
    w_so = din("w_ssd_o", [DEPTH, 2 * D, D])
    w_o = din("w_out", [DEPTH, D, D])
    w_1 = din("w_mlp1", [DEPTH, D, 4 * D])
    w_2 = din("w_mlp2", [DEPTH, 4 * D, D])
    outT = nc.dram_tensor("outT", [D, NB * NL], F32, kind="ExternalOutput").ap()

    xres = dscr("xres", [D, T], F32)
    qT = dscr("qT", [D, T], BF16)
    kT = dscr("kT", [D, T], BF16)
    vaug = dscr("vaug", [T, 8, 129], BF16)
    szT = dscr("szT", [2 * D, T], BF16)
    xbcT = dscr("xbcT", [4 * D, T], BF16)
    xbtok = dscr("xbtok", [T, 3 * D], BF16)
    dtD = dscr("dtD", [T, 64], F32)
    sgT = dscr("sgT", [2 * D, T], BF16)
    attT = dscr("attT", [D, T], BF16)
    yT = dscr("yT", [2 * D, T], F32)
    gT = dscr("gT", [2 * D, T], BF16)
    uT = dscr("uT", [4 * D, T], BF16)

    def fm(ap):
        return ap.rearrange("(k p) t -> p k t", p=128)

    seqs = []
    for b in range(NB):
        seqs.append((b, 0, b * TS, NL))
        seqs.append((b, 1, b * TS + NL, NCX))
    tiles = []
    for si, (b, ic, so, sl) in enumerate(seqs):
        for o in range(0, sl, 512):
            n = min(512, sl - o)
            tiles.append((b, ic, so + o, n, o, si, o + n >= sl))

    with ExitStack() as stack:
        k = KB(nc, stack)
        k.persist = True
        cst_f = k.alloc([6, 128], F32, "cst_f")
        cst = k.alloc([6, 128], BF16, "cst")
        pv = k.alloc([DEPTH, NPV], F32, "pv")
        modT = k.alloc([DEPTH, 48, NC3], F32, "modT")
        gmA = k.alloc([DEPTH, 8, NC3], F32, "gmA")
        gmF = k.alloc([DEPTH, 8, NC3], F32, "gmF")
        aneg = k.alloc([DEPTH, 64], F32, "aneg")
        nlam = k.alloc([DEPTH, 1], F32, "nlam")
        subg = k.alloc([DEPTH, 128], F32, "subg")
        fg = k.alloc([8], F32, "fg")
        cact_f = k.alloc([8, NC3], F32, "cact_f")
        cact = k.alloc([8, NC3], BF16, "cact")
        sml = k.alloc([16], F32, "sml")
        k.dma(cst_f.v(), consts_in.rearrange("p (a b) -> p a b", a=6))
        k.dma(fg.v(), gvec_in)
        k.dma(cact_f.v(), cvec_in.rearrange("p (a b) -> p a b", a=8))
        for l in range(DEPTH):
            k.dma(pv[:, l, :], pvec_in[l])
        k.cp("dve", cst.v(), cst_f.v())
        k.act(cact_f.v(), cact_f.v(), AF.Silu)
        k.cp("dve", cact.v(), cact_f.v())
        IDN, LE, GT, GE, LT, ONE = (cst[:, i, :] for i in range(6))
        k.persist = False
        k.base = k.off

        def pvs(l, name):
            a, b = PV[name]
            return pv[:, l, a:b]

        for l in range(DEPTH):
            wk = [k.ring("adaw", 8, [6 * D], BF16) for _ in range(8)]
            for kk in range(8):
                k.dma(wk[kk].v(), ada_w[l, kk * 128:(kk + 1) * 128, :], queue="pool")
            ps = k.ps([0, 1], "m")
            for j in range(48):
                for kk in range(8):
                    k.mm(ps[:, j * NC3:(j + 1) * NC3], wk[kk][:, j * 128:(j + 1) * 128], cact[:, kk, :],
                         start=(kk == 0), stop=(kk == 7))
            k.tt("dve", modT[:, l, :, :], ps[:, 0:48 * NC3].re("p (j c) -> p j c", c=NC3),
                 pvs(l, "adab").un(2).bc([128, 48, NC3]), ALU.add)
            for m in (1, 4):
                k.ts("dve", modT[:, l, 8 * m:8 * m + 8, :], modT[:, l, 8 * m:8 * m + 8, :], 1.0, None, ALU.add)
            k.tt("dve", gmA[:, l, :, :], modT[:, l, 8:16, :], pvs(l, "gmix").un(2).bc([128, 8, NC3]), ALU.mult)
            k.tt("dve", gmF[:, l, :, :], modT[:, l, 32:40, :], pvs(l, "gmlp").un(2).bc([128, 8, NC3]), ALU.mult)
            k.act(aneg[:, l, :], pvs(l, "alog"), AF.Exp)
            k.ts("dve", aneg[:, l, :], aneg[:, l, :], -1.0, None, ALU.mult)
            lam_init = 0.8 - 0.6 * math.exp(-0.3 * l)
            tmpl = k.ring("lamt", 2, [64], F32)
            k.tt("dve", tmpl.v(), pvs(l, "lq1"), pvs(l, "lk1"), ALU.mult)
            k.rsum(sml[:, 0:1], tmpl.v())
            tmpl2 = k.ring("lamt", 2, [64], F32)
            k.tt("dve", tmpl2.v(), pvs(l, "lq2"), pvs(l, "lk2"), ALU.mult)
            k.rsum(sml[:, 1:2], tmpl2.v())
            k.act(sml[:, 2:4], sml[:, 0:2], AF.Exp)
            k.tt("dve", sml[:, 4:5], sml[:, 3:4], sml[:, 2:3], ALU.subtract)
            k.ts("dve", nlam[:, l, :], sml[:, 4:5], -lam_init, None, ALU.add)
            k.ts("dve", subg[:, l, :], pvs(l, "subg"), 1.0 - lam_init, None, ALU.mult)
        k.end_phase()

        def modulate(src, hts, gm, l, shift_m, tl):
            for ti, (b, ic, off, n, pos0, si, last) in enumerate(tl):
                col = NB if ic else b
                xt = k.ring("mod_x", 2, [8, 512], F32)
                k.dma(xt[:, :, :n], fm(src)[:, :, off:off + n])
                sq = k.ring("mod_sq", 2, [8, 512], BF16)
                k.act(sq[:, :, :n], xt[:, :, :n], AF.Square)
                ps = k.ps([0, 1], "mod")
                for j in range(8):
                    k.mm(ps[:, :n], ONE, sq[:, j, :n], start=(j == 0), stop=(j == 7))
                rstd = k.ring("mod_r", 2, [512], F32)
                k.ts("dve", rstd[:, :n], ps[:, :n], 1.0 / D, 1e-6, ALU.mult, ALU.add)
                k.act(rstd[:, :n], rstd[:, :n], AF.Sqrt)
                k.recip(rstd[:, :n], rstd[:, :n])
                xn = k.ring("mod_xn", 2, [8, 512], F32)
                k.tt("dve", xn[:, :, :n], xt[:, :, :n], rstd[:, :n].un(1).bc([128, 8, n]), ALU.mult)
                for j in range(8):
                    k.act(hts[ti][:, j, :n], xn[:, j, :n], AF.Identity,
                          bias=modT[:, l, 8 * shift_m + j, col:col + 1], scale=gm[:, l, j, col:col + 1])

        for l in range(DEPTH):
            last_layer = (l == DEPTH - 1)
            xsrc = xin if l == 0 else xres
            act_tiles = [t for t in tiles if not (last_layer and t[1])]
            hts = [k.alloc([8, tiles[i][3]], BF16, "hT%d" % i) for i in range(len(tiles))]
            mark = k.off
            modulate(xsrc, hts, gmA, l, 0, tiles)
            k.barrier()
            k.off = mark
            k.rot = {}
            rope = k.alloc([2, NL], F32, "rope")
            k.dma(rope.v(), rope_in.rearrange("p (a b) -> p a b", a=2))
            vst = [k.alloc([4, 129], BF16, "vst%d" % i) for i in range(3)]
            for t_ in vst:
                k.memset("pool", t_[:, :, 128:129], 1.0)
            vst_i = 0
            segs = (("q", 0, 1024), ("k", 1024, 2048), ("v", 2048, 3072), ("z", 3072, 5120),
                    ("xbc", 5120, 9216), ("dt", 9216, 9280), ("ga", 9280, 10304), ("gs", 10304, 11328))
            wfm = w_in[l].rearrange("(k p) c -> p k c", p=128)
            for sname, s0, s1 in segs:
                for c0 in range(s0, s1, 512):
                    cw_ = min(512, s1 - c0)
                    wb = k.ring("wb", 2, [8, 512], BF16)
                    k.dma(wb[:, :, :cw_], wfm[:, :, c0:c0 + cw_], queue="pool")
                    if sname == "v":
                        vb = (c0 - s0) // 512
                        for ti, (b, ic, off, n, pos0, si, last) in enumerate(tiles):
                            for tb in range(n // 128):
                                ps = k.ps([0, 1, 2, 3, 4, 5], "b")
                                for kk in range(8):
                                    k.mm(ps[:, :], hts[ti][:, kk, tb * 128:(tb + 1) * 128], wb[:, kk, :],
                                         start=(kk == 0), stop=(kk == 7))
                                vs_ = vst[vst_i % 3]; vst_i += 1
                                k.act(vs_[:, :, 0:128], ps[:, :].re("p (h e) -> p h e", h=4), AF.Copy)
                                t0 = off + tb * 128
                                k.dma(vaug[t0:t0 + 128, vb * 4:(vb + 1) * 4, :], vs_.v())
                        continue
                    if sname == "dt":
                        for ti, (b, ic, off, n, pos0, si, last) in enumerate(tiles):
                            nb_ = n // 128
                            ps = k.ps([0, 1, 2, 3, 4, 5], "b")
                            for tb in range(nb_):
                                for kk in range(8):
                                    k.mm(ps[:, tb * 64:(tb + 1) * 64], hts[ti][:, kk, tb * 128:(tb + 1) * 128],
                                         wb[:, kk, :64], start=(kk == 0), stop=(kk == 7))
                            d1 = k.ring("dt1", 2, [4, 64], F32)
                            k.tt("dve", d1[:, :nb_, :], ps[:, :nb_ * 64].re("p (a b) -> p a b", b=64),
                                 pvs(l, "dtb").un(1).bc([128, nb_, 64]), ALU.add)
                            k.act(d1[:, :nb_, :], d1[:, :nb_, :], AF.Exp)
                            d2 = k.ring("dt2", 2, [4, 64], F32)
                            k.act(d2[:, :nb_, :], d1[:, :nb_, :], AF.Ln, bias=1.0)
                            k.dma(dtD[off:off + n, :].rearrange("(a p) c -> p a c", p=128), d2[:, :nb_, :])
                        continue
                    for jc in range(cw_ // 128):
                        gcol = c0 + jc * 128
                        stage = None
                        for ti, (b, ic, off, n, pos0, si, last) in enumerate(tiles):
                            ps = k.ps([0, 1, 2, 3, 4, 5], "b")
                            for kk in range(8):
                                k.mm(ps[:, :n], wb[:, kk, jc * 128:(jc + 1) * 128], hts[ti][:, kk, :n],
                                     start=(kk == 0), stop=(kk == 7))
                            if sname in ("q", "k"):
                                h = (gcol - s0) // 128
                                dst = qT if sname == "q" else kT
                                scl = 0.125 if sname == "q" else 1.0
                                ob = k.ring("ob", 3, [512], BF16)
                                if ic:
                                    k.act(ob[:, :n], ps[:, :n], AF.Identity, scale=scl)
                                else:
                                    qf = k.ring("qf", 2, [512], F32)
                                    k.act(qf[:, :n], ps[:, :n], AF.Identity, scale=scl)
                                    A = k.ring("ropeA", 2, [512], F32)
                                    k.tt("pool", A[:, :n], qf[:, :n], rope[:, 0, pos0:pos0 + n], ALU.mult)
                                    Bt = k.ring("ropeB", 2, [512], F32)
                                    for g in range(4):
                                        gs_ = g ^ 1
                                        k.tt("dve", Bt[32 * g:32 * g + 32, :n], qf[32 * gs_:32 * gs_ + 32, :n],
                                             rope[32 * gs_:32 * gs_ + 32, 1, pos0:pos0 + n], ALU.mult)
                                    k.tt("pool", ob[:, :n], A[:, :n], Bt[:, :n], ALU.add)
                                k.dma(dst[h * 128:(h + 1) * 128, off:off + n], ob[:, :n])
                            elif sname == "z":
                                r0 = gcol - s0
                                ob = k.ring("ob", 3, [512], BF16)
                                k.act(ob[:, :n], ps[:, :n], AF.Silu)
                                k.dma(szT[r0:r0 + 128, off:off + n], ob[:, :n])
                            elif sname in ("ga", "gs"):
                                r0 = gcol - 9280
                                ob = k.ring("ob", 3, [512], BF16)
                                k.act(ob[:, :n], ps[:, :n], AF.Sigmoid)
                                k.dma(sgT[r0:r0 + 128, off:off + n], ob[:, :n])
                            else:
                                jx = (gcol - s0) // 128
                                sb_, sic, soff, slen = seqs[si]
                                if pos0 == 0:
                                    stage = k.ring("stage", 2, [NL + 4], F32)
                                    k.memset("pool", stage[:, 0:2], 0.0)
                                    k.memset("pool", stage[:, 2 + slen:4 + slen], 0.0)
                                k.act(stage[:, 2 + pos0:2 + pos0 + n], ps[:, :n], AF.Copy)
                                if last:
                                    a0, _ = PV["convw"]
                                    acc = k.ring("cacc", 2, [NL], F32)
                                    k.act(acc[:, :slen], stage[:, 0:slen], AF.Identity,
                                          scale=pv[:, l, a0 + jx * 5:a0 + jx * 5 + 1])
                                    for kq in range(1, 5):
                                        k.stt("dve", acc[:, :slen], stage[:, kq:kq + slen],
                                              pv[:, l, a0 + jx * 5 + kq:a0 + jx * 5 + kq + 1], acc[:, :slen],
                                              ALU.mult, ALU.add)
                                    cvo = k.ring("cvo", 2, [NL], BF16)
                                    b0_, _ = PV["convb"]
                                    k.act(cvo[:, :slen], acc[:, :slen], AF.Silu, bias=pv[:, l, b0_ + jx:b0_ + jx + 1])
                                    k.dma(xbcT[jx * 128:(jx + 1) * 128, soff:soff + slen], cvo[:, :slen])
                                    if jx < 24:
                                        for t4 in range(0, slen // 128, 4):
                                            nq = min(4, slen // 128 - t4)
                                            pst = k.ps([6, 7], "bt")
                                            pv_ = V(pst, pst.ap.bitcast(BF16))
                                            for q_ in range(nq):
                                                k.tr(pv_[:, q_ * 128:(q_ + 1) * 128],
                                                     cvo[:, (t4 + q_) * 128:(t4 + q_ + 1) * 128], IDN)
                                            tst = k.ring("tst", 2, [512], BF16)
                                            k.cp("act", tst[:, :nq * 128], pv_[:, :nq * 128])
                                            tk0 = soff + t4 * 128
                                            k.dma(xbtok[tk0:tk0 + nq * 128, jx * 128:(jx + 1) * 128]
                                                  .rearrange("(q p) c -> p q c", p=128),
                                                  tst[:, :nq * 128].re("p (q c) -> p q c", c=128))
            k.end_phase()

            deferred = []

            def flush_deferred():
                while deferred:
                    deferred.pop(0)()
            for b in range(NB):
                kts = []
                for h in range(8):
                    t_ = k.ring("KT%d" % h, 1, [TS], BF16)
                    k.dma(t_.v(), kT[h * 128:(h + 1) * 128, b * TS:(b + 1) * TS])
                    kts.append(t_)
                va = k.ring("VA", 1, [NKC, 8, 129], BF16)
                k.dma(va.v(), vaug[b * TS:(b + 1) * TS].rearrange("(c p) h e -> p c h e", p=128))
                for h in range(8):
                    for (tb_, ic, off, n, pos0, si, last) in tiles:
                        if tb_ != b or (ic and last_layer):
                            continue
                        nqb = n // 128
                        qt = k.ring("QT", 2, [512], BF16)
                        k.dma(qt[:, :n], qT[h * 128:(h + 1) * 128, off:off + n])
                        kcs = list(range(NL // 128, NKC)) if ic else list(range(NKC))
                        obank = [k.psum[4], k.psum[5], k.psum[6]]

                        def oreg(c, qb):
                            r = c * 4 + qb
                            return obank[r // 3][:, (r % 3) * 129:(r % 3) * 129 + 129]

                        def qk(kc):
                            pts = []
                            for c in range(2):
                                pss = k.ps([0, 1, 2, 3], "s")
                                k.mm(pss[:, :n], kts[h][64 * c:64 * c + 64, kc * 128:(kc + 1) * 128],
                                     qt[64 * c:64 * c + 64, :n])
                                pt = k.ring("PT", 6, [512], BF16)
                                k.act(pt[:, :n], pss[:, :n], AF.Exp)
                                pts.append(pt)
                            return pts
                        started = set()
                        pend = qk(kcs[0])
                        for i_, kc in enumerate(kcs):
                            pts = pend
                            if i_ + 1 < len(kcs):
                                pend = qk(kcs[i_ + 1])
                            if i_ == 1:
                                flush_deferred()
                            for c in range(2):
                                for qb in range(nqb):
                                    r = c * 4 + qb
                                    st = (r // 3) not in started
                                    started.add(r // 3)
                                    k.mm(oreg(c, qb), pts[c][:, qb * 128:(qb + 1) * 128], va[:, kc, h, :],
                                         start=st, stop=(kc == kcs[-1]), skip=True)
                        osb = k.ring("osb", 2, [3, 387], F32)
                        for bi in sorted(started):
                            k.cp("dve", osb[:, bi, :], obank[bi][:, 0:387])

                        def osr(c, qb):
                            r = c * 4 + qb
                            return osb[:, r // 3, (r % 3) * 129:(r % 3) * 129 + 129]
                        rec = k.ring("rec", 2, [8], F32)
                        for c in range(2):
                            for qb in range(nqb):
                                k.recip(rec[:, c * 4 + qb:c * 4 + qb + 1], osr(c, qb)[:, 128:129])
                        k.ts("dve", rec[:, 4:8], rec[:, 4:8], nlam[:, l, :], None, ALU.mult)
                        ssq = k.ring("ssq", 2, [4], F32)
                        k.memset("dve", ssq.v(), 0.0)
                        os_ = []
                        for qb in range(nqb):
                            t1 = k.ring("at1", 2, [128], F32)
                            k.ts("dve", t1.v(), osr(0, qb)[:, 0:128], rec[:, qb:qb + 1], None, ALU.mult)
                            o_ = k.ring("ao", 8, [128], F32)
                            k.stt("dve", o_.v(), osr(1, qb)[:, 0:128], rec[:, 4 + qb:5 + qb], t1.v(),
                                  ALU.mult, ALU.add)
                            junk = k.ring("ajunk", 2, [128], F32)
                            k.act(junk.v(), o_.v(), AF.Square, accum=ssq[:, qb:qb + 1])
                            os_.append(o_)
                        rs = k.ring("ars", 2, [4], F32)
                        k.ts("dve", rs.v(), ssq.v(), 1.0 / 128, 1e-6, ALU.mult, ALU.add)
                        k.act(rs.v(), rs.v(), AF.Ln)
                        k.act(rs.v(), rs.v(), AF.Exp, scale=-0.5)
                        on = k.ring("aon", 3, [4, 128], BF16)
                        for qb in range(nqb):
                            k.stt("dve", on[:, qb, :], os_[qb].v(), rs[:, qb:qb + 1], subg[:, l, :],
                                  ALU.mult, ALU.mult)

                        def fin(on=on, n=n, nqb=nqb, h=h, off=off):
                            pst = k.psum[7]
                            pv_ = V(pst, pst.ap.bitcast(BF16))
                            for qb in range(nqb):
                                k.tr(pv_[:, qb * 128:(qb + 1) * 128], on[:, qb, :], IDN)
                            ob = k.ring("ob", 3, [512], BF16)
                            k.cp("act", ob[:, :n], pv_[:, :n])
                            k.dma(attT[h * 128:(h + 1) * 128, off:off + n], ob[:, :n])
                        deferred.append(fin)
            flush_deferred()
            k.end_phase()

            d0, _ = PV["ssdd"]
            LA = 3
            for b in range(NB):
                for dr in range(2):
                    Ms, Um, Vm, Dm = (GT, LE, LE, GT) if dr == 0 else (LT, GE, GE, LT)
                    lastcol = 127 if dr == 0 else 0
                    STg = [k.ring("ST%d" % g, 1, [256], F32) for g in range(8)]
                    STb = [k.ring("STb%d" % g, 1, [4, 128], BF16) for g in range(8)]
                    for g in range(8):
                        k.memset("pool", STg[g].v(), 0.0)
                        k.memset("pool", STb[g].v(), 0.0)
                    cl = [(1, ci) for ci in range(NCX // 128)] + [(0, ci) for ci in range(NL // 128)]
                    if dr == 1:
                        cl = [(1, ci) for ci in reversed(range(NCX // 128))] + \
                             [(0, ci) for ci in reversed(range(NL // 128))]

                    def chunk_pre(idx, b=b, dr=dr, cl=cl, Dm=Dm, Um=Um):
                        ic, ci = cl[idx]
                        C = {}
                        tok0 = b * TS + (NL if ic else 0) + ci * 128
                        C["tok0"] = tok0
                        xs_tok = k.ring("xs_tok", 2, [32, 64], BF16)
                        k.dma(xs_tok.v(), xbtok[tok0:tok0 + 128, 0:2048].rearrange("p (h e) -> p h e", e=64))
                        B_tok = k.ring("B_tok", 2, [1024], BF16)
                        k.dma(B_tok.v(), xbtok[tok0:tok0 + 128, 2048:3072])
                        BT = k.ring("BT", 2, [8, 128], BF16)
                        k.dma(BT.v(), fm(xbcT[2048:3072, :])[:, :, tok0:tok0 + 128])
                        CT = k.ring("CT", 2, [8, 128], BF16)
                        k.dma(CT.v(), fm(xbcT[3072:4096, :])[:, :, tok0:tok0 + 128])
                        dt = k.ring("dt", 2, [32], F32)
                        k.dma(dt.v(), dtD[tok0:tok0 + 128, dr * 32:(dr + 1) * 32])
                        C.update(B_tok=B_tok, BT=BT, CT=CT)
                        if dr == 1:
                            yf = k.ring("yf", 2, [16, 128], F32)
                            k.dma(yf.v(), fm(yT)[:, :, tok0:tok0 + 128])
                            xsT = k.ring("xsT", 2, [16, 128], BF16)
                            k.dma(xsT.v(), fm(xbcT[0:2048, :])[:, :, tok0:tok0 + 128])
                            szt = k.ring("szt", 2, [16, 128], BF16)
                            k.dma(szt.v(), fm(szT)[:, :, tok0:tok0 + 128])
                            C.update(yf=yf, xsT=xsT, szt=szt)
                        dta = k.ring("dta", 2, [32], F32)
                        k.tt("dve", dta.v(), dt.v(), aneg[:, l, dr * 32:(dr + 1) * 32], ALU.mult)
                        dhi = k.ring("dhi", 2, [32], BF16)
                        k.cp("dve", dhi.v(), dta.v())
                        dlf = k.ring("dlf", 2, [32], F32)
                        k.tt("dve", dlf.v(), dta.v(), dhi.v(), ALU.subtract)
                        dlo = k.ring("dlo", 2, [32], BF16)
                        k.cp("dve", dlo.v(), dlf.v())
                        psd = k.psum[7]
                        k.mm(psd[:, 0:32], Dm, dhi.v(), start=True, stop=False)
                        k.mm(psd[:, 0:32], Dm, dlo.v(), start=False, stop=True)
                        dte = k.ring("dte", 2, [32], F32)
                        k.act(dte.v(), psd[:, 0:32], AF.Exp)
                        xcp = k.ring("xcp", 2, [32, 128], BF16)
                        if k.rot["xcp"][1] <= 2:
                            k.memset("pool", xcp.v(), 0.0)
                        xcv = xcp.v().re("p (a two) c -> p a two c", two=2)
                        xsv = xs_tok.v().re("p (a two) e -> p a two e", two=2)
                        dtv = dt.v().re("p (a two) -> p a two", two=2)
                        for par in range(2):
                            k.tt("pool", xcv[:, :, par, par * 64:par * 64 + 64], xsv[:, :, par, :],
                                 dtv[:, :, par].un(2).bc([128, 16, 64]), ALU.mult)
                        xcd = k.ring("xcd", 2, [32, 64], BF16)
                        dev = dte.v().re("p (a two) -> p a two", two=2)
                        xdv = xcd.v().re("p (a two) e -> p a two e", two=2)
                        for par in range(2):
                            k.tt("pool", xdv[:, :, par, :], xcv[:, :, par, par * 64:par * 64 + 64],
                                 dev[:, :, par].un(2).bc([128, 16, 64]), ALU.mult)
                        rhi = k.ring("rhi", 2, [32, 128], BF16)
                        k.tt("dve", rhi.v(), dhi.v().un(2).bc([128, 32, 128]), Um.un(1).bc([128, 32, 128]), ALU.mult)
                        rlo = k.ring("rlo", 2, [32, 128], BF16)
                        k.tt("pool", rlo.v(), dlo.v().un(2).bc([128, 32, 128]), Um.un(1).bc([128, 32, 128]), ALU.mult)
                        C.update(xcp=xcp, xcd=xcd, rhi=rhi, rlo=rlo)
                        return C

                    def stage1(C, g, Ms=Ms, Vm=Vm):
                        rhi, rlo, BT, CT = C["rhi"], C["rlo"], C["BT"], C["CT"]
                        r4h = rhi[:, 4 * g:4 * g + 4, :].re("p h l -> p (h l)")
                        r4l = rlo[:, 4 * g:4 * g + 4, :].re("p h l -> p (h l)")
                        pseg = k.ps([0, 1, 2], "seg")
                        k.mm(pseg[:, :], Ms, r4h, start=True, stop=False)
                        k.mm(pseg[:, :], Ms, r4l, start=False, stop=True)
                        pcs = k.ps([0, 1, 2], "seg")
                        k.mm(pcs[:, :], ONE, r4h, start=True, stop=False)
                        k.mm(pcs[:, :], ONE, r4l, start=False, stop=True)
                        E = k.ring("E", LA + 2, [4, 128], BF16)
                        k.act(E.v().re("p h l -> p (h l)"), pseg[:, :], AF.Exp)
                        E0 = k.ring("E0", LA + 2, [4, 128], F32)
                        k.act(E0.v().re("p h l -> p (h l)"), pcs[:, :], AF.Exp)
                        pcb = k.psum[4]
                        k.mm(pcb[:, 0:128], BT[:, g, :], CT[:, g, :])
                        cbm = k.ring("cbm", LA + 2, [128], BF16)
                        k.tt("dve", cbm.v(), pcb[:, 0:128], Vm, ALU.mult)
                        MT = k.ring("MT", LA + 2, [4, 128], BF16)
                        k.tt("dve", MT.v(), E.v(), cbm.v().un(1).bc([128, 4, 128]), ALU.mult)
                        MV = k.ring("MV", LA + 2, [4, 128], BF16)
                        k.tt("pool", MV.v(), E0.v(), CT[:, g, :].un(1).bc([128, 4, 128]), ALU.mult)
                        return (E0, MT, MV)

                    def stage2(C, g, S, dr=dr, lastcol=lastcol, STg=STg, STb=STb):
                        E0, MT, MV = S
                        xcp, xcd, B_tok, tok0 = C["xcp"], C["xcd"], C["B_tok"], C["tok0"]
                        py = k.ps([3, 5], "y")
                        for hp in range(2):
                            for hh in range(2):
                                hl = 2 * hp + hh
                                k.mm(py[:, hp * 128:(hp + 1) * 128], xcp[:, 4 * g + hl, :], MT[:, hl, :],
                                     start=(hh == 0), stop=False)
                            for hh in range(2):
                                hl = 2 * hp + hh
                                k.mm(py[:, hp * 128:(hp + 1) * 128], STb[g][:, hl, :], MV[:, hl, :],
                                     start=False, stop=(hh == 1))
                        if dr == 0:
                            yst = k.ring("yst", 3, [2, 128], F32)
                            k.cp("act", yst.v().re("p a t -> p (a t)"), py[:, 0:256])
                            k.dma(yT[2 * g * 128:(2 * g + 2) * 128, tok0:tok0 + 128]
                                  .rearrange("(a p) t -> p a t", p=128), yst.v())
                        else:
                            yf, xsT, szt = C["yf"], C["xsT"], C["szt"]
                            gs1 = k.ring("gs1", 2, [2, 128], F32)
                            k.tt("dve", gs1.v(), py[:, 0:256].re("p (a t) -> p a t", a=2),
                                 yf[:, 2 * g:2 * g + 2, :], ALU.add)
                            gs2 = k.ring("gs2", 2, [2, 128], F32)
                            for hp in range(2):
                                k.stt("dve", gs2[:, hp, :], xsT[:, 2 * g + hp, :],
                                      pv[:, l, d0 + 2 * g + hp:d0 + 2 * g + hp + 1], gs1[:, hp, :],
                                      ALU.mult, ALU.add)
                            go = k.ring("go", 3, [2, 128], BF16)
                            k.tt("pool", go.v(), gs2.v(), szt[:, 2 * g:2 * g + 2, :], ALU.mult)
                            k.dma(gT[2 * g * 128:(2 * g + 2) * 128, tok0:tok0 + 128]
                                  .rearrange("(a p) t -> p a t", p=128), go.v())
                        pst_ = k.psum[6]
                        k.mm(pst_[:, 0:256], B_tok[:, g * 128:(g + 1) * 128],
                             xcd[:, 4 * g:4 * g + 4, :].re("p h e -> p (h e)"))
                        tmp = k.ring("sttmp", 2, [4, 64], F32)
                        k.tt("dve", tmp.v(), STg[g].v().re("p (h e) -> p h e", e=64),
                             E0[:, :, lastcol:lastcol + 1].bc([128, 4, 64]), ALU.mult)
                        k.tt("dve", STg[g].v(), tmp.v().re("p h e -> p (h e)"), pst_[:, 0:256], ALU.add)
                        sgv = STg[g].v().re("p (a two e) -> p a two e", two=2, e=64)
                        sbv = STb[g].v().re("p (a two) c -> p a two c", two=2)
                        for par in range(2):
                            k.cp("act", sbv[:, :, par, par * 64:par * 64 + 64], sgv[:, :, par, :])

                    Cs = {}

                    def get_C(idx):
                        if idx not in Cs:
                            Cs[idx] = chunk_pre(idx)
                        return Cs[idx]
                    work = [(idx, g) for idx in range(len(cl)) for g in range(8)]
                    s1 = {}
                    for i_ in range(len(work) + LA):
                        if i_ < len(work):
                            idx, g = work[i_]
                            C_ = get_C(idx)
                            if g == 4 and idx + 1 < len(cl):
                                get_C(idx + 1)
                            s1[i_] = stage1(C_, g)
                        j_ = i_ - LA
                        if j_ >= 0:
                            idx, g = work[j_]
                            stage2(get_C(idx), g, s1.pop(j_))
                            if g == 7:
                                Cs.pop(idx)
            k.end_phase()

            wA = k.alloc([8, D], BF16, "wA")
            wS = k.alloc([16, D], BF16, "wS")
            wO = k.alloc([8, D], BF16, "wO")
            for kk in range(8):
                k.dma(wA[:, kk, :], w_ao[l, kk * 128:(kk + 1) * 128, :], queue="pool")
            for kk in range(16):
                k.dma(wS[:, kk, :], w_so[l, kk * 128:(kk + 1) * 128, :], queue="pool")
            for kk in range(8):
                k.dma(wO[:, kk, :], w_o[l, kk * 128:(kk + 1) * 128, :], queue="pool")
            g0, _ = PV["ssdg"]
            for kk in range(16):
                k.ts("dve", wS[:, kk, :], wS[:, kk, :], pv[:, l, g0 + kk:g0 + kk + 1], None, ALU.mult)
            for (b, ic, off, n, pos0, si, last) in act_tiles:
                col = NB if ic else b
                aT = k.ring("aT", 1, [8, 512], BF16)
                k.dma(aT[:, :, :n], fm(attT)[:, :, off:off + n])
                gt = k.ring("gTt", 1, [16, 512], BF16)
                k.dma(gt[:, :, :n], fm(gT)[:, :, off:off + n])
                sg = k.ring("sg", 1, [16, 512], BF16)
                k.dma(sg[:, :, :n], fm(sgT)[:, :, off:off + n])
                xt = k.ring("xt", 1, [8, 512], F32)
                k.dma(xt[:, :, :n], fm(xsrc)[:, :, off:off + n])
                sq = k.ring("sq", 1, [16, 512], BF16)
                k.act(sq[:, :, :n], gt[:, :, :n], AF.Square)
                pss = k.psum[0]
                for j in range(16):
                    k.mm(pss[:, :n], ONE, sq[:, j, :n], start=(j == 0), stop=(j == 15))
                rstd = k.ring("rstd", 2, [512], F32)
                k.ts("dve", rstd[:, :n], pss[:, :n], 1.0 / (2 * D), 1e-6, ALU.mult, ALU.add)
                k.act(rstd[:, :n], rstd[:, :n], AF.Sqrt)
                k.recip(rstd[:, :n], rstd[:, :n])
                mT = k.ring("mT", 1, [8, 512], BF16)
                for dj in range(8):
                    psA = k.ps([1, 2], "A")
                    for kk in range(8):
                        k.mm(psA[:, :n], wA[:, kk, dj * 128:(dj + 1) * 128], aT[:, kk, :n],
                             start=(kk == 0), stop=(kk == 7))
                    psS = k.ps([3, 4], "S")
                    for kk in range(16):
                        k.mm(psS[:, :n], wS[:, kk, dj * 128:(dj + 1) * 128], gt[:, kk, :n],
                             start=(kk == 0), stop=(kk == 15))
                    t1 = k.ring("et1", 2, [512], F32)
                    k.tt("dve", t1[:, :n], psA[:, :n], sg[:, dj, :n], ALU.mult)
                    t2 = k.ring("et2", 2, [512], F32)
                    k.tt("dve", t2[:, :n], psS[:, :n], rstd[:, :n], ALU.mult)
                    t3 = k.ring("et3", 2, [512], F32)
                    k.tt("pool", t3[:, :n], t2[:, :n], sg[:, 8 + dj, :n], ALU.mult)
                    k.tt("pool", mT[:, dj, :n], t1[:, :n], t3[:, :n], ALU.add)
                for dj in range(8):
                    psO = k.ps([5, 6], "O")
                    for kk in range(8):
                        k.mm(psO[:, :n], wO[:, kk, dj * 128:(dj + 1) * 128], mT[:, kk, :n],
                             start=(kk == 0), stop=(kk == 7))
                    k.stt("dve", xt[:, dj, :n], psO[:, :n], modT[:, l, 16 + dj, col:col + 1], xt[:, dj, :n],
                          ALU.mult, ALU.add)
                k.dma(fm(xres)[:, :, off:off + n], xt[:, :, :n])
            k.end_phase()

            hts = [k.alloc([8, act_tiles[i][3]], BF16, "h2T%d" % i) for i in range(len(act_tiles))]
            mark = k.off
            modulate(xres, hts, gmF, l, 3, act_tiles)
            k.barrier()
            k.off = mark
            k.rot = {}
            w1fm = w_1[l].rearrange("(k p) c -> p k c", p=128)
            for c0 in range(0, 4 * D, 512):
                wb = k.ring("wb", 3, [8, 512], BF16)
                k.dma(wb.v(), w1fm[:, :, c0:c0 + 512], queue="pool")
                for jc in range(4):
                    r0 = c0 + jc * 128
                    for ti, (b, ic, off, n, pos0, si, last) in enumerate(act_tiles):
                        ps = k.ps([0, 1, 2, 3, 4, 5, 6, 7], "f1")
                        for kk in range(8):
                            k.mm(ps[:, :n], wb[:, kk, jc * 128:(jc + 1) * 128], hts[ti][:, kk, :n],
                                 start=(kk == 0), stop=(kk == 7))
                        r_ = k.ring("relu", 3, [512], BF16)
                        k.act(r_[:, :n], ps[:, :n], AF.Relu)
                        ob = k.ring("ob", 3, [512], BF16)
                        k.tt("pool", ob[:, :n], r_[:, :n], r_[:, :n], ALU.mult)
                        k.dma(uT[r0:r0 + 128, off:off + n], ob[:, :n])
            k.end_phase()

            w2 = k.alloc([32, D], BF16, "w2")
            for kk in range(32):
                k.dma(w2[:, kk, :], w_2[l, kk * 128:(kk + 1) * 128, :], queue="pool")
            for (b, ic, off, n, pos0, si, last) in act_tiles:
                col = NB if ic else b
                ut = k.ring("ut", 2, [32, 512], BF16)
                k.dma(ut[:, :, :n], fm(uT)[:, :, off:off + n])
                xt = k.ring("xt", 2, [8, 512], F32)
                k.dma(xt[:, :, :n], fm(xres)[:, :, off:off + n])
                for dj in range(8):
                    ps = k.ps([0, 1, 2, 3], "f2")
                    for kk in range(32):
                        k.mm(ps[:, :n], w2[:, kk, dj * 128:(dj + 1) * 128], ut[:, kk, :n],
                             start=(kk == 0), stop=(kk == 31))
                    k.stt("dve", xt[:, dj, :n], ps[:, :n], modT[:, l, 40 + dj, col:col + 1], xt[:, dj, :n],
                          ALU.mult, ALU.add)
                k.dma(fm(xres)[:, :, off:off + n], xt[:, :, :n])
            k.end_phase()

        for (b, ic, off, n, pos0, si, last) in tiles:
            if ic:
                continue
            xt = k.ring("xt", 2, [8, 512], F32)
            k.dma(xt[:, :, :n], fm(xres)[:, :, off:off + n])
            sq = k.ring("sq", 2, [8, 512], BF16)
            k.act(sq[:, :, :n], xt[:, :, :n], AF.Square)
            ps = k.ps([0, 1], "fin")
            for j in range(8):
                k.mm(ps[:, :n], ONE, sq[:, j, :n], start=(j == 0), stop=(j == 7))
            rstd = k.ring("rstd", 2, [512], F32)
            k.ts("dve", rstd[:, :n], ps[:, :n], 1.0 / D, 1e-6, ALU.mult, ALU.add)
            k.act(rstd[:, :n], rstd[:, :n], AF.Sqrt)
            k.recip(rstd[:, :n], rstd[:, :n])
            xo = k.ring("xo", 2, [8, 512], F32)
            k.tt("dve", xo[:, :, :n], xt[:, :, :n], rstd[:, :n].un(1).bc([128, 8, n]), ALU.mult)
            for j in range(8):
                k.ts("dve", xo[:, j, :n], xo[:, j, :n], fg[:, j:j + 1], None, ALU.mult)
            o0 = b * NL + pos0
            k.dma(fm(outT)[:, :, o0:o0 + n], xo[:, :, :n])
        k.end_phase()
        k.emit()
    return nc


def _fmaj(v, nchunk):
    return np.ascontiguousarray(np.asarray(v, np.float32).reshape(nchunk, 128).T)


def _consts():
    r = np.arange(128)[:, None]
    c = np.arange(128)[None, :]
    mats = [(r == c), (r <= c), (r > c), (r >= c), (r < c), np.ones((128, 128), bool)]
    return np.ascontiguousarray(np.concatenate([m.astype(np.float32) for m in mats], axis=1))


def _rope(NL, grid_w=64):
    rows = NL // grid_w
    row = np.broadcast_to(np.arange(rows)[:, None], (rows, grid_w)).reshape(-1).astype(np.float32)
    col = np.broadcast_to(np.arange(grid_w)[None, :], (rows, grid_w)).reshape(-1).astype(np.float32)
    inv = (np.float32(10000.0) ** (-np.arange(16, dtype=np.float32) / np.float32(16))).astype(np.float32)
    ang = np.concatenate([row[:, None] * inv, col[:, None] * inv], axis=-1).astype(np.float32)
    cos = np.cos(ang).astype(np.float32).T
    sin = np.sin(ang).astype(np.float32).T
    p = np.arange(128)
    cosT = cos[p % 32]
    sgn = np.where((p % 64) < 32, 1.0, -1.0).astype(np.float32)[:, None]
    sinS = sin[p % 32] * sgn
    return np.ascontiguousarray(np.concatenate([cosT, sinS], axis=1).astype(np.float32))


def host_inputs(inp, NB, NL, NCX, DEPTH, ncores):
    f = lambda a: np.asarray(a, np.float32)
    x, c, ctx, c_ctx = f(inp["x"]), f(inp["c"]), f(inp["ctx"]), f(inp["c_ctx"])
    pvec = np.zeros((DEPTH, 128, NPV), np.float32)

    def put(l, name, arr):
        a, b = PV[name]
        pvec[l, :, a:b] = arr
    for l in range(DEPTH):
        put(l, "gmix", _fmaj(inp["norm_mix_g"][l], 8))
        put(l, "gmlp", _fmaj(inp["norm_mlp_g"][l], 8))
        cw = f(inp["conv_w"][l])
        put(l, "convw", cw.T.reshape(32, 128, 5).transpose(1, 0, 2).reshape(128, 160))
        put(l, "convb", _fmaj(inp["conv_b"][l], 32))
        put(l, "dtb", np.broadcast_to(np.concatenate([f(inp["dt_bias_f"][l]), f(inp["dt_bias_b"][l])])[None], (128, 64)))
        put(l, "alog", np.broadcast_to(np.concatenate([f(inp["a_log_f"][l]), f(inp["a_log_b"][l])])[None], (128, 64)))
        put(l, "ssdd", _fmaj(np.repeat(f(inp["ssd_d"][l]), 64), 16))
        put(l, "ssdg", _fmaj(inp["ssd_norm_g"][l], 16))
        put(l, "subg", np.broadcast_to(f(inp["attn_subln_g"][l])[None], (128, 128)))
        for nm, key in (("lq1", "lambda_q1"), ("lk1", "lambda_k1"), ("lq2", "lambda_q2"), ("lk2", "lambda_k2")):
            put(l, nm, np.broadcast_to(f(inp[key][l])[None], (128, 64)))
        put(l, "adab", _fmaj(inp["ada_b"][l], 48))
    shared = {
        "pvec": pvec, "gvec": _fmaj(inp["final_norm_g"], 8), "consts": _consts(), "rope": _rope(NL),
        "ada_w": f(inp["ada_w"]), "w_in": f(inp["w_in"]), "w_attn_o": f(inp["w_attn_o"]),
        "w_ssd_o": f(inp["w_ssd_o"]), "w_out": f(inp["w_out"]), "w_mlp1": f(inp["w_mlp1"]),
        "w_mlp2": f(inp["w_mlp2"]),
    }
    maps = []
    for ci in range(ncores):
        bs = list(range(ci * NB, (ci + 1) * NB))
        toks = np.concatenate([np.concatenate([x[b], ctx[b]], axis=0) for b in bs], axis=0)
        cv = np.stack([c[b] for b in bs] + [c_ctx], axis=1)
        cvec = cv.reshape(8, 128, NB + 1).transpose(1, 0, 2).reshape(128, 8 * (NB + 1))
        m = dict(shared)
        m["xT"] = np.ascontiguousarray(toks.T)
        m["cvec"] = np.ascontiguousarray(cvec)
        maps.append(m)
    return maps


_NC_CACHE = {}


def kernel(**inputs):
    x = np.asarray(inputs["x"])
    B, NL, _ = x.shape
    NCX = np.asarray(inputs["ctx"]).shape[1]
    DEPTH = np.asarray(inputs["w_in"]).shape[0]
    ncores = 8
    NB = B // ncores
    key = (NB, NL, NCX, DEPTH)
    if key not in _NC_CACHE:
        _NC_CACHE[key] = build(NB, NL, NCX, DEPTH)
    nc = _NC_CACHE[key]
    maps = host_inputs(inputs, NB, NL, NCX, DEPTH, ncores)
    res = run_bass_kernel_spmd(nc, maps, core_ids=list(range(ncores)))
    out = np.empty((B, NL, D), np.float32)
    for ci in range(ncores):
        oT = np.asarray(res.results[ci]["outT"])
        for j in range(NB):
            out[ci * NB + j] = oT[:, j * NL:(j + 1) * NL].T
    return out
```

```python
import math
import numpy as np
import concourse.bass as bass
import concourse.mybir as mybir
from concourse.bass_utils import run_bass_kernel_spmd
from contextlib import ExitStack

F32 = mybir.dt.float32
BF16 = mybir.dt.bfloat16
ALU = mybir.AluOpType
AF = mybir.ActivationFunctionType
AX = mybir.AxisListType

D = 1024
IN_DIM = 11328
NSEM = 80
ARENA_F32 = 53120

PV = {}
_o = 0
for _n, _w in (("gmix", 8), ("gmlp", 8), ("convw", 160), ("convb", 32), ("dtb", 64), ("alog", 64),
               ("ssdd", 16), ("ssdg", 16), ("subg", 128), ("lq1", 64), ("lk1", 64), ("lq2", 64),
               ("lk2", 64), ("adab", 48)):
    PV[_n] = (_o, _o + _w)
    _o += _w
NPV = _o


class Tile:
    def __init__(s, ap, name=""):
        s.ap = ap; s.w = None; s.r = {}; s.dsem = None; s.dcnt = 0; s.name = name

    def __getitem__(s, k):
        return V(s, s.ap[k])

    def v(s):
        return V(s, s.ap)


class V:
    def __init__(s, t, ap):
        s.t = t; s.ap = ap

    def __getitem__(s, k):
        return V(s.t, s.ap[k])

    def re(s, pat, **kw):
        return V(s.t, s.ap.rearrange(pat, **kw))

    def bc(s, shape):
        return V(s.t, s.ap.to_broadcast(list(shape)))

    def un(s, ax):
        return V(s.t, s.ap.unsqueeze(ax))


def _a(x):
    return x.ap if isinstance(x, V) else x


class KB:
    def __init__(s, nc, stack):
        s.nc = nc
        s.q = {e: [] for e in ("pe", "act", "dve", "pool", "sp")}
        s.esem = {e: stack.enter_context(nc.semaphore("e_" + e)) for e in ("pe", "act", "dve", "pool")}
        s.cnt = {e: 0 for e in s.esem}
        s.known = {e: {} for e in s.q}
        s.free_sems = [stack.enter_context(nc.semaphore("d%d" % i)) for i in range(NSEM)]
        s.semcnt = {sm: 0 for sm in s.free_sems}
        s.phase_sems = []
        s.persist = False
        s.arena = stack.enter_context(nc.sbuf_tensor("arena", [128, ARENA_F32], F32))
        s.off = 0
        s.base = 0
        s.psum = []
        s.psum2 = []
        for i in range(4):
            pp = stack.enter_context(nc.psum_tensor("pp%d" % i, [128, 1024], F32))
            s.psum2.append(Tile(pp[:, :], "pp%d" % i))
            s.psum.append(Tile(pp[:, 0:512], "ps%d" % (2 * i)))
            s.psum.append(Tile(pp[:, 512:1024], "ps%d" % (2 * i + 1)))
        s.rot = {}

    def alloc(s, free_shape, dt=F32, name=""):
        n = int(np.prod(free_shape))
        nf = (n + 1) // 2 if dt == BF16 else n
        nf = (nf + 7) // 8 * 8
        assert s.off + nf <= ARENA_F32, "SBUF arena overflow %s %d" % (name, s.off + nf)
        ap = s.arena[:, s.off:s.off + nf]
        s.off += nf
        if dt == BF16:
            ap = ap.bitcast(BF16)
        ap = ap[:, 0:n]
        if len(free_shape) == 2:
            ap = ap.rearrange("p (a b) -> p a b", a=free_shape[0])
        elif len(free_shape) == 3:
            ap = ap.rearrange("p (a b c) -> p a b c", a=free_shape[0], b=free_shape[1])
        return Tile(ap, name)

    def ring(s, key, n, free_shape, dt=F32):
        if key not in s.rot:
            s.rot[key] = [[s.alloc(free_shape, dt, key + str(i)) for i in range(n)], 0]
        r = s.rot[key]
        t = r[0][r[1] % n]
        r[1] += 1
        return t

    def ps(s, idxs, key):
        r = s.rot.setdefault("ps_" + key, [None, 0])
        t = s.psum[idxs[r[1] % len(idxs)]]
        r[1] += 1
        return t

    def ps2(s, idxs, key):
        r = s.rot.setdefault("ps2_" + key, [None, 0])
        t = s.psum2[idxs[r[1] % len(idxs)]]
        r[1] += 1
        return t

    def _getsem(s):
        sm = s.free_sems.pop()
        if not s.persist:
            s.phase_sems.append(sm)
        return sm

    def _waits(s, eng, reads, writes, skip_dma_sem=None):
        w = {}

        def need(d):
            if d is None:
                return
            sem, val, src = d
            if src == "pe" and eng == "pe":
                return
            if w.get(sem, 0) < val:
                w[sem] = val

        for t in reads:
            need(t.w)
        for t in writes:
            if not (skip_dma_sem is not None and t.w is not None and t.w[2] == "dma" and t.w[0] is skip_dma_sem):
                need(t.w)
            for d in t.r.values():
                need(d)
        out = []
        kn = s.known[eng]
        for sem, val in w.items():
            if kn.get(sem, 0) < val:
                kn[sem] = val
                out.append((sem, val))
        return out

    def op(s, eng, fn, outs, ins):
        reads = []
        for v in ins:
            if isinstance(v, V) and v.t not in reads:
                reads.append(v.t)
        writes = []
        for v in outs:
            if isinstance(v, V) and v.t not in writes:
                writes.append(v.t)
        wl = s._waits(eng, reads, writes)
        s.cnt[eng] += 1
        me = (s.esem[eng], s.cnt[eng], eng)
        s.q[eng].append((wl, fn, (s.esem[eng], 1)))
        for t in reads:
            t.r[eng] = me
        for t in writes:
            t.w = me
            t.r = {}

    def dma(s, out, in_, queue="sp"):
        reads = [in_.t] if isinstance(in_, V) else []
        writes = [out.t] if isinstance(out, V) else []
        owner = writes[0] if writes else reads[0]
        if owner.dsem is None:
            owner.dsem = s._getsem()
            owner.dcnt = s.semcnt[owner.dsem]
        wl = s._waits(queue, reads, writes, skip_dma_sem=owner.dsem)
        owner.dcnt += 16
        s.semcnt[owner.dsem] = owner.dcnt
        dep = (owner.dsem, owner.dcnt, "dma")
        oap, iap = _a(out), _a(in_)
        s.q[queue].append((wl, lambda e: e.dma_start(out=oap, in_=iap), (owner.dsem, 16)))
        for t in reads:
            t.r[("dma", id(owner))] = dep
        for t in writes:
            t.w = dep
            t.r = {}

    def barrier(s):
        allsem = {}
        for e, sm in s.esem.items():
            allsem[sm] = s.cnt[e]
        for sm, c in s.semcnt.items():
            allsem[sm] = c
        for eng in s.q:
            kn = s.known[eng]
            wl = []
            for sm, c in allsem.items():
                if c > kn.get(sm, 0):
                    kn[sm] = c
                    wl.append((sm, c))
            s.q[eng].append((wl, None, None))

    def end_phase(s):
        s.barrier()
        s.free_sems.extend(s.phase_sems)
        s.phase_sems = []
        s.off = s.base
        s.rot = {}
        for t in s.psum + s.psum2:
            t.w = None; t.r = {}

    def emit(s):
        with s.nc.Block() as block:
            def rep(name):
                def f(e):
                    for wl, fn, inc in s.q[name]:
                        for sem, val in wl:
                            e.wait_ge(sem, val)
                        if fn is not None:
                            fn(e).then_inc(inc[0], inc[1])
                return f
            block.tensor(rep("pe"))
            block.scalar(rep("act"))
            block.vector(rep("dve"))
            block.gpsimd(rep("pool"))
            block.sync(rep("sp"))

    def mm(s, out, lhsT, rhs, start=True, stop=True, skip=False):
        o, l, r = _a(out), _a(lhsT), _a(rhs)
        if skip:
            s.op("pe", lambda e: e.matmul(o, lhsT=l, rhs=r, start=start, stop=stop, skip_group_check=True),
                 [out], [lhsT, rhs])
        else:
            s.op("pe", lambda e: e.matmul(o, lhsT=l, rhs=r, start=start, stop=stop), [out], [lhsT, rhs])

    def tr(s, out, in_, ident):
        o, i, d = _a(out), _a(in_), _a(ident)
        s.op("pe", lambda e: e.transpose(o, i, d), [out], [in_, ident])

    def act(s, out, in_, func, bias=0.0, scale=1.0, accum=None):
        o, i, b, sc, ac = _a(out), _a(in_), _a(bias), _a(scale), _a(accum)
        if ac is None:
            s.op("act", lambda e: e.activation(out=o, in_=i, func=func, bias=b, scale=sc), [out], [in_, bias, scale])
        else:
            s.op("act", lambda e: e.activation(out=o, in_=i, func=func, bias=b, scale=sc, accum_out=ac),
                 [out, accum], [in_, bias, scale])

    def tt(s, eng, out, a, b, op):
        o, x, y = _a(out), _a(a), _a(b)
        s.op(eng, lambda e: e.tensor_tensor(out=o, in0=x, in1=y, op=op), [out], [a, b])

    def ts(s, eng, out, a, s1, s2, op0, op1=None):
        o, x, c1, c2 = _a(out), _a(a), _a(s1), _a(s2)
        if op1 is None:
            s.op(eng, lambda e: e.tensor_scalar(out=o, in0=x, scalar1=c1, scalar2=None, op0=op0), [out], [a, s1])
        else:
            s.op(eng, lambda e: e.tensor_scalar(out=o, in0=x, scalar1=c1, scalar2=c2, op0=op0, op1=op1),
                 [out], [a, s1, s2])

    def stt(s, eng, out, a, sc, b, op0, op1):
        o, x, c, y = _a(out), _a(a), _a(sc), _a(b)
        s.op(eng, lambda e: e.scalar_tensor_tensor(out=o, in0=x, scalar=c, in1=y, op0=op0, op1=op1),
             [out], [a, sc, b])

    def cp(s, eng, out, a):
        o, x = _a(out), _a(a)
        if eng == "act":
            s.op("act", lambda e: e.activation(out=o, in_=x, func=AF.Copy), [out], [a])
        else:
            s.op(eng, lambda e: e.tensor_copy(out=o, in_=x), [out], [a])

    def memset(s, eng, out, val):
        o = _a(out)
        s.op(eng, lambda e: e.memset(o, val), [out], [])

    def recip(s, out, a):
        o, x = _a(out), _a(a)
        s.op("dve", lambda e: e.reciprocal(out=o, in_=x), [out], [a])

    def rsum(s, out, a):
        o, x = _a(out), _a(a)
        s.op("dve", lambda e: e.tensor_reduce(out=o, in_=x, axis=AX.X, op=ALU.add), [out], [a])


def build(NB=2, NL=2048, NCX=256, DEPTH=4, dbg=False):
    TS = NL + NCX
    T = NB * TS
    NC3 = NB + 1
    NKC = TS // 128
    nc = bass.Bass("TRN2", target_bir_lowering=False)

    def din(name, shape, dt=F32):
        return nc.dram_tensor(name, list(shape), dt, kind="ExternalInput").ap()

    skind = "ExternalOutput" if dbg else "Internal"

    def dscr(name, shape, dt):
        return nc.dram_tensor(name, list(shape), dt, kind=skind).ap()

    xin = din("xT", [D, T])
    cvec_in = din("cvec", [128, 8 * NC3])
    pvec_in = din("pvec", [DEPTH, 128, NPV])
    gvec_in = din("gvec", [128, 8])
    consts_in = din("consts", [128, 6 * 128])
    rope_in = din("rope", [128, 2 * NL])
    ada_w = din("ada_w", [DEPTH, D, 6 * D])
    w_in = din("w_in", [DEPTH, D, IN_DIM])
    w_ao = din("w_attn_o", [DEPTH, D, D])
    w_so = din("w_ssd_o", [DEPTH, 2 * D, D])
    w_o = din("w_out", [DEPTH, D, D])
    w_1 = din("w_mlp1", [DEPTH, D, 4 * D])
    w_2 = din("w_mlp2", [DEPTH, 4 * D, D])
    outT = nc.dram_tensor("outT", [D, NB * NL], F32, kind="ExternalOutput").ap()

    xres = dscr("xres", [D, T], F32)
    qT = dscr("qT", [D, T], BF16)
    kT = dscr("kT", [D, T], BF16)
    vaug = dscr("vaug", [T, 8, 129], BF16)
    szT = dscr("szT", [2 * D, T], BF16)
    xbcT = dscr("xbcT", [4 * D, T], BF16)
    xbtok = dscr("xbtok", [T, 3 * D], BF16)
    dtD = dscr("dtD", [T, 64], F32)
    sgT = dscr("sgT", [2 * D, T], BF16)
    attT = dscr("attT", [D, T], BF16)
    yT = dscr("yT", [2 * D, T], F32)
    gT = dscr("gT", [2 * D, T], BF16)
    uT = dscr("uT", [4 * D, T], BF16)

    def fm(ap):
        return ap.rearrange("(k p) t -> p k t", p=128)

    seqs = []
    for b in range(NB):
        seqs.append((b, 0, b * TS, NL))
        seqs.append((b, 1, b * TS + NL, NCX))
    tiles = []
    for si, (b, ic, so, sl) in enumerate(seqs):
        for o in range(0, sl, 512):
            n = min(512, sl - o)
            tiles.append((b, ic, so + o, n, o, si, o + n >= sl))

    with ExitStack() as stack:
        k = KB(nc, stack)
        k.persist = True
        cst_f = k.alloc([6, 128], F32, "cst_f")
        cst = k.alloc([6, 128], BF16, "cst")
        pv = k.alloc([DEPTH, NPV], F32, "pv")
        modT = k.alloc([DEPTH, 48, NC3], F32, "modT")
        gmA = k.alloc([DEPTH, 8, NC3], F32, "gmA")
        gmF = k.alloc([DEPTH, 8, NC3], F32, "gmF")
        aneg = k.alloc([DEPTH, 64], F32, "aneg")
        nlam = k.alloc([DEPTH, 1], F32, "nlam")
        subg = k.alloc([DEPTH, 128], F32, "subg")
        fg = k.alloc([8], F32, "fg")
        cact_f = k.alloc([8, NC3], F32, "cact_f")
        cact = k.alloc([8, NC3], BF16, "cact")
        sml = k.alloc([16], F32, "sml")
        k.dma(cst_f.v(), consts_in.rearrange("p (a b) -> p a b", a=6))
        k.dma(fg.v(), gvec_in)
        k.dma(cact_f.v(), cvec_in.rearrange("p (a b) -> p a b", a=8))
        for l in range(DEPTH):
            k.dma(pv[:, l, :], pvec_in[l])
        k.cp("dve", cst.v(), cst_f.v())
        k.act(cact_f.v(), cact_f.v(), AF.Silu)
        k.cp("dve", cact.v(), cact_f.v())
        IDN, LE, GT, GE, LT, ONE = (cst[:, i, :] for i in range(6))
        k.persist = False
        k.base = k.off

        def pvs(l, name):
            a, b = PV[name]
            return pv[:, l, a:b]

        for l in range(DEPTH):
            wk = [k.ring("adaw", 8, [6 * D], BF16) for _ in range(8)]
            for kk in range(8):
                k.dma(wk[kk].v(), ada_w[l, kk * 128:(kk + 1) * 128, :], queue="pool")
            ps = k.ps([0, 1], "m")
            for j in range(48):
                for kk in range(8):
                    k.mm(ps[:, j * NC3:(j + 1) * NC3], wk[kk][:, j * 128:(j + 1) * 128], cact[:, kk, :],
                         start=(kk == 0), stop=(kk == 7))
            k.tt("dve", modT[:, l, :, :], ps[:, 0:48 * NC3].re("p (j c) -> p j c", c=NC3),
                 pvs(l, "adab").un(2).bc([128, 48, NC3]), ALU.add)
            for m in (1, 4):
                k.ts("dve", modT[:, l, 8 * m:8 * m + 8, :], modT[:, l, 8 * m:8 * m + 8, :], 1.0, None, ALU.add)
            k.tt("dve", gmA[:, l, :, :], modT[:, l, 8:16, :], pvs(l, "gmix").un(2).bc([128, 8, NC3]), ALU.mult)
            k.tt("dve", gmF[:, l, :, :], modT[:, l, 32:40, :], pvs(l, "gmlp").un(2).bc([128, 8, NC3]), ALU.mult)
            k.act(aneg[:, l, :], pvs(l, "alog"), AF.Exp)
            k.ts("dve", aneg[:, l, :], aneg[:, l, :], -1.0, None, ALU.mult)
            lam_init = 0.8 - 0.6 * math.exp(-0.3 * l)
            tmpl = k.ring("lamt", 2, [64], F32)
            k.tt("dve", tmpl.v(), pvs(l, "lq1"), pvs(l, "lk1"), ALU.mult)
            k.rsum(sml[:, 0:1], tmpl.v())
            tmpl2 = k.ring("lamt", 2, [64], F32)
            k.tt("dve", tmpl2.v(), pvs(l, "lq2"), pvs(l, "lk2"), ALU.mult)
            k.rsum(sml[:, 1:2], tmpl2.v())
            k.act(sml[:, 2:4], sml[:, 0:2], AF.Exp)
            k.tt("dve", sml[:, 4:5], sml[:, 3:4], sml[:, 2:3], ALU.subtract)
            k.ts("dve", nlam[:, l, :], sml[:, 4:5], -lam_init, None, ALU.add)
            k.ts("dve", subg[:, l, :], pvs(l, "subg"), 1.0 - lam_init, None, ALU.mult)
        k.end_phase()

        def modulate(src, hts, gm, l, shift_m, tl):
            for ti, (b, ic, off, n, pos0, si, last) in enumerate(tl):
                col = NB if ic else b
                xt = k.ring("mod_x", 2, [8, 512], F32)
                k.dma(xt[:, :, :n], fm(src)[:, :, off:off + n])
                sq = k.ring("mod_sq", 2, [8, 512], BF16)
                k.act(sq[:, :, :n], xt[:, :, :n], AF.Square)
                ps = k.ps([0, 1], "mod")
                for j in range(8):
                    k.mm(ps[:, :n], ONE, sq[:, j, :n], start=(j == 0), stop=(j == 7))
                rstd = k.ring("mod_r", 2, [512], F32)
                k.ts("dve", rstd[:, :n], ps[:, :n], 1.0 / D, 1e-6, ALU.mult, ALU.add)
                k.act(rstd[:, :n], rstd[:, :n], AF.Sqrt)
                k.recip(rstd[:, :n], rstd[:, :n])
                xn = k.ring("mod_xn", 2, [8, 512], F32)
                k.tt("dve", xn[:, :, :n], xt[:, :, :n], rstd[:, :n].un(1).bc([128, 8, n]), ALU.mult)
                for j in range(8):
                    k.act(hts[ti][:, j, :n], xn[:, j, :n], AF.Identity,
                          bias=modT[:, l, 8 * shift_m + j, col:col + 1], scale=gm[:, l, j, col:col + 1])

        for l in range(DEPTH):
            last_layer = (l == DEPTH - 1)
            xsrc = xin if l == 0 else xres
            act_tiles = [t for t in tiles if not (last_layer and t[1])]
            hts = [k.alloc([8, tiles[i][3]], BF16, "hT%d" % i) for i in range(len(tiles))]
            mark = k.off
            modulate(xsrc, hts, gmA, l, 0, tiles)
            k.barrier()
            k.off = mark
            k.rot = {}
            rope = k.alloc([2, NL], F32, "rope")
            k.dma(rope.v(), rope_in.rearrange("p (a b) -> p a b", a=2))
            vst = [k.alloc([4, 129], BF16, "vst%d" % i) for i in range(3)]
            for t_ in vst:
                k.memset("pool", t_[:, :, 128:129], 1.0)
            vst_i = 0
            dq = []
            dgj = [-1, None]

            def defer(fn, delay):
                dq.append([delay, fn])

            def tick():
                ready = []
                for it in list(dq):
                    it[0] -= 1
                    if it[0] <= 0:
                        ready.append(it)
                        dq.remove(it)
                for it in ready:
                    it[1]()
            segs = (("q", 0, 1024), ("k", 1024, 2048), ("v", 2048, 3072), ("z", 3072, 5120),
                    ("xbc", 5120, 9216), ("dt", 9216, 9280), ("ga", 9280, 10304), ("gs", 10304, 11328))
            wfm = w_in[l].rearrange("(k p) c -> p k c", p=128)
            for sname, s0, s1 in segs:
                for c0 in range(s0, s1, 512):
                    cw_ = min(512, s1 - c0)
                    wb = k.ring("wb", 2, [8, 512], BF16)
                    k.dma(wb[:, :, :cw_], wfm[:, :, c0:c0 + cw_], queue="pool")
                    if sname == "v":
                        vb = (c0 - s0) // 512
                        for ti, (b, ic, off, n, pos0, si, last) in enumerate(tiles):
                            for tb in range(n // 128):
                                ps = k.ps([0, 1, 2, 3, 4, 5], "b")
                                for kk in range(8):
                                    k.mm(ps[:, :], hts[ti][:, kk, tb * 128:(tb + 1) * 128], wb[:, kk, :],
                                         start=(kk == 0), stop=(kk == 7))
                                vs_ = vst[vst_i % 3]; vst_i += 1
                                k.act(vs_[:, :, 0:128], ps[:, :].re("p (h e) -> p h e", h=4), AF.Copy)
                                t0 = off + tb * 128
                                k.dma(vaug[t0:t0 + 128, vb * 4:(vb + 1) * 4, :], vs_.v())
                        continue
                    if sname == "dt":
                        for ti, (b, ic, off, n, pos0, si, last) in enumerate(tiles):
                            nb_ = n // 128
                            ps = k.ps([0, 1, 2, 3, 4, 5], "b")
                            for tb in range(nb_):
                                for kk in range(8):
                                    k.mm(ps[:, tb * 64:(tb + 1) * 64], hts[ti][:, kk, tb * 128:(tb + 1) * 128],
                                         wb[:, kk, :64], start=(kk == 0), stop=(kk == 7))
                            d1 = k.ring("dt1", 2, [4, 64], F32)
                            k.tt("dve", d1[:, :nb_, :], ps[:, :nb_ * 64].re("p (a b) -> p a b", b=64),
                                 pvs(l, "dtb").un(1).bc([128, nb_, 64]), ALU.add)
                            k.act(d1[:, :nb_, :], d1[:, :nb_, :], AF.Exp)
                            d2 = k.ring("dt2", 2, [4, 64], F32)
                            k.act(d2[:, :nb_, :], d1[:, :nb_, :], AF.Ln, bias=1.0)
                            k.dma(dtD[off:off + n, :].rearrange("(a p) c -> p a c", p=128), d2[:, :nb_, :])
                        continue
                    for jc in range(cw_ // 128):
                        gcol = c0 + jc * 128
                        stage = None
                        for ti, (b, ic, off, n, pos0, si, last) in enumerate(tiles):
                            ps = k.ps([0, 1, 2, 3, 4, 5], "b")
                            for kk in range(8):
                                k.mm(ps[:, :n], wb[:, kk, jc * 128:(jc + 1) * 128], hts[ti][:, kk, :n],
                                     start=(kk == 0), stop=(kk == 7))
                            tick()
                            if sname in ("q", "k"):
                                h = (gcol - s0) // 128
                                dst = qT if sname == "q" else kT
                                scl = 0.125 if sname == "q" else 1.0
                                ob = k.ring("ob", 3, [512], BF16)
                                if ic:
                                    k.act(ob[:, :n], ps[:, :n], AF.Identity, scale=scl)
                                else:
                                    qf = k.ring("qf", 2, [512], F32)
                                    k.act(qf[:, :n], ps[:, :n], AF.Identity, scale=scl)
                                    A = k.ring("ropeA", 2, [512], F32)
                                    k.tt("pool", A[:, :n], qf[:, :n], rope[:, 0, pos0:pos0 + n], ALU.mult)
                                    Bt = k.ring("ropeB", 2, [512], F32)
                                    for g in range(4):
                                        gs_ = g ^ 1
                                        k.tt("dve", Bt[32 * g:32 * g + 32, :n], qf[32 * gs_:32 * gs_ + 32, :n],
                                             rope[32 * gs_:32 * gs_ + 32, 1, pos0:pos0 + n], ALU.mult)
                                    k.tt("pool", ob[:, :n], A[:, :n], Bt[:, :n], ALU.add)
                                k.dma(dst[h * 128:(h + 1) * 128, off:off + n], ob[:, :n])
                            elif sname == "z":
                                r0 = gcol - s0
                                ob = k.ring("ob", 3, [512], BF16)
                                k.act(ob[:, :n], ps[:, :n], AF.Silu)
                                k.dma(szT[r0:r0 + 128, off:off + n], ob[:, :n])
                            elif sname in ("ga", "gs"):
                                r0 = gcol - 9280
                                ob = k.ring("ob", 3, [512], BF16)
                                k.act(ob[:, :n], ps[:, :n], AF.Sigmoid)
                                k.dma(sgT[r0:r0 + 128, off:off + n], ob[:, :n])
                            else:
                                jx = (gcol - s0) // 128
                                sb_, sic, soff, slen = seqs[si]
                                a0, _ = PV["convw"]
                                if pos0 == 0:
                                    stage = k.ring("stage", 3, [NL + 4], BF16)
                                    k.memset("pool", stage[:, 0:2], 0.0)
                                    k.memset("pool", stage[:, 2 + slen:4 + slen], 0.0)
                                    stageB = k.ring("stageB", 3, [NL + 4], BF16)
                                    k.memset("pool", stageB[:, 0:2], 0.0)
                                    k.memset("pool", stageB[:, slen:4 + slen], 0.0)
                                    if dgj[0] != jx:
                                        dgj[0] = jx
                                        dgj[1] = k.ring("dg", 2, [5, 128], BF16)
                                        for kq in range(5):
                                            k.ts("dve", dgj[1][:, kq, :], IDN,
                                                 pv[:, l, a0 + jx * 5 + kq:a0 + jx * 5 + kq + 1], None, ALU.mult)
                                k.act(stage[:, 2 + pos0:2 + pos0 + n], ps[:, :n], AF.Copy)
                                k.act(stageB[:, 1 + pos0:1 + pos0 + n], ps[:, :n], AF.Copy)
                                if last:
                                    def conv_unit(stage=stage, stageB=stageB, dg=dgj[1], jx=jx, soff=soff, slen=slen):
                                        cvo = k.ring("cvo", 3, [NL], BF16)
                                        b0_, _ = PV["convb"]
                                        for o_ in range(0, slen, 512):
                                            nn = min(512, slen - o_)
                                            pc = k.ps([0, 1, 2, 3, 4, 5], "b")
                                            for kq in range(5):
                                                src_ = stage[:, o_ + kq:o_ + kq + nn] if kq % 2 == 0 else \
                                                    stageB[:, o_ + kq - 1:o_ + kq - 1 + nn]
                                                k.mm(pc[:, :nn], dg[:, kq, :], src_, start=(kq == 0), stop=(kq == 4))
                                            k.act(cvo[:, o_:o_ + nn], pc[:, :nn], AF.Silu,
                                                  bias=pv[:, l, b0_ + jx:b0_ + jx + 1])
                                        k.dma(xbcT[jx * 128:(jx + 1) * 128, soff:soff + slen], cvo[:, :slen])
                                        if jx < 24:
                                            def trans(cvo=cvo, jx=jx, soff=soff, slen=slen):
                                                for t4 in range(0, slen // 128, 4):
                                                    nq = min(4, slen // 128 - t4)
                                                    pst = k.ps([6, 7], "bt")
                                                    pv_ = V(pst, pst.ap.bitcast(BF16))
                                                    for q_ in range(nq):
                                                        k.tr(pv_[:, q_ * 128:(q_ + 1) * 128],
                                                             cvo[:, (t4 + q_) * 128:(t4 + q_ + 1) * 128], IDN)
                                                    tst = k.ring("tst", 2, [512], BF16)
                                                    k.cp("act", tst[:, :nq * 128], pv_[:, :nq * 128])
                                                    tk0 = soff + t4 * 128
                                                    k.dma(xbtok[tk0:tk0 + nq * 128, jx * 128:(jx + 1) * 128]
                                                          .rearrange("(q p) c -> p q c", p=128),
                                                          tst[:, :nq * 128].re("p (q c) -> p q c", c=128))
                                            defer(trans, 1)
                                    defer(conv_unit, 1)
            while dq:
                tick()
            k.end_phase()

            deferred = []

            def flush_deferred():
                while deferred:
                    deferred.pop(0)()
            for b in range(NB):
                kts = []
                for h in range(8):
                    t_ = k.ring("KT%d" % h, 1, [TS], BF16)
                    k.dma(t_.v(), kT[h * 128:(h + 1) * 128, b * TS:(b + 1) * TS])
                    kts.append(t_)
                va = k.ring("VA", 1, [NKC, 8, 129], BF16)
                k.dma(va.v(), vaug[b * TS:(b + 1) * TS].rearrange("(c p) h e -> p c h e", p=128))
                for h in range(8):
                    for (tb_, ic, off, n, pos0, si, last) in tiles:
                        if tb_ != b or (ic and last_layer):
                            continue
                        nqb = n // 128
                        qt = k.ring("QT", 2, [512], BF16)
                        k.dma(qt[:, :n], qT[h * 128:(h + 1) * 128, off:off + n])
                        kcs = list(range(NL // 128, NKC)) if ic else list(range(NKC))
                        obank = [k.psum[4], k.psum[5], k.psum[6]]

                        def oreg(c, qb):
                            r = c * 4 + qb
                            return obank[r // 3][:, (r % 3) * 129:(r % 3) * 129 + 129]

                        def qk(kc):
                            pss = k.ps2([0, 1], "s")
                            for c in range(2):
                                k.mm(pss[:, c * 512:c * 512 + n], kts[h][64 * c:64 * c + 64, kc * 128:(kc + 1) * 128],
                                     qt[64 * c:64 * c + 64, :n])
                            pt = k.ring("PT", 3, [2, 512], BF16)
                            k.act(pt[:, :, :n], pss[:, :].re("p (c q) -> p c q", c=2)[:, :, :n], AF.Exp)
                            return pt
                        started = set()
                        pend = qk(kcs[0])
                        for i_, kc in enumerate(kcs):
                            pts = pend
                            if i_ + 1 < len(kcs):
                                pend = qk(kcs[i_ + 1])
                            if i_ == 1:
                                flush_deferred()
                            for c in range(2):
                                for qb in range(nqb):
                                    r = c * 4 + qb
                                    st = (r // 3) not in started
                                    started.add(r // 3)
                                    k.mm(oreg(c, qb), pts[:, c, qb * 128:(qb + 1) * 128], va[:, kc, h, :],
                                         start=st, stop=(kc == kcs[-1]), skip=True)
                        osb = k.ring("osb", 2, [3, 387], F32)
                        for bi in sorted(started):
                            k.cp("dve", osb[:, bi, :], obank[bi][:, 0:387])

                        def osr(c, qb):
                            r = c * 4 + qb
                            return osb[:, r // 3, (r % 3) * 129:(r % 3) * 129 + 129]
                        rec = k.ring("rec", 2, [8], F32)
                        for c in range(2):
                            for qb in range(nqb):
                                k.recip(rec[:, c * 4 + qb:c * 4 + qb + 1], osr(c, qb)[:, 128:129])
                        k.ts("dve", rec[:, 4:8], rec[:, 4:8], nlam[:, l, :], None, ALU.mult)
                        ssq = k.ring("ssq", 2, [4], F32)
                        k.memset("dve", ssq.v(), 0.0)
                        os_ = []
                        for qb in range(nqb):
                            t1 = k.ring("at1", 2, [128], F32)
                            k.ts("dve", t1.v(), osr(0, qb)[:, 0:128], rec[:, qb:qb + 1], None, ALU.mult)
                            o_ = k.ring("ao", 8, [128], F32)
                            k.stt("dve", o_.v(), osr(1, qb)[:, 0:128], rec[:, 4 + qb:5 + qb], t1.v(),
                                  ALU.mult, ALU.add)
                            junk = k.ring("ajunk", 2, [128], F32)
                            k.act(junk.v(), o_.v(), AF.Square, accum=ssq[:, qb:qb + 1])
                            os_.append(o_)
                        rs = k.ring("ars", 2, [4], F32)
                        k.ts("dve", rs.v(), ssq.v(), 1.0 / 128, 1e-6, ALU.mult, ALU.add)
                        k.act(rs.v(), rs.v(), AF.Ln)
                        k.act(rs.v(), rs.v(), AF.Exp, scale=-0.5)
                        on = k.ring("aon", 3, [4, 128], BF16)
                        for qb in range(nqb):
                            k.stt("dve", on[:, qb, :], os_[qb].v(), rs[:, qb:qb + 1], subg[:, l, :],
                                  ALU.mult, ALU.mult)

                        def fin(on=on, n=n, nqb=nqb, h=h, off=off):
                            pst = k.psum[7]
                            pv_ = V(pst, pst.ap.bitcast(BF16))
                            for qb in range(nqb):
                                k.tr(pv_[:, qb * 128:(qb + 1) * 128], on[:, qb, :], IDN)
                            ob = k.ring("ob", 3, [512], BF16)
                            k.cp("act", ob[:, :n], pv_[:, :n])
                            k.dma(attT[h * 128:(h + 1) * 128, off:off + n], ob[:, :n])
                        deferred.append(fin)
            flush_deferred()
            k.end_phase()

            d0, _ = PV["ssdd"]
            LA = 2
            for b in range(NB):
                for dr in range(2):
                    Ms, Um, Vm, Dm = (GT, LE, LE, GT) if dr == 0 else (LT, GE, GE, LT)
                    lastcol = 127 if dr == 0 else 0
                    STp = [k.ring("STp%d" % g, 1, [512], F32) for g in range(4)]
                    STbp = [k.ring("STbp%d" % g, 1, [8, 128], BF16) for g in range(4)]
                    for g in range(4):
                        k.memset("pool", STp[g].v(), 0.0)
                        k.memset("pool", STbp[g].v(), 0.0)
                    cl = [(1, ci) for ci in range(NCX // 128)] + [(0, ci) for ci in range(NL // 128)]
                    if dr == 1:
                        cl = [(1, ci) for ci in reversed(range(NCX // 128))] + \
                             [(0, ci) for ci in reversed(range(NL // 128))]

                    def chunk_pre(idx, b=b, dr=dr, cl=cl, Dm=Dm, Um=Um):
                        ic, ci = cl[idx]
                        C = {}
                        tok0 = b * TS + (NL if ic else 0) + ci * 128
                        C["tok0"] = tok0
                        xs_tok = k.ring("xs_tok", 2, [32, 64], BF16)
                        k.dma(xs_tok.v(), xbtok[tok0:tok0 + 128, 0:2048].rearrange("p (h e) -> p h e", e=64))
                        B_tok = k.ring("B_tok", 2, [1024], BF16)
                        k.dma(B_tok.v(), xbtok[tok0:tok0 + 128, 2048:3072])
                        BT = k.ring("BT", 2, [8, 128], BF16)
                        k.dma(BT.v(), fm(xbcT[2048:3072, :])[:, :, tok0:tok0 + 128])
                        CT = k.ring("CT", 2, [8, 128], BF16)
                        k.dma(CT.v(), fm(xbcT[3072:4096, :])[:, :, tok0:tok0 + 128])
                        dt = k.ring("dt", 2, [32], F32)
                        k.dma(dt.v(), dtD[tok0:tok0 + 128, dr * 32:(dr + 1) * 32])
                        C.update(B_tok=B_tok, BT=BT, CT=CT)
                        if dr == 1:
                            yf = k.ring("yf", 2, [16, 128], F32)
                            k.dma(yf.v(), fm(yT)[:, :, tok0:tok0 + 128])
                            xsT = k.ring("xsT", 2, [16, 128], BF16)
                            k.dma(xsT.v(), fm(xbcT[0:2048, :])[:, :, tok0:tok0 + 128])
                            szt = k.ring("szt", 2, [16, 128], BF16)
                            k.dma(szt.v(), fm(szT)[:, :, tok0:tok0 + 128])
                            xsD = k.ring("xsD", 1, [16, 128], F32)
                            k.tt("pool", xsD.v(), xsT.v(), pv[:, l, d0:d0 + 16].un(2).bc([128, 16, 128]), ALU.mult)
                            k.tt("pool", yf.v(), yf.v(), xsD.v(), ALU.add)
                            C.update(yf=yf, szt=szt)
                        dta = k.ring("dta", 2, [32], F32)
                        k.tt("dve", dta.v(), dt.v(), aneg[:, l, dr * 32:(dr + 1) * 32], ALU.mult)
                        dhi = k.ring("dhi", 2, [32], BF16)
                        k.cp("dve", dhi.v(), dta.v())
                        dlf = k.ring("dlf", 2, [32], F32)
                        k.tt("dve", dlf.v(), dta.v(), dhi.v(), ALU.subtract)
                        dlo = k.ring("dlo", 2, [32], BF16)
                        k.cp("dve", dlo.v(), dlf.v())
                        psd = k.psum[4]
                        k.mm(psd[:, 256:288], Dm, dhi.v(), start=True, stop=False)
                        k.mm(psd[:, 256:288], Dm, dlo.v(), start=False, stop=True)
                        dte = k.ring("dte", 2, [32], F32)
                        k.act(dte.v(), psd[:, 256:288], AF.Exp)
                        xcp = k.ring("xcp", 2, [32, 128], BF16)
                        if k.rot["xcp"][1] <= 2:
                            k.memset("pool", xcp.v(), 0.0)
                        xcv = xcp.v().re("p (a two) c -> p a two c", two=2)
                        xsv = xs_tok.v().re("p (a two) e -> p a two e", two=2)
                        dtv = dt.v().re("p (a two) -> p a two", two=2)
                        for par in range(2):
                            k.tt("pool", xcv[:, :, par, par * 64:par * 64 + 64], xsv[:, :, par, :],
                                 dtv[:, :, par].un(2).bc([128, 16, 64]), ALU.mult)
                        xcd = k.ring("xcd", 2, [32, 64], BF16)
                        dev = dte.v().re("p (a two) -> p a two", two=2)
                        xdv = xcd.v().re("p (a two) e -> p a two e", two=2)
                        for par in range(2):
                            k.tt("pool", xdv[:, :, par, :], xcv[:, :, par, par * 64:par * 64 + 64],
                                 dev[:, :, par].un(2).bc([128, 16, 64]), ALU.mult)
                        rhi = k.ring("rhi", 2, [32, 128], BF16)
                        k.tt("dve", rhi.v(), dhi.v().un(2).bc([128, 32, 128]), Um.un(1).bc([128, 32, 128]), ALU.mult)
                        C.update(xcp=xcp, xcd=xcd, rhi=rhi)
                        return C

                    def stage1(C, gp, Ms=Ms, Vm=Vm):
                        rhi, BT, CT = C["rhi"], C["BT"], C["CT"]
                        g0 = 2 * gp
                        pseg = k.ps2([0, 1], "segcs")
                        for gi in range(2):
                            k.mm(pseg[:, gi * 512:(gi + 1) * 512], Ms,
                                 rhi[:, 4 * (g0 + gi):4 * (g0 + gi) + 4, :].re("p h l -> p (h l)"))
                        pcs = k.ps2([0, 1], "segcs")
                        for gi in range(2):
                            k.mm(pcs[:, gi * 512:(gi + 1) * 512], ONE,
                                 rhi[:, 4 * (g0 + gi):4 * (g0 + gi) + 4, :].re("p h l -> p (h l)"))
                        E = k.ring("E", LA + 2, [8, 128], BF16)
                        k.act(E.v().re("p h l -> p (h l)"), pseg[:, :], AF.Exp)
                        E0 = k.ring("E0", LA + 2, [8, 128], F32)
                        k.act(E0.v().re("p h l -> p (h l)"), pcs[:, :], AF.Exp)
                        pcb = k.psum[4]
                        for gi in range(2):
                            k.mm(pcb[:, gi * 128:(gi + 1) * 128], BT[:, g0 + gi, :], CT[:, g0 + gi, :])
                        cbm = k.ring("cbm", LA + 2, [2, 128], BF16)
                        k.tt("dve", cbm.v(), pcb[:, 0:256].re("p (a l) -> p a l", a=2),
                             Vm.un(1).bc([128, 2, 128]), ALU.mult)
                        MT = k.ring("MT", LA + 2, [2, 4, 128], BF16)
                        k.tt("dve", MT.v(), E.v().re("p (a h) l -> p a h l", a=2),
                             cbm.v().un(2).bc([128, 2, 4, 128]), ALU.mult)
                        MV = k.ring("MV", LA + 2, [2, 4, 128], BF16)
                        k.tt("pool", MV.v(), E0.v().re("p (a h) l -> p a h l", a=2),
                             CT[:, g0:g0 + 2, :].un(2).bc([128, 2, 4, 128]), ALU.mult)
                        return (E0, MT, MV)

                    def stage2(C, gp, S, dr=dr, lastcol=lastcol, STp=STp, STbp=STbp):
                        E0, MT, MV = S
                        xcp, xcd, B_tok, tok0 = C["xcp"], C["xcd"], C["B_tok"], C["tok0"]
                        g0 = 2 * gp
                        py = k.ps([5, 6], "y")
                        for gi in range(2):
                            for hp in range(2):
                                col = (gi * 2 + hp) * 128
                                for hh in range(2):
                                    hl = 2 * hp + hh
                                    k.mm(py[:, col:col + 128], xcp[:, 4 * (g0 + gi) + hl, :], MT[:, gi, hl, :],
                                         start=(hh == 0), stop=False)
                                for hh in range(2):
                                    hl = 2 * hp + hh
                                    k.mm(py[:, col:col + 128], STbp[gp][:, 4 * gi + hl, :], MV[:, gi, hl, :],
                                         start=False, stop=(hh == 1))
                        rows = yT[4 * gp * 128:(4 * gp + 4) * 128, tok0:tok0 + 128]
                        if dr == 0:
                            yst = k.ring("yst", 3, [4, 128], F32)
                            k.cp("act", yst.v().re("p a t -> p (a t)"), py[:, 0:512])
                            k.dma(rows.rearrange("(a p) t -> p a t", p=128), yst.v())
                        else:
                            yf, szt = C["yf"], C["szt"]
                            gs1 = k.ring("gs1", 2, [4, 128], F32)
                            k.tt("dve", gs1.v(), py[:, 0:512].re("p (a t) -> p a t", a=4),
                                 yf[:, 4 * gp:4 * gp + 4, :], ALU.add)
                            go = k.ring("go", 3, [4, 128], BF16)
                            k.tt("pool", go.v(), gs1.v(), szt[:, 4 * gp:4 * gp + 4, :], ALU.mult)
                            k.dma(gT[4 * gp * 128:(4 * gp + 4) * 128, tok0:tok0 + 128]
                                  .rearrange("(a p) t -> p a t", p=128), go.v())
                        pst_ = k.psum[7]
                        for gi in range(2):
                            k.mm(pst_[:, gi * 256:(gi + 1) * 256], B_tok[:, (g0 + gi) * 128:(g0 + gi + 1) * 128],
                                 xcd[:, 4 * (g0 + gi):4 * (g0 + gi) + 4, :].re("p h e -> p (h e)"))
                        tmp = k.ring("sttmp", 2, [8, 64], F32)
                        k.tt("dve", tmp.v(), STp[gp].v().re("p (h e) -> p h e", e=64),
                             E0[:, :, lastcol:lastcol + 1].bc([128, 8, 64]), ALU.mult)
                        k.tt("dve", STp[gp].v(), tmp.v().re("p h e -> p (h e)"), pst_[:, 0:512], ALU.add)
                        sgv = STp[gp].v().re("p (a two e) -> p a two e", two=2, e=64)
                        sbv = STbp[gp].v().re("p (a two) c -> p a two c", two=2)
                        for par in range(2):
                            k.cp("act", sbv[:, :, par, par * 64:par * 64 + 64], sgv[:, :, par, :])

                    Cs = {}

                    def get_C(idx):
                        if idx not in Cs:
                            Cs[idx] = chunk_pre(idx)
                        return Cs[idx]
                    work = [(idx, gp) for idx in range(len(cl)) for gp in range(4)]
                    s1 = {}
                    for i_ in range(len(work) + LA):
                        if i_ < len(work):
                            idx, gp = work[i_]
                            C_ = get_C(idx)
                            if gp == 2 and idx + 1 < len(cl):
                                get_C(idx + 1)
                            s1[i_] = stage1(C_, gp)
                        j_ = i_ - LA
                        if j_ >= 0:
                            idx, gp = work[j_]
                            stage2(get_C(idx), gp, s1.pop(j_))
                            if gp == 3:
                                Cs.pop(idx)
            k.end_phase()

            wA = k.alloc([8, D], BF16, "wA")
            wS = k.alloc([16, D], BF16, "wS")
            wO = k.alloc([8, D], BF16, "wO")
            for kk in range(8):
                k.dma(wA[:, kk, :], w_ao[l, kk * 128:(kk + 1) * 128, :], queue="pool")
            for kk in range(16):
                k.dma(wS[:, kk, :], w_so[l, kk * 128:(kk + 1) * 128, :], queue="pool")
            for kk in range(8):
                k.dma(wO[:, kk, :], w_o[l, kk * 128:(kk + 1) * 128, :], queue="pool")
            g0, _ = PV["ssdg"]
            for kk in range(16):
                k.ts("dve", wS[:, kk, :], wS[:, kk, :], pv[:, l, g0 + kk:g0 + kk + 1], None, ALU.mult)
            for (b, ic, off, n, pos0, si, last) in act_tiles:
                col = NB if ic else b
                aT = k.ring("aT", 1, [8, 512], BF16)
                k.dma(aT[:, :, :n], fm(attT)[:, :, off:off + n])
                gt = k.ring("gTt", 1, [16, 512], BF16)
                k.dma(gt[:, :, :n], fm(gT)[:, :, off:off + n])
                sg = k.ring("sg", 1, [16, 512], BF16)
                k.dma(sg[:, :, :n], fm(sgT)[:, :, off:off + n])
                xt = k.ring("xt", 1, [8, 512], F32)
                k.dma(xt[:, :, :n], fm(xsrc)[:, :, off:off + n])
                sq = k.ring("sq", 1, [16, 512], BF16)
                k.act(sq[:, :, :n], gt[:, :, :n], AF.Square)
                pss = k.psum[0]
                for j in range(16):
                    k.mm(pss[:, :n], ONE, sq[:, j, :n], start=(j == 0), stop=(j == 15))
                rstd = k.ring("rstd", 2, [512], F32)
                k.ts("dve", rstd[:, :n], pss[:, :n], 1.0 / (2 * D), 1e-6, ALU.mult, ALU.add)
                k.act(rstd[:, :n], rstd[:, :n], AF.Sqrt)
                k.recip(rstd[:, :n], rstd[:, :n])
                mT = k.ring("mT", 1, [8, 512], BF16)
                for dj in range(8):
                    psA = k.ps([1, 2], "A")
                    for kk in range(8):
                        k.mm(psA[:, :n], wA[:, kk, dj * 128:(dj + 1) * 128], aT[:, kk, :n],
                             start=(kk == 0), stop=(kk == 7))
                    psS = k.ps([3, 4], "S")
                    for kk in range(16):
                        k.mm(psS[:, :n], wS[:, kk, dj * 128:(dj + 1) * 128], gt[:, kk, :n],
                             start=(kk == 0), stop=(kk == 15))
                    t1 = k.ring("et1", 2, [512], F32)
                    k.tt("dve", t1[:, :n], psA[:, :n], sg[:, dj, :n], ALU.mult)
                    t2 = k.ring("et2", 2, [512], F32)
                    k.tt("dve", t2[:, :n], psS[:, :n], rstd[:, :n], ALU.mult)
                    t3 = k.ring("et3", 2, [512], F32)
                    k.tt("pool", t3[:, :n], t2[:, :n], sg[:, 8 + dj, :n], ALU.mult)
                    k.tt("pool", mT[:, dj, :n], t1[:, :n], t3[:, :n], ALU.add)
                for dj in range(8):
                    psO = k.ps([5, 6], "O")
                    for kk in range(8):
                        k.mm(psO[:, :n], wO[:, kk, dj * 128:(dj + 1) * 128], mT[:, kk, :n],
                             start=(kk == 0), stop=(kk == 7))
                    k.stt("dve", xt[:, dj, :n], psO[:, :n], modT[:, l, 16 + dj, col:col + 1], xt[:, dj, :n],
                          ALU.mult, ALU.add)
                k.dma(fm(xres)[:, :, off:off + n], xt[:, :, :n])
            k.end_phase()

            hts = [k.alloc([8, act_tiles[i][3]], BF16, "h2T%d" % i) for i in range(len(act_tiles))]
            mark = k.off
            modulate(xres, hts, gmF, l, 3, act_tiles)
            k.barrier()
            k.off = mark
            k.rot = {}
            w1fm = w_1[l].rearrange("(k p) c -> p k c", p=128)
            for c0 in range(0, 4 * D, 512):
                wb = k.ring("wb", 3, [8, 512], BF16)
                k.dma(wb.v(), w1fm[:, :, c0:c0 + 512], queue="pool")
                for jc in range(4):
                    r0 = c0 + jc * 128
                    for ti, (b, ic, off, n, pos0, si, last) in enumerate(act_tiles):
                        ps = k.ps([0, 1, 2, 3, 4, 5, 6, 7], "f1")
                        for kk in range(8):
                            k.mm(ps[:, :n], wb[:, kk, jc * 128:(jc + 1) * 128], hts[ti][:, kk, :n],
                                 start=(kk == 0), stop=(kk == 7))
                        r_ = k.ring("relu", 3, [512], BF16)
                        k.act(r_[:, :n], ps[:, :n], AF.Relu)
                        ob = k.ring("ob", 3, [512], BF16)
                        k.tt("pool", ob[:, :n], r_[:, :n], r_[:, :n], ALU.mult)
                        k.dma(uT[r0:r0 + 128, off:off + n], ob[:, :n])
            k.end_phase()

            w2 = k.alloc([32, D], BF16, "w2")
            for kk in range(32):
                k.dma(w2[:, kk, :], w_2[l, kk * 128:(kk + 1) * 128, :], queue="pool")
            for (b, ic, off, n, pos0, si, last) in act_tiles:
                col = NB if ic else b
                ut = k.ring("ut", 2, [32, 512], BF16)
                k.dma(ut[:, :, :n], fm(uT)[:, :, off:off + n])
                xt = k.ring("xt", 2, [8, 512], F32)
                k.dma(xt[:, :, :n], fm(xres)[:, :, off:off + n])
                for dj in range(8):
                    ps = k.ps([0, 1, 2, 3], "f2")
                    for kk in range(32):
                        k.mm(ps[:, :n], w2[:, kk, dj * 128:(dj + 1) * 128], ut[:, kk, :n],
                             start=(kk == 0), stop=(kk == 31))
                    k.stt("dve", xt[:, dj, :n], ps[:, :n], modT[:, l, 40 + dj, col:col + 1], xt[:, dj, :n],
                          ALU.mult, ALU.add)
                k.dma(fm(xres)[:, :, off:off + n], xt[:, :, :n])
            k.end_phase()

        for (b, ic, off, n, pos0, si, last) in tiles:
            if ic:
                continue
            xt = k.ring("xt", 2, [8, 512], F32)
            k.dma(xt[:, :, :n], fm(xres)[:, :, off:off + n])
            sq = k.ring("sq", 2, [8, 512], BF16)
            k.act(sq[:, :, :n], xt[:, :, :n], AF.Square)
            ps = k.ps([0, 1], "fin")
            for j in range(8):
                k.mm(ps[:, :n], ONE, sq[:, j, :n], start=(j == 0), stop=(j == 7))
            rstd = k.ring("rstd", 2, [512], F32)
            k.ts("dve", rstd[:, :n], ps[:, :n], 1.0 / D, 1e-6, ALU.mult, ALU.add)
            k.act(rstd[:, :n], rstd[:, :n], AF.Sqrt)
            k.recip(rstd[:, :n], rstd[:, :n])
            xo = k.ring("xo", 2, [8, 512], F32)
            k.tt("dve", xo[:, :, :n], xt[:, :, :n], rstd[:, :n].un(1).bc([128, 8, n]), ALU.mult)
            for j in range(8):
                k.ts("dve", xo[:, j, :n], xo[:, j, :n], fg[:, j:j + 1], None, ALU.mult)
            o0 = b * NL + pos0
            k.dma(fm(outT)[:, :, o0:o0 + n], xo[:, :, :n])
        k.end_phase()
        k.emit()
    return nc


def _fmaj(v, nchunk):
    return np.ascontiguousarray(np.asarray(v, np.float32).reshape(nchunk, 128).T)


def _consts():
    r = np.arange(128)[:, None]
    c = np.arange(128)[None, :]
    mats = [(r == c), (r <= c), (r > c), (r >= c), (r < c), np.ones((128, 128), bool)]
    return np.ascontiguousarray(np.concatenate([m.astype(np.float32) for m in mats], axis=1))


def _rope(NL, grid_w=64):
    rows = NL // grid_w
    row = np.broadcast_to(np.arange(rows)[:, None], (rows, grid_w)).reshape(-1).astype(np.float32)
    col = np.broadcast_to(np.arange(grid_w)[None, :], (rows, grid_w)).reshape(-1).astype(np.float32)
    inv = (np.float32(10000.0) ** (-np.arange(16, dtype=np.float32) / np.float32(16))).astype(np.float32)
    ang = np.concatenate([row[:, None] * inv, col[:, None] * inv], axis=-1).astype(np.float32)
    cos = np.cos(ang).astype(np.float32).T
    sin = np.sin(ang).astype(np.float32).T
    p = np.arange(128)
    cosT = cos[p % 32]
    sgn = np.where((p % 64) < 32, 1.0, -1.0).astype(np.float32)[:, None]
    sinS = sin[p % 32] * sgn
    return np.ascontiguousarray(np.concatenate([cosT, sinS], axis=1).astype(np.float32))


def host_inputs(inp, NB, NL, NCX, DEPTH, ncores):
    f = lambda a: np.asarray(a, np.float32)
    x, c, ctx, c_ctx = f(inp["x"]), f(inp["c"]), f(inp["ctx"]), f(inp["c_ctx"])
    pvec = np.zeros((DEPTH, 128, NPV), np.float32)

    def put(l, name, arr):
        a, b = PV[name]
        pvec[l, :, a:b] = arr
    for l in range(DEPTH):
        put(l, "gmix", _fmaj(inp["norm_mix_g"][l], 8))
        put(l, "gmlp", _fmaj(inp["norm_mlp_g"][l], 8))
        cw = f(inp["conv_w"][l])
        put(l, "convw", cw.T.reshape(32, 128, 5).transpose(1, 0, 2).reshape(128, 160))
        put(l, "convb", _fmaj(inp["conv_b"][l], 32))
        put(l, "dtb", np.broadcast_to(np.concatenate([f(inp["dt_bias_f"][l]), f(inp["dt_bias_b"][l])])[None], (128, 64)))
        put(l, "alog", np.broadcast_to(np.concatenate([f(inp["a_log_f"][l]), f(inp["a_log_b"][l])])[None], (128, 64)))
        put(l, "ssdd", _fmaj(np.repeat(f(inp["ssd_d"][l]), 64), 16))
        put(l, "ssdg", _fmaj(inp["ssd_norm_g"][l], 16))
        put(l, "subg", np.broadcast_to(f(inp["attn_subln_g"][l])[None], (128, 128)))
        for nm, key in (("lq1", "lambda_q1"), ("lk1", "lambda_k1"), ("lq2", "lambda_q2"), ("lk2", "lambda_k2")):
            put(l, nm, np.broadcast_to(f(inp[key][l])[None], (128, 64)))
        put(l, "adab", _fmaj(inp["ada_b"][l], 48))
    shared = {
        "pvec": pvec, "gvec": _fmaj(inp["final_norm_g"], 8), "consts": _consts(), "rope": _rope(NL),
        "ada_w": f(inp["ada_w"]), "w_in": f(inp["w_in"]), "w_attn_o": f(inp["w_attn_o"]),
        "w_ssd_o": f(inp["w_ssd_o"]), "w_out": f(inp["w_out"]), "w_mlp1": f(inp["w_mlp1"]),
        "w_mlp2": f(inp["w_mlp2"]),
    }
    maps = []
    for ci in range(ncores):
        bs = list(range(ci * NB, (ci + 1) * NB))
        toks = np.concatenate([np.concatenate([x[b], ctx[b]], axis=0) for b in bs], axis=0)
        cv = np.stack([c[b] for b in bs] + [c_ctx], axis=1)
        cvec = cv.reshape(8, 128, NB + 1).transpose(1, 0, 2).reshape(128, 8 * (NB + 1))
        m = dict(shared)
        m["xT"] = np.ascontiguousarray(toks.T)
        m["cvec"] = np.ascontiguousarray(cvec)
        maps.append(m)
    return maps


_NC_CACHE = {}


def kernel(**inputs):
    x = np.asarray(inputs["x"])
    B, NL, _ = x.shape
    NCX = np.asarray(inputs["ctx"]).shape[1]
    DEPTH = np.asarray(inputs["w_in"]).shape[0]
    ncores = 8
    NB = B // ncores
    key = (NB, NL, NCX, DEPTH)
    if key not in _NC_CACHE:
        _NC_CACHE[key] = build(NB, NL, NCX, DEPTH)
    nc = _NC_CACHE[key]
    maps = host_inputs(inputs, NB, NL, NCX, DEPTH, ncores)
    res = run_bass_kernel_spmd(nc, maps, core_ids=list(range(ncores)))
    out = np.empty((B, NL, D), np.float32)
    for ci in range(ncores):
        oT = np.asarray(res.results[ci]["outT"])
        for j in range(NB):
            out[ci * NB + j] = oT[:, j * NL:(j + 1) * NL].T
    return out
```

```python
import math
import numpy as np
import concourse.bass as bass
import concourse.mybir as mybir
from concourse.bass_utils import run_bass_kernel_spmd
from contextlib import ExitStack

F32 = mybir.dt.float32
BF16 = mybir.dt.bfloat16
ALU = mybir.AluOpType
AF = mybir.ActivationFunctionType
AX = mybir.AxisListType

D = 1024
IN_DIM = 11328
NSEM = 80
ARENA_F32 = 53120

PV = {}
_o = 0
for _n, _w in (("gmix", 8), ("gmlp", 8), ("convw", 160), ("convb", 32), ("dtb", 64), ("alog", 64),
               ("ssdd", 16), ("ssdg", 16), ("subg", 128), ("lq1", 64), ("lk1", 64), ("lq2", 64),
               ("lk2", 64), ("adab", 48)):
    PV[_n] = (_o, _o + _w)
    _o += _w
NPV = _o


class Tile:
    def __init__(s, ap, name=""):
        s.ap = ap; s.w = None; s.r = {}; s.dsem = None; s.dcnt = 0; s.name = name

    def __getitem__(s, k):
        return V(s, s.ap[k])

    def v(s):
        return V(s, s.ap)


class V:
    def __init__(s, t, ap):
        s.t = t; s.ap = ap

    def __getitem__(s, k):
        return V(s.t, s.ap[k])

    def re(s, pat, **kw):
        return V(s.t, s.ap.rearrange(pat, **kw))

    def bc(s, shape):
        return V(s.t, s.ap.to_broadcast(list(shape)))

    def un(s, ax):
        return V(s.t, s.ap.unsqueeze(ax))


def _a(x):
    return x.ap if isinstance(x, V) else x


class KB:
    def __init__(s, nc, stack):
        s.nc = nc
        s.q = {e: [] for e in ("pe", "act", "dve", "pool", "sp")}
        s.esem = {e: stack.enter_context(nc.semaphore("e_" + e)) for e in ("pe", "act", "dve", "pool")}
        s.cnt = {e: 0 for e in s.esem}
        s.known = {e: {} for e in s.q}
        s.free_sems = [stack.enter_context(nc.semaphore("d%d" % i)) for i in range(NSEM)]
        s.semcnt = {sm: 0 for sm in s.free_sems}
        s.phase_sems = []
        s.persist = False
        s.arena = stack.enter_context(nc.sbuf_tensor("arena", [128, ARENA_F32], F32))
        s.off = 0
        s.base = 0
        s.psum = []
        s.psum2 = []
        for i in range(4):
            pp = stack.enter_context(nc.psum_tensor("pp%d" % i, [128, 1024], F32))
            s.psum2.append(Tile(pp[:, :], "pp%d" % i))
            s.psum.append(Tile(pp[:, 0:512], "ps%d" % (2 * i)))
            s.psum.append(Tile(pp[:, 512:1024], "ps%d" % (2 * i + 1)))
        s.rot = {}

    def alloc(s, free_shape, dt=F32, name=""):
        n = int(np.prod(free_shape))
        nf = (n + 1) // 2 if dt == BF16 else n
        nf = (nf + 7) // 8 * 8
        assert s.off + nf <= ARENA_F32, "SBUF arena overflow %s %d" % (name, s.off + nf)
        ap = s.arena[:, s.off:s.off + nf]
        s.off += nf
        if dt == BF16:
            ap = ap.bitcast(BF16)
        ap = ap[:, 0:n]
        if len(free_shape) == 2:
            ap = ap.rearrange("p (a b) -> p a b", a=free_shape[0])
        elif len(free_shape) == 3:
            ap = ap.rearrange("p (a b c) -> p a b c", a=free_shape[0], b=free_shape[1])
        return Tile(ap, name)

    def ring(s, key, n, free_shape, dt=F32):
        if key not in s.rot:
            s.rot[key] = [[s.alloc(free_shape, dt, key + str(i)) for i in range(n)], 0]
        r = s.rot[key]
        t = r[0][r[1] % n]
        r[1] += 1
        return t

    def ps(s, idxs, key):
        r = s.rot.setdefault("ps_" + key, [None, 0])
        t = s.psum[idxs[r[1] % len(idxs)]]
        r[1] += 1
        return t

    def ps2(s, idxs, key):
        r = s.rot.setdefault("ps2_" + key, [None, 0])
        t = s.psum2[idxs[r[1] % len(idxs)]]
        r[1] += 1
        return t

    def _getsem(s):
        sm = s.free_sems.pop()
        if not s.persist:
            s.phase_sems.append(sm)
        return sm

    def _waits(s, eng, reads, writes, skip_dma_sem=None):
        w = {}

        def need(d):
            if d is None:
                return
            sem, val, src = d
            if src == "pe" and eng == "pe":
                return
            if w.get(sem, 0) < val:
                w[sem] = val

        for t in reads:
            need(t.w)
        for t in writes:
            if not (skip_dma_sem is not None and t.w is not None and t.w[2] == "dma" and t.w[0] is skip_dma_sem):
                need(t.w)
            for d in t.r.values():
                need(d)
        out = []
        kn = s.known[eng]
        for sem, val in w.items():
            if kn.get(sem, 0) < val:
                kn[sem] = val
                out.append((sem, val))
        return out

    def op(s, eng, fn, outs, ins):
        reads = []
        for v in ins:
            if isinstance(v, V) and v.t not in reads:
                reads.append(v.t)
        writes = []
        for v in outs:
            if isinstance(v, V) and v.t not in writes:
                writes.append(v.t)
        wl = s._waits(eng, reads, writes)
        s.cnt[eng] += 1
        me = (s.esem[eng], s.cnt[eng], eng)
        s.q[eng].append((wl, fn, (s.esem[eng], 1)))
        for t in reads:
            t.r[eng] = me
        for t in writes:
            t.w = me
            t.r = {}

    def dma(s, out, in_, queue="sp"):
        reads = [in_.t] if isinstance(in_, V) else []
        writes = [out.t] if isinstance(out, V) else []
        owner = writes[0] if writes else reads[0]
        if owner.dsem is None:
            owner.dsem = s._getsem()
            owner.dcnt = s.semcnt[owner.dsem]
        wl = s._waits(queue, reads, writes, skip_dma_sem=owner.dsem)
        owner.dcnt += 16
        s.semcnt[owner.dsem] = owner.dcnt
        dep = (owner.dsem, owner.dcnt, "dma")
        oap, iap = _a(out), _a(in_)
        s.q[queue].append((wl, lambda e: e.dma_start(out=oap, in_=iap), (owner.dsem, 16)))
        for t in reads:
            t.r[("dma", id(owner))] = dep
        for t in writes:
            t.w = dep
            t.r = {}

    def barrier(s):
        allsem = {}
        for e, sm in s.esem.items():
            allsem[sm] = s.cnt[e]
        for sm, c in s.semcnt.items():
            allsem[sm] = c
        for eng in s.q:
            kn = s.known[eng]
            wl = []
            for sm, c in allsem.items():
                if c > kn.get(sm, 0):
                    kn[sm] = c
                    wl.append((sm, c))
            s.q[eng].append((wl, None, None))

    def end_phase(s):
        s.barrier()
        s.free_sems.extend(s.phase_sems)
        s.phase_sems = []
        s.off = s.base
        s.rot = {}
        for t in s.psum + s.psum2:
            t.w = None; t.r = {}

    def emit(s):
        with s.nc.Block() as block:
            def rep(name):
                def f(e):
                    for wl, fn, inc in s.q[name]:
                        for sem, val in wl:
                            e.wait_ge(sem, val)
                        if fn is not None:
                            fn(e).then_inc(inc[0], inc[1])
                return f
            block.tensor(rep("pe"))
            block.scalar(rep("act"))
            block.vector(rep("dve"))
            block.gpsimd(rep("pool"))
            block.sync(rep("sp"))

    def mm(s, out, lhsT, rhs, start=True, stop=True, skip=False):
        o, l, r = _a(out), _a(lhsT), _a(rhs)
        if skip:
            s.op("pe", lambda e: e.matmul(o, lhsT=l, rhs=r, start=start, stop=stop, skip_group_check=True),
                 [out], [lhsT, rhs])
        else:
            s.op("pe", lambda e: e.matmul(o, lhsT=l, rhs=r, start=start, stop=stop), [out], [lhsT, rhs])

    def tr(s, out, in_, ident):
        o, i, d = _a(out), _a(in_), _a(ident)
        s.op("pe", lambda e: e.transpose(o, i, d), [out], [in_, ident])

    def act(s, out, in_, func, bias=0.0, scale=1.0, accum=None):
        o, i, b, sc, ac = _a(out), _a(in_), _a(bias), _a(scale), _a(accum)
        if ac is None:
            s.op("act", lambda e: e.activation(out=o, in_=i, func=func, bias=b, scale=sc), [out], [in_, bias, scale])
        else:
            s.op("act", lambda e: e.activation(out=o, in_=i, func=func, bias=b, scale=sc, accum_out=ac),
                 [out, accum], [in_, bias, scale])

    def tt(s, eng, out, a, b, op):
        o, x, y = _a(out), _a(a), _a(b)
        s.op(eng, lambda e: e.tensor_tensor(out=o, in0=x, in1=y, op=op), [out], [a, b])

    def ts(s, eng, out, a, s1, s2, op0, op1=None):
        o, x, c1, c2 = _a(out), _a(a), _a(s1), _a(s2)
        if op1 is None:
            s.op(eng, lambda e: e.tensor_scalar(out=o, in0=x, scalar1=c1, scalar2=None, op0=op0), [out], [a, s1])
        else:
            s.op(eng, lambda e: e.tensor_scalar(out=o, in0=x, scalar1=c1, scalar2=c2, op0=op0, op1=op1),
                 [out], [a, s1, s2])

    def stt(s, eng, out, a, sc, b, op0, op1):
        o, x, c, y = _a(out), _a(a), _a(sc), _a(b)
        s.op(eng, lambda e: e.scalar_tensor_tensor(out=o, in0=x, scalar=c, in1=y, op0=op0, op1=op1),
             [out], [a, sc, b])

    def cp(s, eng, out, a):
        o, x = _a(out), _a(a)
        if eng == "act":
            s.op("act", lambda e: e.activation(out=o, in_=x, func=AF.Copy), [out], [a])
        else:
            s.op(eng, lambda e: e.tensor_copy(out=o, in_=x), [out], [a])

    def memset(s, eng, out, val):
        o = _a(out)
        s.op(eng, lambda e: e.memset(o, val), [out], [])

    def recip(s, out, a):
        o, x = _a(out), _a(a)
        s.op("dve", lambda e: e.reciprocal(out=o, in_=x), [out], [a])

    def rsum(s, out, a):
        o, x = _a(out), _a(a)
        s.op("dve", lambda e: e.tensor_reduce(out=o, in_=x, axis=AX.X, op=ALU.add), [out], [a])


def build(NB=2, NL=2048, NCX=256, DEPTH=4, dbg=False):
    TS = NL + NCX
    T = NB * TS
    NC3 = NB + 1
    NKC = TS // 128
    nc = bass.Bass("TRN2", target_bir_lowering=False)

    def din(name, shape, dt=F32):
        return nc.dram_tensor(name, list(shape), dt, kind="ExternalInput").ap()

    skind = "ExternalOutput" if dbg else "Internal"

    def dscr(name, shape, dt):
        return nc.dram_tensor(name, list(shape), dt, kind=skind).ap()

    xin = din("xT", [D, T])
    cvec_in = din("cvec", [128, 8 * NC3])
    pvec_in = din("pvec", [DEPTH, 128, NPV])
    gvec_in = din("gvec", [128, 8])
    consts_in = din("consts", [128, 6 * 128])
    rope_in = din("rope", [128, 2 * NL])
    ada_w = din("ada_w", [DEPTH, D, 6 * D])
    w_in = din("w_in", [DEPTH, D, IN_DIM])
    w_ao = din("w_attn_o", [DEPTH, D, D])
    w_so = din("w_ssd_o", [DEPTH, 2 * D, D])
    w_o = din("w_out", [DEPTH, D, D])
    w_1 = din("w_mlp1", [DEPTH, D, 4 * D])
    w_2 = din("w_mlp2", [DEPTH, 4 * D, D])
    outT = nc.dram_tensor("outT", [D, NB * NL], F32, kind="ExternalOutput").ap()

    xres = dscr("xres", [D, T], F32)
    qT = dscr("qT", [D, T], BF16)
    kT = dscr("kT", [D, T], BF16)
    vaug = dscr("vaug", [T, 8, 129], BF16)
    szT = dscr("szT", [2 * D, T], BF16)
    xbcT = dscr("xbcT", [4 * D, T], BF16)
    xbtok = dscr("xbtok", [T, 3 * D], BF16)
    dtD = dscr("dtD", [T, 64], F32)
    sgT = dscr("sgT", [2 * D, T], BF16)
    attT = dscr("attT", [D, T], BF16)
    yT = dscr("yT", [2 * D, T], F32)
    gT = dscr("gT", [2 * D, T], BF16)
    uT = dscr("uT", [4 * D, T], BF16)

    def fm(ap):
        return ap.rearrange("(k p) t -> p k t", p=128)

    seqs = []
    for b in range(NB):
        seqs.append((b, 0, b * TS, NL))
        seqs.append((b, 1, b * TS + NL, NCX))
    tiles = []
    for si, (b, ic, so, sl) in enumerate(seqs):
        for o in range(0, sl, 512):
            n = min(512, sl - o)
            tiles.append((b, ic, so + o, n, o, si, o + n >= sl))

    with ExitStack() as stack:
        k = KB(nc, stack)
        k.persist = True
        cst_f = k.alloc([6, 128], F32, "cst_f")
        cst = k.alloc([6, 128], BF16, "cst")
        pv = k.alloc([DEPTH, NPV], F32, "pv")
        modT = k.alloc([DEPTH, 48, NC3], F32, "modT")
        gmA = k.alloc([DEPTH, 8, NC3], F32, "gmA")
        gmF = k.alloc([DEPTH, 8, NC3], F32, "gmF")
        aneg = k.alloc([DEPTH, 64], F32, "aneg")
        nlam = k.alloc([DEPTH, 1], F32, "nlam")
        subg = k.alloc([DEPTH, 128], F32, "subg")
        fg = k.alloc([8], F32, "fg")
        cact_f = k.alloc([8, NC3], F32, "cact_f")
        cact = k.alloc([8, NC3], BF16, "cact")
        sml = k.alloc([16], F32, "sml")
        k.dma(cst_f.v(), consts_in.rearrange("p (a b) -> p a b", a=6))
        k.dma(fg.v(), gvec_in)
        k.dma(cact_f.v(), cvec_in.rearrange("p (a b) -> p a b", a=8))
        for l in range(DEPTH):
            k.dma(pv[:, l, :], pvec_in[l])
        k.cp("dve", cst.v(), cst_f.v())
        k.act(cact_f.v(), cact_f.v(), AF.Silu)
        k.cp("dve", cact.v(), cact_f.v())
        IDN, LE, GT, GE, LT, ONE = (cst[:, i, :] for i in range(6))
        k.persist = False
        k.base = k.off

        def pvs(l, name):
            a, b = PV[name]
            return pv[:, l, a:b]

        for l in range(DEPTH):
            wk = [k.ring("adaw", 8, [6 * D], BF16) for _ in range(8)]
            for kk in range(8):
                k.dma(wk[kk].v(), ada_w[l, kk * 128:(kk + 1) * 128, :], queue="pool")
            ps = k.ps([0, 1], "m")
            for j in range(48):
                for kk in range(8):
                    k.mm(ps[:, j * NC3:(j + 1) * NC3], wk[kk][:, j * 128:(j + 1) * 128], cact[:, kk, :],
                         start=(kk == 0), stop=(kk == 7))
            k.tt("dve", modT[:, l, :, :], ps[:, 0:48 * NC3].re("p (j c) -> p j c", c=NC3),
                 pvs(l, "adab").un(2).bc([128, 48, NC3]), ALU.add)
            for m in (1, 4):
                k.ts("dve", modT[:, l, 8 * m:8 * m + 8, :], modT[:, l, 8 * m:8 * m + 8, :], 1.0, None, ALU.add)
            k.tt("dve", gmA[:, l, :, :], modT[:, l, 8:16, :], pvs(l, "gmix").un(2).bc([128, 8, NC3]), ALU.mult)
            k.tt("dve", gmF[:, l, :, :], modT[:, l, 32:40, :], pvs(l, "gmlp").un(2).bc([128, 8, NC3]), ALU.mult)
            k.act(aneg[:, l, :], pvs(l, "alog"), AF.Exp)
            k.ts("dve", aneg[:, l, :], aneg[:, l, :], -1.0, None, ALU.mult)
            lam_init = 0.8 - 0.6 * math.exp(-0.3 * l)
            tmpl = k.ring("lamt", 2, [64], F32)
            k.tt("dve", tmpl.v(), pvs(l, "lq1"), pvs(l, "lk1"), ALU.mult)
            k.rsum(sml[:, 0:1], tmpl.v())
            tmpl2 = k.ring("lamt", 2, [64], F32)
            k.tt("dve", tmpl2.v(), pvs(l, "lq2"), pvs(l, "lk2"), ALU.mult)
            k.rsum(sml[:, 1:2], tmpl2.v())
            k.act(sml[:, 2:4], sml[:, 0:2], AF.Exp)
            k.tt("dve", sml[:, 4:5], sml[:, 3:4], sml[:, 2:3], ALU.subtract)
            k.ts("dve", nlam[:, l, :], sml[:, 4:5], -lam_init, None, ALU.add)
            k.ts("dve", subg[:, l, :], pvs(l, "subg"), 1.0 - lam_init, None, ALU.mult)
        k.end_phase()

        def modulate(src, hts, gm, l, shift_m, tl):
            for ti, (b, ic, off, n, pos0, si, last) in enumerate(tl):
                col = NB if ic else b
                xt = k.ring("mod_x", 2, [8, 512], F32)
                k.dma(xt[:, :, :n], fm(src)[:, :, off:off + n])
                sq = k.ring("mod_sq", 2, [8, 512], BF16)
                k.act(sq[:, :, :n], xt[:, :, :n], AF.Square)
                ps = k.ps([0, 1], "mod")
                for j in range(8):
                    k.mm(ps[:, :n], ONE, sq[:, j, :n], start=(j == 0), stop=(j == 7))
                rstd = k.ring("mod_r", 2, [512], F32)
                k.ts("dve", rstd[:, :n], ps[:, :n], 1.0 / D, 1e-6, ALU.mult, ALU.add)
                k.act(rstd[:, :n], rstd[:, :n], AF.Sqrt)
                k.recip(rstd[:, :n], rstd[:, :n])
                xn = k.ring("mod_xn", 2, [8, 512], F32)
                k.tt("dve", xn[:, :, :n], xt[:, :, :n], rstd[:, :n].un(1).bc([128, 8, n]), ALU.mult)
                for j in range(8):
                    k.act(hts[ti][:, j, :n], xn[:, j, :n], AF.Identity,
                          bias=modT[:, l, 8 * shift_m + j, col:col + 1], scale=gm[:, l, j, col:col + 1])

        for l in range(DEPTH):
            last_layer = (l == DEPTH - 1)
            xsrc = xin if l == 0 else xres
            act_tiles = [t for t in tiles if not (last_layer and t[1])]
            hts = [k.alloc([8, tiles[i][3]], BF16, "hT%d" % i) for i in range(len(tiles))]
            mark = k.off
            modulate(xsrc, hts, gmA, l, 0, tiles)
            k.barrier()
            k.off = mark
            k.rot = {}
            rope = k.alloc([2, NL], F32, "rope")
            k.dma(rope.v(), rope_in.rearrange("p (a b) -> p a b", a=2))
            vst = [k.alloc([4, 129], BF16, "vst%d" % i) for i in range(3)]
            for t_ in vst:
                k.memset("pool", t_[:, :, 128:129], 1.0)
            vst_i = 0
            dq = []
            dgj = [-1, None]

            def defer(fn, delay):
                dq.append([delay, fn])

            def tick():
                ready = []
                for it in list(dq):
                    it[0] -= 1
                    if it[0] <= 0:
                        ready.append(it)
                        dq.remove(it)
                for it in ready:
                    it[1]()
            segs = (("q", 0, 1024), ("k", 1024, 2048), ("v", 2048, 3072), ("z", 3072, 5120),
                    ("xbc", 5120, 9216), ("dt", 9216, 9280), ("ga", 9280, 10304), ("gs", 10304, 11328))
            wfm = w_in[l].rearrange("(k p) c -> p k c", p=128)
            for sname, s0, s1 in segs:
                for c0 in range(s0, s1, 512):
                    cw_ = min(512, s1 - c0)
                    wb = k.ring("wb", 2, [8, 512], BF16)
                    k.dma(wb[:, :, :cw_], wfm[:, :, c0:c0 + cw_], queue="pool")
                    if sname == "v":
                        vb = (c0 - s0) // 512
                        for ti, (b, ic, off, n, pos0, si, last) in enumerate(tiles):
                            for tb in range(n // 128):
                                ps = k.ps([0, 1, 2, 3, 4, 5], "b")
                                for kk in range(8):
                                    k.mm(ps[:, :], hts[ti][:, kk, tb * 128:(tb + 1) * 128], wb[:, kk, :],
                                         start=(kk == 0), stop=(kk == 7))
                                vs_ = vst[vst_i % 3]; vst_i += 1
                                k.act(vs_[:, :, 0:128], ps[:, :].re("p (h e) -> p h e", h=4), AF.Copy)
                                t0 = off + tb * 128
                                k.dma(vaug[t0:t0 + 128, vb * 4:(vb + 1) * 4, :], vs_.v())
                        continue
                    if sname == "dt":
                        for ti, (b, ic, off, n, pos0, si, last) in enumerate(tiles):
                            nb_ = n // 128
                            ps = k.ps([0, 1, 2, 3, 4, 5], "b")
                            for tb in range(nb_):
                                for kk in range(8):
                                    k.mm(ps[:, tb * 64:(tb + 1) * 64], hts[ti][:, kk, tb * 128:(tb + 1) * 128],
                                         wb[:, kk, :64], start=(kk == 0), stop=(kk == 7))
                            d1 = k.ring("dt1", 2, [4, 64], F32)
                            k.tt("dve", d1[:, :nb_, :], ps[:, :nb_ * 64].re("p (a b) -> p a b", b=64),
                                 pvs(l, "dtb").un(1).bc([128, nb_, 64]), ALU.add)
                            k.act(d1[:, :nb_, :], d1[:, :nb_, :], AF.Exp)
                            d2 = k.ring("dt2", 2, [4, 64], F32)
                            k.act(d2[:, :nb_, :], d1[:, :nb_, :], AF.Ln, bias=1.0)
                            k.dma(dtD[off:off + n, :].rearrange("(a p) c -> p a c", p=128), d2[:, :nb_, :])
                        continue
                    for jc in range(cw_ // 128):
                        gcol = c0 + jc * 128
                        stage = None
                        for ti, (b, ic, off, n, pos0, si, last) in enumerate(tiles):
                            ps = k.ps([0, 1, 2, 3, 4, 5], "b")
                            for kk in range(8):
                                k.mm(ps[:, :n], wb[:, kk, jc * 128:(jc + 1) * 128], hts[ti][:, kk, :n],
                                     start=(kk == 0), stop=(kk == 7))
                            tick()
                            if sname in ("q", "k"):
                                h = (gcol - s0) // 128
                                dst = qT if sname == "q" else kT
                                scl = 0.125 if sname == "q" else 1.0
                                ob = k.ring("ob", 3, [512], BF16)
                                if ic:
                                    k.act(ob[:, :n], ps[:, :n], AF.Identity, scale=scl)
                                else:
                                    qf = k.ring("qf", 2, [512], F32)
                                    k.act(qf[:, :n], ps[:, :n], AF.Identity, scale=scl)
                                    A = k.ring("ropeA", 2, [512], F32)
                                    k.tt("pool", A[:, :n], qf[:, :n], rope[:, 0, pos0:pos0 + n], ALU.mult)
                                    Bt = k.ring("ropeB", 2, [512], F32)
                                    for g in range(4):
                                        gs_ = g ^ 1
                                        k.tt("dve", Bt[32 * g:32 * g + 32, :n], qf[32 * gs_:32 * gs_ + 32, :n],
                                             rope[32 * gs_:32 * gs_ + 32, 1, pos0:pos0 + n], ALU.mult)
                                    k.tt("pool", ob[:, :n], A[:, :n], Bt[:, :n], ALU.add)
                                k.dma(dst[h * 128:(h + 1) * 128, off:off + n], ob[:, :n])
                            elif sname == "z":
                                r0 = gcol - s0
                                ob = k.ring("ob", 3, [512], BF16)
                                k.act(ob[:, :n], ps[:, :n], AF.Silu)
                                k.dma(szT[r0:r0 + 128, off:off + n], ob[:, :n])
                            elif sname in ("ga", "gs"):
                                r0 = gcol - 9280
                                ob = k.ring("ob", 3, [512], BF16)
                                k.act(ob[:, :n], ps[:, :n], AF.Sigmoid)
                                k.dma(sgT[r0:r0 + 128, off:off + n], ob[:, :n])
                            else:
                                jx = (gcol - s0) // 128
                                sb_, sic, soff, slen = seqs[si]
                                a0, _ = PV["convw"]
                                if pos0 == 0:
                                    stage = k.ring("stage", 3, [NL + 4], BF16)
                                    k.memset("pool", stage[:, 0:2], 0.0)
                                    k.memset("pool", stage[:, 2 + slen:4 + slen], 0.0)
                                    stageB = k.ring("stageB", 3, [NL + 4], BF16)
                                    k.memset("pool", stageB[:, 0:2], 0.0)
                                    k.memset("pool", stageB[:, slen:4 + slen], 0.0)
                                    if dgj[0] != jx:
                                        dgj[0] = jx
                                        dgj[1] = k.ring("dg", 2, [5, 128], BF16)
                                        for kq in range(5):
                                            k.ts("dve", dgj[1][:, kq, :], IDN,
                                                 pv[:, l, a0 + jx * 5 + kq:a0 + jx * 5 + kq + 1], None, ALU.mult)
                                k.act(stage[:, 2 + pos0:2 + pos0 + n], ps[:, :n], AF.Copy)
                                k.act(stageB[:, 1 + pos0:1 + pos0 + n], ps[:, :n], AF.Copy)
                                if last:
                                    def conv_unit(stage=stage, stageB=stageB, dg=dgj[1], jx=jx, soff=soff, slen=slen):
                                        cvo = k.ring("cvo", 3, [NL], BF16)
                                        b0_, _ = PV["convb"]
                                        for o_ in range(0, slen, 512):
                                            nn = min(512, slen - o_)
                                            pc = k.ps([0, 1, 2, 3, 4, 5], "b")
                                            for kq in range(5):
                                                src_ = stage[:, o_ + kq:o_ + kq + nn] if kq % 2 == 0 else \
                                                    stageB[:, o_ + kq - 1:o_ + kq - 1 + nn]
                                                k.mm(pc[:, :nn], dg[:, kq, :], src_, start=(kq == 0), stop=(kq == 4))
                                            k.act(cvo[:, o_:o_ + nn], pc[:, :nn], AF.Silu,
                                                  bias=pv[:, l, b0_ + jx:b0_ + jx + 1])
                                        k.dma(xbcT[jx * 128:(jx + 1) * 128, soff:soff + slen], cvo[:, :slen])
                                        if jx < 24:
                                            def trans(cvo=cvo, jx=jx, soff=soff, slen=slen):
                                                for t4 in range(0, slen // 128, 4):
                                                    nq = min(4, slen // 128 - t4)
                                                    pst = k.ps([6, 7], "bt")
                                                    pv_ = V(pst, pst.ap.bitcast(BF16))
                                                    for q_ in range(nq):
                                                        k.tr(pv_[:, q_ * 128:(q_ + 1) * 128],
                                                             cvo[:, (t4 + q_) * 128:(t4 + q_ + 1) * 128], IDN)
                                                    tst = k.ring("tst", 2, [512], BF16)
                                                    k.cp("act", tst[:, :nq * 128], pv_[:, :nq * 128])
                                                    tk0 = soff + t4 * 128
                                                    k.dma(xbtok[tk0:tk0 + nq * 128, jx * 128:(jx + 1) * 128]
                                                          .rearrange("(q p) c -> p q c", p=128),
                                                          tst[:, :nq * 128].re("p (q c) -> p q c", c=128))
                                            defer(trans, 1)
                                    defer(conv_unit, 1)
            while dq:
                tick()
            k.end_phase()

            deferred = []
            deferred2 = []

            def flush_deferred():
                while deferred:
                    deferred.pop(0)()

            def flush_deferred2():
                while deferred2:
                    deferred2.pop(0)()
            for b in range(NB):
                kts = []
                for h in range(8):
                    t_ = k.ring("KT%d" % h, 1, [TS], BF16)
                    k.dma(t_.v(), kT[h * 128:(h + 1) * 128, b * TS:(b + 1) * TS])
                    kts.append(t_)
                va = k.ring("VA", 1, [NKC, 8, 129], BF16)
                k.dma(va.v(), vaug[b * TS:(b + 1) * TS].rearrange("(c p) h e -> p c h e", p=128))
                for h in range(8):
                    for (tb_, ic, off, n, pos0, si, last) in tiles:
                        if tb_ != b or (ic and last_layer):
                            continue
                        nqb = n // 128
                        qt = k.ring("QT", 2, [512], BF16)
                        k.dma(qt[:, :n], qT[h * 128:(h + 1) * 128, off:off + n])
                        kcs = list(range(NL // 128, NKC)) if ic else list(range(NKC))
                        obank = [k.psum[4], k.psum[5], k.psum[6]]

                        def oreg(c, qb):
                            r = c * 4 + qb
                            return obank[r // 3][:, (r % 3) * 129:(r % 3) * 129 + 129]

                        def qk(kc):
                            pss = k.ps2([0, 1], "s")
                            for c in range(2):
                                k.mm(pss[:, c * 512:c * 512 + n], kts[h][64 * c:64 * c + 64, kc * 128:(kc + 1) * 128],
                                     qt[64 * c:64 * c + 64, :n])
                            pt = k.ring("PT", 3, [2, 512], BF16)
                            k.act(pt[:, :, :n], pss[:, :].re("p (c q) -> p c q", c=2)[:, :, :n], AF.Exp)
                            return pt
                        started = set()
                        pend = qk(kcs[0])
                        for i_, kc in enumerate(kcs):
                            pts = pend
                            if i_ + 1 < len(kcs):
                                pend = qk(kcs[i_ + 1])
                            if i_ == 2 or (i_ == 1 and len(kcs) < 3):
                                flush_deferred()
                            if i_ == 8:
                                flush_deferred2()
                            for c in range(2):
                                for qb in range(nqb):
                                    r = c * 4 + qb
                                    st = (r // 3) not in started
                                    started.add(r // 3)
                                    k.mm(oreg(c, qb), pts[:, c, qb * 128:(qb + 1) * 128], va[:, kc, h, :],
                                         start=st, stop=(kc == kcs[-1]), skip=True)
                        if len(kcs) <= 8:
                            flush_deferred2()
                        osb = k.ring("osb", 2, [3, 387], F32)
                        for bi in sorted(started):
                            k.cp("dve", osb[:, bi, :], obank[bi][:, 0:387])

                        def osr(c, qb, osb=osb):
                            r = c * 4 + qb
                            return osb[:, r // 3, (r % 3) * 129:(r % 3) * 129 + 129]

                        def chain(osr=osr, n=n, nqb=nqb, h=h, off=off):
                            rec = k.ring("rec", 2, [8], F32)
                            for c in range(2):
                                for qb in range(nqb):
                                    k.recip(rec[:, c * 4 + qb:c * 4 + qb + 1], osr(c, qb)[:, 128:129])
                            k.ts("dve", rec[:, 4:8], rec[:, 4:8], nlam[:, l, :], None, ALU.mult)
                            ssq = k.ring("ssq", 2, [4], F32)
                            k.memset("dve", ssq.v(), 0.0)
                            os_ = []
                            for qb in range(nqb):
                                t1 = k.ring("at1", 2, [128], F32)
                                k.ts("dve", t1.v(), osr(0, qb)[:, 0:128], rec[:, qb:qb + 1], None, ALU.mult)
                                o_ = k.ring("ao", 8, [128], F32)
                                k.stt("dve", o_.v(), osr(1, qb)[:, 0:128], rec[:, 4 + qb:5 + qb], t1.v(),
                                      ALU.mult, ALU.add)
                                junk = k.ring("ajunk", 2, [128], F32)
                                k.act(junk.v(), o_.v(), AF.Square, accum=ssq[:, qb:qb + 1])
                                os_.append(o_)
                            rs = k.ring("ars", 2, [4], F32)
                            k.ts("dve", rs.v(), ssq.v(), 1.0 / 128, 1e-6, ALU.mult, ALU.add)
                            k.act(rs.v(), rs.v(), AF.Ln)
                            k.act(rs.v(), rs.v(), AF.Exp, scale=-0.5)
                            on = k.ring("aon", 3, [4, 128], BF16)
                            for qb in range(nqb):
                                k.stt("dve", on[:, qb, :], os_[qb].v(), rs[:, qb:qb + 1], subg[:, l, :],
                                      ALU.mult, ALU.mult)

                            def fin(on=on, n=n, nqb=nqb, h=h, off=off):
                                pst = k.psum[7]
                                pv_ = V(pst, pst.ap.bitcast(BF16))
                                for qb in range(nqb):
                                    k.tr(pv_[:, qb * 128:(qb + 1) * 128], on[:, qb, :], IDN)
                                ob = k.ring("ob", 3, [512], BF16)
                                k.cp("act", ob[:, :n], pv_[:, :n])
                                k.dma(attT[h * 128:(h + 1) * 128, off:off + n], ob[:, :n])
                            deferred2.append(fin)
                        deferred.append(chain)
            flush_deferred()
            flush_deferred2()
            k.end_phase()

            d0, _ = PV["ssdd"]
            LA = 2
            for b in range(NB):
                for dr in range(2):
                    Ms, Um, Vm, Dm = (GT, LE, LE, GT) if dr == 0 else (LT, GE, GE, LT)
                    lastcol = 127 if dr == 0 else 0
                    STp = [k.ring("STp%d" % g, 1, [512], F32) for g in range(4)]
                    STbp = [k.ring("STbp%d" % g, 1, [8, 128], BF16) for g in range(4)]
                    for g in range(4):
                        k.memset("pool", STp[g].v(), 0.0)
                        k.memset("pool", STbp[g].v(), 0.0)
                    cl = [(1, ci) for ci in range(NCX // 128)] + [(0, ci) for ci in range(NL // 128)]
                    if dr == 1:
                        cl = [(1, ci) for ci in reversed(range(NCX // 128))] + \
                             [(0, ci) for ci in reversed(range(NL // 128))]

                    def chunk_pre(idx, b=b, dr=dr, cl=cl, Dm=Dm, Um=Um):
                        ic, ci = cl[idx]
                        C = {}
                        tok0 = b * TS + (NL if ic else 0) + ci * 128
                        C["tok0"] = tok0
                        xs_tok = k.ring("xs_tok", 2, [32, 64], BF16)
                        k.dma(xs_tok.v(), xbtok[tok0:tok0 + 128, 0:2048].rearrange("p (h e) -> p h e", e=64))
                        B_tok = k.ring("B_tok", 2, [1024], BF16)
                        k.dma(B_tok.v(), xbtok[tok0:tok0 + 128, 2048:3072])
                        BT = k.ring("BT", 2, [8, 128], BF16)
                        k.dma(BT.v(), fm(xbcT[2048:3072, :])[:, :, tok0:tok0 + 128])
                        CT = k.ring("CT", 2, [8, 128], BF16)
                        k.dma(CT.v(), fm(xbcT[3072:4096, :])[:, :, tok0:tok0 + 128])
                        dt = k.ring("dt", 2, [32], F32)
                        k.dma(dt.v(), dtD[tok0:tok0 + 128, dr * 32:(dr + 1) * 32])
                        C.update(B_tok=B_tok, BT=BT, CT=CT)
                        if dr == 1:
                            yf = k.ring("yf", 2, [16, 128], F32)
                            k.dma(yf.v(), fm(yT)[:, :, tok0:tok0 + 128])
                            xsT = k.ring("xsT", 2, [16, 128], BF16)
                            k.dma(xsT.v(), fm(xbcT[0:2048, :])[:, :, tok0:tok0 + 128])
                            szt = k.ring("szt", 2, [16, 128], BF16)
                            k.dma(szt.v(), fm(szT)[:, :, tok0:tok0 + 128])
                            xsD = k.ring("xsD", 1, [16, 128], F32)
                            k.tt("pool", xsD.v(), xsT.v(), pv[:, l, d0:d0 + 16].un(2).bc([128, 16, 128]), ALU.mult)
                            k.tt("pool", yf.v(), yf.v(), xsD.v(), ALU.add)
                            C.update(yf=yf, szt=szt)
                        dta = k.ring("dta", 2, [32], F32)
                        k.tt("dve", dta.v(), dt.v(), aneg[:, l, dr * 32:(dr + 1) * 32], ALU.mult)
                        dhi = k.ring("dhi", 2, [32], BF16)
                        k.cp("dve", dhi.v(), dta.v())
                        dlf = k.ring("dlf", 2, [32], F32)
                        k.tt("dve", dlf.v(), dta.v(), dhi.v(), ALU.subtract)
                        dlo = k.ring("dlo", 2, [32], BF16)
                        k.cp("dve", dlo.v(), dlf.v())
                        psd = k.psum[4]
                        k.mm(psd[:, 256:288], Dm, dhi.v(), start=True, stop=False)
                        k.mm(psd[:, 256:288], Dm, dlo.v(), start=False, stop=True)
                        dte = k.ring("dte", 2, [32], F32)
                        k.act(dte.v(), psd[:, 256:288], AF.Exp)
                        xcp = k.ring("xcp", 2, [32, 128], BF16)
                        if k.rot["xcp"][1] <= 2:
                            k.memset("pool", xcp.v(), 0.0)
                        xcv = xcp.v().re("p (a two) c -> p a two c", two=2)
                        xsv = xs_tok.v().re("p (a two) e -> p a two e", two=2)
                        dtv = dt.v().re("p (a two) -> p a two", two=2)
                        for par in range(2):
                            k.tt("pool", xcv[:, :, par, par * 64:par * 64 + 64], xsv[:, :, par, :],
                                 dtv[:, :, par].un(2).bc([128, 16, 64]), ALU.mult)
                        xcd = k.ring("xcd", 2, [32, 64], BF16)
                        dev = dte.v().re("p (a two) -> p a two", two=2)
                        xdv = xcd.v().re("p (a two) e -> p a two e", two=2)
                        for par in range(2):
                            k.tt("pool", xdv[:, :, par, :], xcv[:, :, par, par * 64:par * 64 + 64],
                                 dev[:, :, par].un(2).bc([128, 16, 64]), ALU.mult)
                        rhi = k.ring("rhi", 2, [32, 128], BF16)
                        k.tt("dve", rhi.v(), dhi.v().un(2).bc([128, 32, 128]), Um.un(1).bc([128, 32, 128]), ALU.mult)
                        C.update(xcp=xcp, xcd=xcd, rhi=rhi)
                        return C

                    def stage1(C, gp, Ms=Ms, Vm=Vm):
                        rhi, BT, CT = C["rhi"], C["BT"], C["CT"]
                        g0 = 2 * gp
                        pseg = k.ps2([0, 1], "segcs")
                        for gi in range(2):
                            k.mm(pseg[:, gi * 512:(gi + 1) * 512], Ms,
                                 rhi[:, 4 * (g0 + gi):4 * (g0 + gi) + 4, :].re("p h l -> p (h l)"))
                        pcs = k.ps2([0, 1], "segcs")
                        for gi in range(2):
                            k.mm(pcs[:, gi * 512:(gi + 1) * 512], ONE,
                                 rhi[:, 4 * (g0 + gi):4 * (g0 + gi) + 4, :].re("p h l -> p (h l)"))
                        E = k.ring("E", LA + 2, [8, 128], BF16)
                        k.act(E.v().re("p h l -> p (h l)"), pseg[:, :], AF.Exp)
                        E0 = k.ring("E0", LA + 2, [8, 128], F32)
                        k.act(E0.v().re("p h l -> p (h l)"), pcs[:, :], AF.Exp)
                        pcb = k.psum[4]
                        for gi in range(2):
                            k.mm(pcb[:, gi * 128:(gi + 1) * 128], BT[:, g0 + gi, :], CT[:, g0 + gi, :])
                        cbm = k.ring("cbm", LA + 2, [2, 128], BF16)
                        k.tt("dve", cbm.v(), pcb[:, 0:256].re("p (a l) -> p a l", a=2),
                             Vm.un(1).bc([128, 2, 128]), ALU.mult)
                        MT = k.ring("MT", LA + 2, [2, 4, 128], BF16)
                        k.tt("dve", MT.v(), E.v().re("p (a h) l -> p a h l", a=2),
                             cbm.v().un(2).bc([128, 2, 4, 128]), ALU.mult)
                        MV = k.ring("MV", LA + 2, [2, 4, 128], BF16)
                        k.tt("pool", MV.v(), E0.v().re("p (a h) l -> p a h l", a=2),
                             CT[:, g0:g0 + 2, :].un(2).bc([128, 2, 4, 128]), ALU.mult)
                        return (E0, MT, MV)

                    def stage2(C, gp, S, dr=dr, lastcol=lastcol, STp=STp, STbp=STbp):
                        E0, MT, MV = S
                        xcp, xcd, B_tok, tok0 = C["xcp"], C["xcd"], C["B_tok"], C["tok0"]
                        g0 = 2 * gp
                        py = k.ps([5, 6, 7], "yst")
                        for gi in range(2):
                            for hp in range(2):
                                col = (gi * 2 + hp) * 128
                                for hh in range(2):
                                    hl = 2 * hp + hh
                                    k.mm(py[:, col:col + 128], xcp[:, 4 * (g0 + gi) + hl, :], MT[:, gi, hl, :],
                                         start=(hh == 0), stop=False)
                                for hh in range(2):
                                    hl = 2 * hp + hh
                                    k.mm(py[:, col:col + 128], STbp[gp][:, 4 * gi + hl, :], MV[:, gi, hl, :],
                                         start=False, stop=(hh == 1))
                        rows = yT[4 * gp * 128:(4 * gp + 4) * 128, tok0:tok0 + 128]
                        if dr == 0:
                            yst = k.ring("yst", 3, [4, 128], F32)
                            k.cp("act", yst.v().re("p a t -> p (a t)"), py[:, 0:512])
                            k.dma(rows.rearrange("(a p) t -> p a t", p=128), yst.v())
                        else:
                            yf, szt = C["yf"], C["szt"]
                            gs1 = k.ring("gs1", 2, [4, 128], F32)
                            k.tt("dve", gs1.v(), py[:, 0:512].re("p (a t) -> p a t", a=4),
                                 yf[:, 4 * gp:4 * gp + 4, :], ALU.add)
                            go = k.ring("go", 3, [4, 128], BF16)
                            k.tt("pool", go.v(), gs1.v(), szt[:, 4 * gp:4 * gp + 4, :], ALU.mult)
                            k.dma(gT[4 * gp * 128:(4 * gp + 4) * 128, tok0:tok0 + 128]
                                  .rearrange("(a p) t -> p a t", p=128), go.v())
                        pst_ = k.ps([5, 6, 7], "yst")
                        for gi in range(2):
                            k.mm(pst_[:, gi * 256:(gi + 1) * 256], B_tok[:, (g0 + gi) * 128:(g0 + gi + 1) * 128],
                                 xcd[:, 4 * (g0 + gi):4 * (g0 + gi) + 4, :].re("p h e -> p (h e)"))
                        tmp = k.ring("sttmp", 2, [8, 64], F32)
                        k.tt("dve", tmp.v(), STp[gp].v().re("p (h e) -> p h e", e=64),
                             E0[:, :, lastcol:lastcol + 1].bc([128, 8, 64]), ALU.mult)
                        k.tt("dve", STp[gp].v(), tmp.v().re("p h e -> p (h e)"), pst_[:, 0:512], ALU.add)
                        sgv = STp[gp].v().re("p (a two e) -> p a two e", two=2, e=64)
                        sbv = STbp[gp].v().re("p (a two) c -> p a two c", two=2)
                        for par in range(2):
                            k.cp("act", sbv[:, :, par, par * 64:par * 64 + 64], sgv[:, :, par, :])

                    Cs = {}

                    def get_C(idx):
                        if idx not in Cs:
                            Cs[idx] = chunk_pre(idx)
                        return Cs[idx]
                    work = [(idx, gp) for idx in range(len(cl)) for gp in range(4)]
                    s1 = {}
                    for i_ in range(len(work) + LA):
                        if i_ < len(work):
                            idx, gp = work[i_]
                            C_ = get_C(idx)
                            if gp == 2 and idx + 1 < len(cl):
                                get_C(idx + 1)
                            s1[i_] = stage1(C_, gp)
                        j_ = i_ - LA
                        if j_ >= 0:
                            idx, gp = work[j_]
                            stage2(get_C(idx), gp, s1.pop(j_))
                            if gp == 3:
                                Cs.pop(idx)
            k.end_phase()

            wA = k.alloc([8, D], BF16, "wA")
            wS = k.alloc([16, D], BF16, "wS")
            wO = k.alloc([8, D], BF16, "wO")
            for kk in range(8):
                k.dma(wA[:, kk, :], w_ao[l, kk * 128:(kk + 1) * 128, :], queue="pool")
            for kk in range(16):
                k.dma(wS[:, kk, :], w_so[l, kk * 128:(kk + 1) * 128, :], queue="pool")
            for kk in range(8):
                k.dma(wO[:, kk, :], w_o[l, kk * 128:(kk + 1) * 128, :], queue="pool")
            g0, _ = PV["ssdg"]
            for kk in range(16):
                k.ts("dve", wS[:, kk, :], wS[:, kk, :], pv[:, l, g0 + kk:g0 + kk + 1], None, ALU.mult)
            for (b, ic, off, n, pos0, si, last) in act_tiles:
                col = NB if ic else b
                aT = k.ring("aT", 1, [8, 512], BF16)
                k.dma(aT[:, :, :n], fm(attT)[:, :, off:off + n])
                gt = k.ring("gTt", 1, [16, 512], BF16)
                k.dma(gt[:, :, :n], fm(gT)[:, :, off:off + n])
                sg = k.ring("sg", 1, [16, 512], BF16)
                k.dma(sg[:, :, :n], fm(sgT)[:, :, off:off + n])
                xt = k.ring("xt", 1, [8, 512], F32)
                k.dma(xt[:, :, :n], fm(xsrc)[:, :, off:off + n])
                sq = k.ring("sq", 1, [16, 512], BF16)
                k.act(sq[:, :, :n], gt[:, :, :n], AF.Square)
                pss = k.psum[0]
                for j in range(16):
                    k.mm(pss[:, :n], ONE, sq[:, j, :n], start=(j == 0), stop=(j == 15))
                rstd = k.ring("rstd", 2, [512], F32)
                k.ts("dve", rstd[:, :n], pss[:, :n], 1.0 / (2 * D), 1e-6, ALU.mult, ALU.add)
                k.act(rstd[:, :n], rstd[:, :n], AF.Sqrt)
                k.recip(rstd[:, :n], rstd[:, :n])
                mT = k.ring("mT", 1, [8, 512], BF16)
                for dj in range(8):
                    psA = k.ps([1, 2], "A")
                    for kk in range(8):
                        k.mm(psA[:, :n], wA[:, kk, dj * 128:(dj + 1) * 128], aT[:, kk, :n],
                             start=(kk == 0), stop=(kk == 7))
                    psS = k.ps([3, 4], "S")
                    for kk in range(16):
                        k.mm(psS[:, :n], wS[:, kk, dj * 128:(dj + 1) * 128], gt[:, kk, :n],
                             start=(kk == 0), stop=(kk == 15))
                    t1 = k.ring("et1", 2, [512], F32)
                    k.tt("dve", t1[:, :n], psA[:, :n], sg[:, dj, :n], ALU.mult)
                    t2 = k.ring("et2", 2, [512], F32)
                    k.tt("dve", t2[:, :n], psS[:, :n], rstd[:, :n], ALU.mult)
                    t3 = k.ring("et3", 2, [512], F32)
                    k.tt("pool", t3[:, :n], t2[:, :n], sg[:, 8 + dj, :n], ALU.mult)
                    k.tt("pool", mT[:, dj, :n], t1[:, :n], t3[:, :n], ALU.add)
                for dj in range(8):
                    psO = k.ps([5, 6], "O")
                    for kk in range(8):
                        k.mm(psO[:, :n], wO[:, kk, dj * 128:(dj + 1) * 128], mT[:, kk, :n],
                             start=(kk == 0), stop=(kk == 7))
                    k.stt("dve", xt[:, dj, :n], psO[:, :n], modT[:, l, 16 + dj, col:col + 1], xt[:, dj, :n],
                          ALU.mult, ALU.add)
                k.dma(fm(xres)[:, :, off:off + n], xt[:, :, :n])
            k.end_phase()

            hts = [k.alloc([8, act_tiles[i][3]], BF16, "h2T%d" % i) for i in range(len(act_tiles))]
            mark = k.off
            modulate(xres, hts, gmF, l, 3, act_tiles)
            k.barrier()
            k.off = mark
            k.rot = {}
            w1fm = w_1[l].rearrange("(k p) c -> p k c", p=128)
            for c0 in range(0, 4 * D, 512):
                wb = k.ring("wb", 3, [8, 512], BF16)
                k.dma(wb.v(), w1fm[:, :, c0:c0 + 512], queue="pool")
                for jc in range(4):
                    r0 = c0 + jc * 128
                    for ti, (b, ic, off, n, pos0, si, last) in enumerate(act_tiles):
                        ps = k.ps([0, 1, 2, 3, 4, 5, 6, 7], "f1")
                        for kk in range(8):
                            k.mm(ps[:, :n], wb[:, kk, jc * 128:(jc + 1) * 128], hts[ti][:, kk, :n],
                                 start=(kk == 0), stop=(kk == 7))
                        r_ = k.ring("relu", 3, [512], BF16)
                        k.act(r_[:, :n], ps[:, :n], AF.Relu)
                        ob = k.ring("ob", 3, [512], BF16)
                        k.tt("pool", ob[:, :n], r_[:, :n], r_[:, :n], ALU.mult)
                        k.dma(uT[r0:r0 + 128, off:off + n], ob[:, :n])
            k.end_phase()

            w2 = k.alloc([32, D], BF16, "w2")
            for kk in range(32):
                k.dma(w2[:, kk, :], w_2[l, kk * 128:(kk + 1) * 128, :], queue="pool")
            for (b, ic, off, n, pos0, si, last) in act_tiles:
                col = NB if ic else b
                ut = k.ring("ut", 2, [32, 512], BF16)
                k.dma(ut[:, :, :n], fm(uT)[:, :, off:off + n])
                xt = k.ring("xt", 2, [8, 512], F32)
                k.dma(xt[:, :, :n], fm(xres)[:, :, off:off + n])
                for dj in range(8):
                    ps = k.ps([0, 1, 2, 3], "f2")
                    for kk in range(32):
                        k.mm(ps[:, :n], w2[:, kk, dj * 128:(dj + 1) * 128], ut[:, kk, :n],
                             start=(kk == 0), stop=(kk == 31))
                    k.stt("dve", xt[:, dj, :n], ps[:, :n], modT[:, l, 40 + dj, col:col + 1], xt[:, dj, :n],
                          ALU.mult, ALU.add)
                k.dma(fm(xres)[:, :, off:off + n], xt[:, :, :n])
            k.end_phase()

        for (b, ic, off, n, pos0, si, last) in tiles:
            if ic:
                continue
            xt = k.ring("xt", 2, [8, 512], F32)
            k.dma(xt[:, :, :n], fm(xres)[:, :, off:off + n])
            sq = k.ring("sq", 2, [8, 512], BF16)
            k.act(sq[:, :, :n], xt[:, :, :n], AF.Square)
            ps = k.ps([0, 1], "fin")
            for j in range(8):
                k.mm(ps[:, :n], ONE, sq[:, j, :n], start=(j == 0), stop=(j == 7))
            rstd = k.ring("rstd", 2, [512], F32)
            k.ts("dve", rstd[:, :n], ps[:, :n], 1.0 / D, 1e-6, ALU.mult, ALU.add)
            k.act(rstd[:, :n], rstd[:, :n], AF.Sqrt)
            k.recip(rstd[:, :n], rstd[:, :n])
            xo = k.ring("xo", 2, [8, 512], F32)
            k.tt("dve", xo[:, :, :n], xt[:, :, :n], rstd[:, :n].un(1).bc([128, 8, n]), ALU.mult)
            for j in range(8):
                k.ts("dve", xo[:, j, :n], xo[:, j, :n], fg[:, j:j + 1], None, ALU.mult)
            o0 = b * NL + pos0
            k.dma(fm(outT)[:, :, o0:o0 + n], xo[:, :, :n])
        k.end_phase()
        k.emit()
    return nc


def _fmaj(v, nchunk):
    return np.ascontiguousarray(np.asarray(v, np.float32).reshape(nchunk, 128).T)


def _consts():
    r = np.arange(128)[:, None]
    c = np.arange(128)[None, :]
    mats = [(r == c), (r <= c), (r > c), (r >= c), (r < c), np.ones((128, 128), bool)]
    return np.ascontiguousarray(np.concatenate([m.astype(np.float32) for m in mats], axis=1))


def _rope(NL, grid_w=64):
    rows = NL // grid_w
    row = np.broadcast_to(np.arange(rows)[:, None], (rows, grid_w)).reshape(-1).astype(np.float32)
    col = np.broadcast_to(np.arange(grid_w)[None, :], (rows, grid_w)).reshape(-1).astype(np.float32)
    inv = (np.float32(10000.0) ** (-np.arange(16, dtype=np.float32) / np.float32(16))).astype(np.float32)
    ang = np.concatenate([row[:, None] * inv, col[:, None] * inv], axis=-1).astype(np.float32)
    cos = np.cos(ang).astype(np.float32).T
    sin = np.sin(ang).astype(np.float32).T
    p = np.arange(128)
    cosT = cos[p % 32]
    sgn = np.where((p % 64) < 32, 1.0, -1.0).astype(np.float32)[:, None]
    sinS = sin[p % 32] * sgn
    return np.ascontiguousarray(np.concatenate([cosT, sinS], axis=1).astype(np.float32))


def host_inputs(inp, NB, NL, NCX, DEPTH, ncores):
    f = lambda a: np.asarray(a, np.float32)
    x, c, ctx, c_ctx = f(inp["x"]), f(inp["c"]), f(inp["ctx"]), f(inp["c_ctx"])
    pvec = np.zeros((DEPTH, 128, NPV), np.float32)

    def put(l, name, arr):
        a, b = PV[name]
        pvec[l, :, a:b] = arr
    for l in range(DEPTH):
        put(l, "gmix", _fmaj(inp["norm_mix_g"][l], 8))
        put(l, "gmlp", _fmaj(inp["norm_mlp_g"][l], 8))
        cw = f(inp["conv_w"][l])
        put(l, "convw", cw.T.reshape(32, 128, 5).transpose(1, 0, 2).reshape(128, 160))
        put(l, "convb", _fmaj(inp["conv_b"][l], 32))
        put(l, "dtb", np.broadcast_to(np.concatenate([f(inp["dt_bias_f"][l]), f(inp["dt_bias_b"][l])])[None], (128, 64)))
        put(l, "alog", np.broadcast_to(np.concatenate([f(inp["a_log_f"][l]), f(inp["a_log_b"][l])])[None], (128, 64)))
        put(l, "ssdd", _fmaj(np.repeat(f(inp["ssd_d"][l]), 64), 16))
        put(l, "ssdg", _fmaj(inp["ssd_norm_g"][l], 16))
        put(l, "subg", np.broadcast_to(f(inp["attn_subln_g"][l])[None], (128, 128)))
        for nm, key in (("lq1", "lambda_q1"), ("lk1", "lambda_k1"), ("lq2", "lambda_q2"), ("lk2", "lambda_k2")):
            put(l, nm, np.broadcast_to(f(inp[key][l])[None], (128, 64)))
        put(l, "adab", _fmaj(inp["ada_b"][l], 48))
    shared = {
        "pvec": pvec, "gvec": _fmaj(inp["final_norm_g"], 8), "consts": _consts(), "rope": _rope(NL),
        "ada_w": f(inp["ada_w"]), "w_in": f(inp["w_in"]), "w_attn_o": f(inp["w_attn_o"]),
        "w_ssd_o": f(inp["w_ssd_o"]), "w_out": f(inp["w_out"]), "w_mlp1": f(inp["w_mlp1"]),
        "w_mlp2": f(inp["w_mlp2"]),
    }
    maps = []
    for ci in range(ncores):
        bs = list(range(ci * NB, (ci + 1) * NB))
        toks = np.concatenate([np.concatenate([x[b], ctx[b]], axis=0) for b in bs], axis=0)
        cv = np.stack([c[b] for b in bs] + [c_ctx], axis=1)
        cvec = cv.reshape(8, 128, NB + 1).transpose(1, 0, 2).reshape(128, 8 * (NB + 1))
        m = dict(shared)
        m["xT"] = np.ascontiguousarray(toks.T)
        m["cvec"] = np.ascontiguousarray(cvec)
        maps.append(m)
    return maps


_NC_CACHE = {}


def kernel(**inputs):
    x = np.asarray(inputs["x"])
    B, NL, _ = x.shape
    NCX = np.asarray(inputs["ctx"]).shape[1]
    DEPTH = np.asarray(inputs["w_in"]).shape[0]
    ncores = 8
    NB = B // ncores
    key = (NB, NL, NCX, DEPTH)
    if key not in _NC_CACHE:
        _NC_CACHE[key] = build(NB, NL, NCX, DEPTH)
    nc = _NC_CACHE[key]
    maps = host_inputs(inputs, NB, NL, NCX, DEPTH, ncores)
    res = run_bass_kernel_spmd(nc, maps, core_ids=list(range(ncores)))
    out = np.empty((B, NL, D), np.float32)
    for ci in range(ncores):
        oT = np.asarray(res.results[ci]["outT"])
        for j in range(NB):
            out[ci * NB + j] = oT[:, j * NL:(j + 1) * NL].T
    return out
```

```python
import math
import numpy as np
import concourse.bass as bass
import concourse.mybir as mybir
from concourse.bass_utils import run_bass_kernel_spmd
from contextlib import ExitStack

F32 = mybir.dt.float32
BF16 = mybir.dt.bfloat16
ALU = mybir.AluOpType
AF = mybir.ActivationFunctionType
AX = mybir.AxisListType

D = 1024
IN_DIM = 11328
NSEM = 80
ARENA_F32 = 53120

PV = {}
_o = 0
for _n, _w in (("gmix", 8), ("gmlp", 8), ("convw", 160), ("convb", 32), ("dtb", 64), ("alog", 64),
               ("ssdd", 16), ("ssdg", 16), ("subg", 128), ("lq1", 64), ("lk1", 64), ("lq2", 64),
               ("lk2", 64), ("adab", 48)):
    PV[_n] = (_o, _o + _w)
    _o += _w
NPV = _o


class Tile:
    def __init__(s, ap, name=""):
        s.ap = ap; s.w = None; s.r = {}; s.dsem = None; s.dcnt = 0; s.name = name

    def __getitem__(s, k):
        return V(s, s.ap[k])

    def v(s):
        return V(s, s.ap)


class V:
    def __init__(s, t, ap):
        s.t = t; s.ap = ap

    def __getitem__(s, k):
        return V(s.t, s.ap[k])

    def re(s, pat, **kw):
        return V(s.t, s.ap.rearrange(pat, **kw))

    def bc(s, shape):
        return V(s.t, s.ap.to_broadcast(list(shape)))

    def un(s, ax):
        return V(s.t, s.ap.unsqueeze(ax))


def _a(x):
    return x.ap if isinstance(x, V) else x


class KB:
    def __init__(s, nc, stack):
        s.nc = nc
        s.q = {e: [] for e in ("pe", "act", "dve", "pool", "sp")}
        s.esem = {e: stack.enter_context(nc.semaphore("e_" + e)) for e in ("pe", "act", "dve", "pool")}
        s.cnt = {e: 0 for e in s.esem}
        s.known = {e: {} for e in s.q}
        s.free_sems = [stack.enter_context(nc.semaphore("d%d" % i)) for i in range(NSEM)]
        s.semcnt = {sm: 0 for sm in s.free_sems}
        s.phase_sems = []
        s.persist = False
        s.arena = stack.enter_context(nc.sbuf_tensor("arena", [128, ARENA_F32], F32))
        s.off = 0
        s.base = 0
        s.psum = []
        s.psum2 = []
        for i in range(4):
            pp = stack.enter_context(nc.psum_tensor("pp%d" % i, [128, 1024], F32))
            s.psum2.append(Tile(pp[:, :], "pp%d" % i))
            s.psum.append(Tile(pp[:, 0:512], "ps%d" % (2 * i)))
            s.psum.append(Tile(pp[:, 512:1024], "ps%d" % (2 * i + 1)))
        s.rot = {}

    def alloc(s, free_shape, dt=F32, name=""):
        n = int(np.prod(free_shape))
        nf = (n + 1) // 2 if dt == BF16 else n
        nf = (nf + 7) // 8 * 8
        assert s.off + nf <= ARENA_F32, "SBUF arena overflow %s %d" % (name, s.off + nf)
        ap = s.arena[:, s.off:s.off + nf]
        s.off += nf
        if dt == BF16:
            ap = ap.bitcast(BF16)
        ap = ap[:, 0:n]
        if len(free_shape) == 2:
            ap = ap.rearrange("p (a b) -> p a b", a=free_shape[0])
        elif len(free_shape) == 3:
            ap = ap.rearrange("p (a b c) -> p a b c", a=free_shape[0], b=free_shape[1])
        return Tile(ap, name)

    def ring(s, key, n, free_shape, dt=F32):
        if key not in s.rot:
            s.rot[key] = [[s.alloc(free_shape, dt, key + str(i)) for i in range(n)], 0]
        r = s.rot[key]
        t = r[0][r[1] % n]
        r[1] += 1
        return t

    def ps(s, idxs, key):
        r = s.rot.setdefault("ps_" + key, [None, 0])
        t = s.psum[idxs[r[1] % len(idxs)]]
        r[1] += 1
        return t

    def ps2(s, idxs, key):
        r = s.rot.setdefault("ps2_" + key, [None, 0])
        t = s.psum2[idxs[r[1] % len(idxs)]]
        r[1] += 1
        return t

    def _getsem(s):
        sm = s.free_sems.pop()
        if not s.persist:
            s.phase_sems.append(sm)
        return sm

    def _waits(s, eng, reads, writes, skip_dma_sem=None):
        w = {}

        def need(d):
            if d is None:
                return
            sem, val, src = d
            if src == "pe" and eng == "pe":
                return
            if w.get(sem, 0) < val:
                w[sem] = val

        for t in reads:
            need(t.w)
        for t in writes:
            if not (skip_dma_sem is not None and t.w is not None and t.w[2] == "dma" and t.w[0] is skip_dma_sem):
                need(t.w)
            for d in t.r.values():
                need(d)
        out = []
        kn = s.known[eng]
        for sem, val in w.items():
            if kn.get(sem, 0) < val:
                kn[sem] = val
                out.append((sem, val))
        return out

    def op(s, eng, fn, outs, ins):
        reads = []
        for v in ins:
            if isinstance(v, V) and v.t not in reads:
                reads.append(v.t)
        writes = []
        for v in outs:
            if isinstance(v, V) and v.t not in writes:
                writes.append(v.t)
        wl = s._waits(eng, reads, writes)
        s.cnt[eng] += 1
        me = (s.esem[eng], s.cnt[eng], eng)
        s.q[eng].append((wl, fn, (s.esem[eng], 1)))
        for t in reads:
            t.r[eng] = me
        for t in writes:
            t.w = me
            t.r = {}

    def dma(s, out, in_, queue="sp"):
        reads = [in_.t] if isinstance(in_, V) else []
        writes = [out.t] if isinstance(out, V) else []
        owner = writes[0] if writes else reads[0]
        if owner.dsem is None:
            owner.dsem = s._getsem()
            owner.dcnt = s.semcnt[owner.dsem]
        wl = s._waits(queue, reads, writes, skip_dma_sem=owner.dsem)
        owner.dcnt += 16
        s.semcnt[owner.dsem] = owner.dcnt
        dep = (owner.dsem, owner.dcnt, "dma")
        oap, iap = _a(out), _a(in_)
        s.q[queue].append((wl, lambda e: e.dma_start(out=oap, in_=iap), (owner.dsem, 16)))
        for t in reads:
            t.r[("dma", id(owner))] = dep
        for t in writes:
            t.w = dep
            t.r = {}

    def barrier(s):
        allsem = {}
        for e, sm in s.esem.items():
            allsem[sm] = s.cnt[e]
        for sm, c in s.semcnt.items():
            allsem[sm] = c
        for eng in s.q:
            kn = s.known[eng]
            wl = []
            for sm, c in allsem.items():
                if c > kn.get(sm, 0):
                    kn[sm] = c
                    wl.append((sm, c))
            s.q[eng].append((wl, None, None))

    def end_phase(s):
        s.barrier()
        s.free_sems.extend(s.phase_sems)
        s.phase_sems = []
        s.off = s.base
        s.rot = {}
        for t in s.psum + s.psum2:
            t.w = None; t.r = {}

    def emit(s):
        with s.nc.Block() as block:
            def rep(name):
                def f(e):
                    for wl, fn, inc in s.q[name]:
                        for sem, val in wl:
                            e.wait_ge(sem, val)
                        if fn is not None:
                            fn(e).then_inc(inc[0], inc[1])
                return f
            block.tensor(rep("pe"))
            block.scalar(rep("act"))
            block.vector(rep("dve"))
            block.gpsimd(rep("pool"))
            block.sync(rep("sp"))

    def mm(s, out, lhsT, rhs, start=True, stop=True, skip=False):
        o, l, r = _a(out), _a(lhsT), _a(rhs)
        if skip:
            s.op("pe", lambda e: e.matmul(o, lhsT=l, rhs=r, start=start, stop=stop, skip_group_check=True),
                 [out], [lhsT, rhs])
        else:
            s.op("pe", lambda e: e.matmul(o, lhsT=l, rhs=r, start=start, stop=stop), [out], [lhsT, rhs])

    def tr(s, out, in_, ident):
        o, i, d = _a(out), _a(in_), _a(ident)
        s.op("pe", lambda e: e.transpose(o, i, d), [out], [in_, ident])

    def act(s, out, in_, func, bias=0.0, scale=1.0, accum=None):
        o, i, b, sc, ac = _a(out), _a(in_), _a(bias), _a(scale), _a(accum)
        if ac is None:
            s.op("act", lambda e: e.activation(out=o, in_=i, func=func, bias=b, scale=sc), [out], [in_, bias, scale])
        else:
            s.op("act", lambda e: e.activation(out=o, in_=i, func=func, bias=b, scale=sc, accum_out=ac),
                 [out, accum], [in_, bias, scale])

    def tt(s, eng, out, a, b, op):
        o, x, y = _a(out), _a(a), _a(b)
        s.op(eng, lambda e: e.tensor_tensor(out=o, in0=x, in1=y, op=op), [out], [a, b])

    def ts(s, eng, out, a, s1, s2, op0, op1=None):
        o, x, c1, c2 = _a(out), _a(a), _a(s1), _a(s2)
        if op1 is None:
            s.op(eng, lambda e: e.tensor_scalar(out=o, in0=x, scalar1=c1, scalar2=None, op0=op0), [out], [a, s1])
        else:
            s.op(eng, lambda e: e.tensor_scalar(out=o, in0=x, scalar1=c1, scalar2=c2, op0=op0, op1=op1),
                 [out], [a, s1, s2])

    def stt(s, eng, out, a, sc, b, op0, op1):
        o, x, c, y = _a(out), _a(a), _a(sc), _a(b)
        s.op(eng, lambda e: e.scalar_tensor_tensor(out=o, in0=x, scalar=c, in1=y, op0=op0, op1=op1),
             [out], [a, sc, b])

    def cp(s, eng, out, a):
        o, x = _a(out), _a(a)
        if eng == "act":
            s.op("act", lambda e: e.activation(out=o, in_=x, func=AF.Copy), [out], [a])
        else:
            s.op(eng, lambda e: e.tensor_copy(out=o, in_=x), [out], [a])

    def memset(s, eng, out, val):
        o = _a(out)
        s.op(eng, lambda e: e.memset(o, val), [out], [])

    def recip(s, out, a):
        o, x = _a(out), _a(a)
        s.op("dve", lambda e: e.reciprocal(out=o, in_=x), [out], [a])

    def rsum(s, out, a):
        o, x = _a(out), _a(a)
        s.op("dve", lambda e: e.tensor_reduce(out=o, in_=x, axis=AX.X, op=ALU.add), [out], [a])


def build(NB=2, NL=2048, NCX=256, DEPTH=4, dbg=False):
    TS = NL + NCX
    T = NB * TS
    NC3 = NB + 1
    NKC = TS // 128
    nc = bass.Bass("TRN2", target_bir_lowering=False)

    def din(name, shape, dt=F32):
        return nc.dram_tensor(name, list(shape), dt, kind="ExternalInput").ap()

    skind = "ExternalOutput" if dbg else "Internal"

    def dscr(name, shape, dt):
        return nc.dram_tensor(name, list(shape), dt, kind=skind).ap()

    xin = din("xT", [D, T])
    cvec_in = din("cvec", [128, 8 * NC3])
    pvec_in = din("pvec", [DEPTH, 128, NPV])
    gvec_in = din("gvec", [128, 8])
    consts_in = din("consts", [128, 6 * 128])
    rope_in = din("rope", [128, 2 * NL])
    ada_w = din("ada_w", [DEPTH, D, 6 * D])
    w_in = din("w_in", [DEPTH, D, IN_DIM])
    w_ao = din("w_attn_o", [DEPTH, D, D])
    w_so = din("w_ssd_o", [DEPTH, 2 * D, D])
    w_o = din("w_out", [DEPTH, D, D])
    w_1 = din("w_mlp1", [DEPTH, D, 4 * D])
    w_2 = din("w_mlp2", [DEPTH, 4 * D, D])
    outT = nc.dram_tensor("outT", [D, NB * NL], F32, kind="ExternalOutput").ap()

    xres = dscr("xres", [D, T], F32)
    qT = dscr("qT", [D, T], BF16)
    kT = dscr("kT", [D, T], BF16)
    vaug = dscr("vaug", [T, 8, 129], BF16)
    szT = dscr("szT", [2 * D, T], BF16)
    xbcT = dscr("xbcT", [4 * D, T], BF16)
    xbtok = dscr("xbtok", [T, 3 * D], BF16)
    dtD = dscr("dtD", [T, 64], F32)
    sgT = dscr("sgT", [2 * D, T], BF16)
    attT = dscr("attT", [D, T], BF16)
    yT = dscr("yT", [2 * D, T], F32)
    gT = dscr("gT", [2 * D, T], BF16)
    uT = dscr("uT", [4 * D, T], BF16)

    def fm(ap):
        return ap.rearrange("(k p) t -> p k t", p=128)

    seqs = []
    for b in range(NB):
        seqs.append((b, 0, b * TS, NL))
        seqs.append((b, 1, b * TS + NL, NCX))
    tiles = []
    for si, (b, ic, so, sl) in enumerate(seqs):
        for o in range(0, sl, 512):
            n = min(512, sl - o)
            tiles.append((b, ic, so + o, n, o, si, o + n >= sl))

    with ExitStack() as stack:
        k = KB(nc, stack)
        k.persist = True
        cst_f = k.alloc([6, 128], F32, "cst_f")
        cst = k.alloc([6, 128], BF16, "cst")
        pv = k.alloc([DEPTH, NPV], F32, "pv")
        modT = k.alloc([DEPTH, 48, NC3], F32, "modT")
        gmA = k.alloc([DEPTH, 8, NC3], F32, "gmA")
        gmF = k.alloc([DEPTH, 8, NC3], F32, "gmF")
        aneg = k.alloc([DEPTH, 64], F32, "aneg")
        nlam = k.alloc([DEPTH, 1], F32, "nlam")
        subg = k.alloc([DEPTH, 128], F32, "subg")
        fg = k.alloc([8], F32, "fg")
        cact_f = k.alloc([8, NC3], F32, "cact_f")
        cact = k.alloc([8, NC3], BF16, "cact")
        sml = k.alloc([16], F32, "sml")
        k.dma(cst_f.v(), consts_in.rearrange("p (a b) -> p a b", a=6))
        k.dma(fg.v(), gvec_in)
        k.dma(cact_f.v(), cvec_in.rearrange("p (a b) -> p a b", a=8))
        for l in range(DEPTH):
            k.dma(pv[:, l, :], pvec_in[l])
        k.cp("dve", cst.v(), cst_f.v())
        k.act(cact_f.v(), cact_f.v(), AF.Silu)
        k.cp("dve", cact.v(), cact_f.v())
        IDN, LE, GT, GE, LT, ONE = (cst[:, i, :] for i in range(6))
        k.persist = False
        k.base = k.off

        def pvs(l, name):
            a, b = PV[name]
            return pv[:, l, a:b]

        for l in range(DEPTH):
            wk = [k.ring("adaw", 8, [6 * D], BF16) for _ in range(8)]
            for kk in range(8):
                k.dma(wk[kk].v(), ada_w[l, kk * 128:(kk + 1) * 128, :], queue="pool")
            ps = k.ps([0, 1], "m")
            for j in range(48):
                for kk in range(8):
                    k.mm(ps[:, j * NC3:(j + 1) * NC3], wk[kk][:, j * 128:(j + 1) * 128], cact[:, kk, :],
                         start=(kk == 0), stop=(kk == 7))
            k.tt("dve", modT[:, l, :, :], ps[:, 0:48 * NC3].re("p (j c) -> p j c", c=NC3),
                 pvs(l, "adab").un(2).bc([128, 48, NC3]), ALU.add)
            for m in (1, 4):
                k.ts("dve", modT[:, l, 8 * m:8 * m + 8, :], modT[:, l, 8 * m:8 * m + 8, :], 1.0, None, ALU.add)
            k.tt("dve", gmA[:, l, :, :], modT[:, l, 8:16, :], pvs(l, "gmix").un(2).bc([128, 8, NC3]), ALU.mult)
            k.tt("dve", gmF[:, l, :, :], modT[:, l, 32:40, :], pvs(l, "gmlp").un(2).bc([128, 8, NC3]), ALU.mult)
            k.act(aneg[:, l, :], pvs(l, "alog"), AF.Exp)
            k.ts("dve", aneg[:, l, :], aneg[:, l, :], -1.0, None, ALU.mult)
            lam_init = 0.8 - 0.6 * math.exp(-0.3 * l)
            tmpl = k.ring("lamt", 2, [64], F32)
            k.tt("dve", tmpl.v(), pvs(l, "lq1"), pvs(l, "lk1"), ALU.mult)
            k.rsum(sml[:, 0:1], tmpl.v())
            tmpl2 = k.ring("lamt", 2, [64], F32)
            k.tt("dve", tmpl2.v(), pvs(l, "lq2"), pvs(l, "lk2"), ALU.mult)
            k.rsum(sml[:, 1:2], tmpl2.v())
            k.act(sml[:, 2:4], sml[:, 0:2], AF.Exp)
            k.tt("dve", sml[:, 4:5], sml[:, 3:4], sml[:, 2:3], ALU.subtract)
            k.ts("dve", nlam[:, l, :], sml[:, 4:5], -lam_init, None, ALU.add)
            k.ts("dve", subg[:, l, :], pvs(l, "subg"), 1.0 - lam_init, None, ALU.mult)
        k.end_phase()

        def modulate(src, hts, gm, l, shift_m, tl):
            for ti, (b, ic, off, n, pos0, si, last) in enumerate(tl):
                col = NB if ic else b
                xt = k.ring("mod_x", 2, [8, 512], F32)
                k.dma(xt[:, :, :n], fm(src)[:, :, off:off + n])
                sq = k.ring("mod_sq", 2, [8, 512], BF16)
                k.act(sq[:, :, :n], xt[:, :, :n], AF.Square)
                ps = k.ps([0, 1], "mod")
                for j in range(8):
                    k.mm(ps[:, :n], ONE, sq[:, j, :n], start=(j == 0), stop=(j == 7))
                rstd = k.ring("mod_r", 2, [512], F32)
                k.ts("dve", rstd[:, :n], ps[:, :n], 1.0 / D, 1e-6, ALU.mult, ALU.add)
                k.act(rstd[:, :n], rstd[:, :n], AF.Sqrt)
                k.recip(rstd[:, :n], rstd[:, :n])
                xn = k.ring("mod_xn", 2, [8, 512], F32)
                k.tt("dve", xn[:, :, :n], xt[:, :, :n], rstd[:, :n].un(1).bc([128, 8, n]), ALU.mult)
                for j in range(8):
                    k.act(hts[ti][:, j, :n], xn[:, j, :n], AF.Identity,
                          bias=modT[:, l, 8 * shift_m + j, col:col + 1], scale=gm[:, l, j, col:col + 1])

        for l in range(DEPTH):
            last_layer = (l == DEPTH - 1)
            xsrc = xin if l == 0 else xres
            act_tiles = [t for t in tiles if not (last_layer and t[1])]
            hts = [k.alloc([8, tiles[i][3]], BF16, "hT%d" % i) for i in range(len(tiles))]
            mark = k.off
            modulate(xsrc, hts, gmA, l, 0, tiles)
            k.barrier()
            k.off = mark
            k.rot = {}
            rope = k.alloc([2, NL], F32, "rope")
            k.dma(rope.v(), rope_in.rearrange("p (a b) -> p a b", a=2))
            vst = [k.alloc([4, 129], BF16, "vst%d" % i) for i in range(3)]
            for t_ in vst:
                k.memset("pool", t_[:, :, 128:129], 1.0)
            vst_i = 0
            dq = []
            dgj = [-1, None]

            def defer(fn, delay):
                dq.append([delay, fn])

            def tick():
                ready = []
                for it in list(dq):
                    it[0] -= 1
                    if it[0] <= 0:
                        ready.append(it)
                        dq.remove(it)
                for it in ready:
                    it[1]()
            segs = (("q", 0, 1024), ("k", 1024, 2048), ("v", 2048, 3072), ("z", 3072, 5120),
                    ("xbc", 5120, 9216), ("dt", 9216, 9280), ("ga", 9280, 10304), ("gs", 10304, 11328))
            wfm = w_in[l].rearrange("(k p) c -> p k c", p=128)
            for sname, s0, s1 in segs:
                for c0 in range(s0, s1, 512):
                    cw_ = min(512, s1 - c0)
                    wb = k.ring("wb", 2, [8, 512], BF16)
                    k.dma(wb[:, :, :cw_], wfm[:, :, c0:c0 + cw_], queue="pool")
                    if sname == "v":
                        vb = (c0 - s0) // 512
                        for ti, (b, ic, off, n, pos0, si, last) in enumerate(tiles):
                            for tb in range(n // 128):
                                ps = k.ps([0, 1, 2, 3, 4, 5], "b")
                                for kk in range(8):
                                    k.mm(ps[:, :], hts[ti][:, kk, tb * 128:(tb + 1) * 128], wb[:, kk, :],
                                         start=(kk == 0), stop=(kk == 7))
                                vs_ = vst[vst_i % 3]; vst_i += 1
                                k.act(vs_[:, :, 0:128], ps[:, :].re("p (h e) -> p h e", h=4), AF.Copy)
                                t0 = off + tb * 128
                                k.dma(vaug[t0:t0 + 128, vb * 4:(vb + 1) * 4, :], vs_.v())
                        continue
                    if sname == "dt":
                        for ti, (b, ic, off, n, pos0, si, last) in enumerate(tiles):
                            nb_ = n // 128
                            ps = k.ps([0, 1, 2, 3, 4, 5], "b")
                            for tb in range(nb_):
                                for kk in range(8):
                                    k.mm(ps[:, tb * 64:(tb + 1) * 64], hts[ti][:, kk, tb * 128:(tb + 1) * 128],
                                         wb[:, kk, :64], start=(kk == 0), stop=(kk == 7))
                            d1 = k.ring("dt1", 2, [4, 64], F32)
                            k.tt("dve", d1[:, :nb_, :], ps[:, :nb_ * 64].re("p (a b) -> p a b", b=64),
                                 pvs(l, "dtb").un(1).bc([128, nb_, 64]), ALU.add)
                            k.act(d1[:, :nb_, :], d1[:, :nb_, :], AF.Exp)
                            d2 = k.ring("dt2", 2, [4, 64], F32)
                            k.act(d2[:, :nb_, :], d1[:, :nb_, :], AF.Ln, bias=1.0)
                            k.dma(dtD[off:off + n, :].rearrange("(a p) c -> p a c", p=128), d2[:, :nb_, :])
                        continue
                    for jc in range(cw_ // 128):
                        gcol = c0 + jc * 128
                        stage = None
                        for ti, (b, ic, off, n, pos0, si, last) in enumerate(tiles):
                            ps = k.ps([0, 1, 2, 3, 4, 5], "b")
                            for kk in range(8):
                                k.mm(ps[:, :n], wb[:, kk, jc * 128:(jc + 1) * 128], hts[ti][:, kk, :n],
                                     start=(kk == 0), stop=(kk == 7))
                            tick()
                            if sname in ("q", "k"):
                                h = (gcol - s0) // 128
                                dst = qT if sname == "q" else kT
                                scl = 0.125 if sname == "q" else 1.0
                                ob = k.ring("ob", 3, [512], BF16)
                                if ic:
                                    k.act(ob[:, :n], ps[:, :n], AF.Identity, scale=scl)
                                else:
                                    qf = k.ring("qf", 2, [512], F32)
                                    k.act(qf[:, :n], ps[:, :n], AF.Identity, scale=scl)
                                    A = k.ring("ropeA", 2, [512], F32)
                                    k.tt("pool", A[:, :n], qf[:, :n], rope[:, 0, pos0:pos0 + n], ALU.mult)
                                    Bt = k.ring("ropeB", 2, [512], F32)
                                    for g in range(4):
                                        gs_ = g ^ 1
                                        k.tt("dve", Bt[32 * g:32 * g + 32, :n], qf[32 * gs_:32 * gs_ + 32, :n],
                                             rope[32 * gs_:32 * gs_ + 32, 1, pos0:pos0 + n], ALU.mult)
                                    k.tt("pool", ob[:, :n], A[:, :n], Bt[:, :n], ALU.add)
                                k.dma(dst[h * 128:(h + 1) * 128, off:off + n], ob[:, :n])
                            elif sname == "z":
                                r0 = gcol - s0
                                ob = k.ring("ob", 3, [512], BF16)
                                k.act(ob[:, :n], ps[:, :n], AF.Silu)
                                k.dma(szT[r0:r0 + 128, off:off + n], ob[:, :n])
                            elif sname in ("ga", "gs"):
                                r0 = gcol - 9280
                                ob = k.ring("ob", 3, [512], BF16)
                                k.act(ob[:, :n], ps[:, :n], AF.Sigmoid)
                                k.dma(sgT[r0:r0 + 128, off:off + n], ob[:, :n])
                            else:
                                jx = (gcol - s0) // 128
                                sb_, sic, soff, slen = seqs[si]
                                a0, _ = PV["convw"]
                                if pos0 == 0:
                                    stage = k.ring("stage", 3, [NL + 4], BF16)
                                    k.memset("pool", stage[:, 0:2], 0.0)
                                    k.memset("pool", stage[:, 2 + slen:4 + slen], 0.0)
                                    stageB = k.ring("stageB", 3, [NL + 4], BF16)
                                    k.memset("pool", stageB[:, 0:2], 0.0)
                                    k.memset("pool", stageB[:, slen:4 + slen], 0.0)
                                    if dgj[0] != jx:
                                        dgj[0] = jx
                                        dgj[1] = k.ring("dg", 2, [5, 128], BF16)
                                        for kq in range(5):
                                            k.ts("dve", dgj[1][:, kq, :], IDN,
                                                 pv[:, l, a0 + jx * 5 + kq:a0 + jx * 5 + kq + 1], None, ALU.mult)
                                k.act(stage[:, 2 + pos0:2 + pos0 + n], ps[:, :n], AF.Copy)
                                k.act(stageB[:, 1 + pos0:1 + pos0 + n], ps[:, :n], AF.Copy)
                                if last:
                                    def conv_unit(stage=stage, stageB=stageB, dg=dgj[1], jx=jx, soff=soff, slen=slen):
                                        cvo = k.ring("cvo", 3, [NL], BF16)
                                        b0_, _ = PV["convb"]
                                        for o_ in range(0, slen, 512):
                                            nn = min(512, slen - o_)
                                            pc = k.ps([0, 1, 2, 3, 4, 5], "b")
                                            for kq in range(5):
                                                src_ = stage[:, o_ + kq:o_ + kq + nn] if kq % 2 == 0 else \
                                                    stageB[:, o_ + kq - 1:o_ + kq - 1 + nn]
                                                k.mm(pc[:, :nn], dg[:, kq, :], src_, start=(kq == 0), stop=(kq == 4))
                                            k.act(cvo[:, o_:o_ + nn], pc[:, :nn], AF.Silu,
                                                  bias=pv[:, l, b0_ + jx:b0_ + jx + 1])
                                        k.dma(xbcT[jx * 128:(jx + 1) * 128, soff:soff + slen], cvo[:, :slen])
                                        if jx < 24:
                                            def trans(cvo=cvo, jx=jx, soff=soff, slen=slen):
                                                for t4 in range(0, slen // 128, 4):
                                                    nq = min(4, slen // 128 - t4)
                                                    pst = k.ps([6, 7], "bt")
                                                    pv_ = V(pst, pst.ap.bitcast(BF16))
                                                    for q_ in range(nq):
                                                        k.tr(pv_[:, q_ * 128:(q_ + 1) * 128],
                                                             cvo[:, (t4 + q_) * 128:(t4 + q_ + 1) * 128], IDN)
                                                    tst = k.ring("tst", 2, [512], BF16)
                                                    k.cp("act", tst[:, :nq * 128], pv_[:, :nq * 128])
                                                    tk0 = soff + t4 * 128
                                                    k.dma(xbtok[tk0:tk0 + nq * 128, jx * 128:(jx + 1) * 128]
                                                          .rearrange("(q p) c -> p q c", p=128),
                                                          tst[:, :nq * 128].re("p (q c) -> p q c", c=128))
                                            defer(trans, 1)
                                    defer(conv_unit, 1)
            while dq:
                tick()
            k.end_phase()

            deferred = []
            deferred2 = []
            pending = []
            has_big = (NKC >= 12)

            def run_pending(i_):
                for it in list(pending):
                    if it[0] <= i_:
                        pending.remove(it)
                        it[1]()

            def flush_deferred():
                while deferred:
                    deferred.pop(0)()

            def flush_deferred2():
                while deferred2:
                    deferred2.pop(0)()
            for b in range(NB):
                kts = []
                for h in range(8):
                    t_ = k.ring("KT%d" % h, 1, [TS], BF16)
                    k.dma(t_.v(), kT[h * 128:(h + 1) * 128, b * TS:(b + 1) * TS])
                    kts.append(t_)
                va = k.ring("VA", 1, [NKC, 8, 129], BF16)
                k.dma(va.v(), vaug[b * TS:(b + 1) * TS].rearrange("(c p) h e -> p c h e", p=128))
                for h in range(8):
                    for (tb_, ic, off, n, pos0, si, last) in tiles:
                        if tb_ != b or (ic and last_layer):
                            continue
                        nqb = n // 128
                        qt = k.ring("QT", 2, [512], BF16)
                        k.dma(qt[:, :n], qT[h * 128:(h + 1) * 128, off:off + n])
                        kcs = list(range(NL // 128, NKC)) if ic else list(range(NKC))
                        obank = [k.psum[4], k.psum[5], k.psum[6]]

                        def oreg(c, qb):
                            r = c * 4 + qb
                            return obank[r // 3][:, (r % 3) * 129:(r % 3) * 129 + 129]

                        def qk(kc):
                            pss = k.ps2([0, 1], "s")
                            for c in range(2):
                                k.mm(pss[:, c * 512:c * 512 + n], kts[h][64 * c:64 * c + 64, kc * 128:(kc + 1) * 128],
                                     qt[64 * c:64 * c + 64, :n])
                            pt = k.ring("PT", 3, [2, 512], BF16)
                            k.act(pt[:, :, :n], pss[:, :].re("p (c q) -> p c q", c=2)[:, :, :n], AF.Exp)
                            return pt
                        started = set()
                        big_ = len(kcs) >= 12
                        pend = qk(kcs[0])
                        for i_, kc in enumerate(kcs):
                            pts = pend
                            if i_ + 1 < len(kcs):
                                pend = qk(kcs[i_ + 1])
                            if big_:
                                run_pending(i_)
                            for c in range(2):
                                for qb in range(nqb):
                                    r = c * 4 + qb
                                    st = (r // 3) not in started
                                    started.add(r // 3)
                                    k.mm(oreg(c, qb), pts[:, c, qb * 128:(qb + 1) * 128], va[:, kc, h, :],
                                         start=st, stop=(kc == kcs[-1]), skip=True)
                        if big_ or not has_big:
                            run_pending(10 ** 6)
                        osb = k.ring("osb", 3, [3, 387], F32)
                        for bi in sorted(started):
                            k.cp("dve", osb[:, bi, :], obank[bi][:, 0:387])

                        def osr(c, qb, osb=osb):
                            r = c * 4 + qb
                            return osb[:, r // 3, (r % 3) * 129:(r % 3) * 129 + 129]

                        def mk(osr=osr, n=n, nqb=nqb, h=h, off=off):
                            st_ = {}

                            def stageA():
                                rec = k.ring("rec", 3, [8], F32)
                                for c in range(2):
                                    for qb in range(nqb):
                                        k.recip(rec[:, c * 4 + qb:c * 4 + qb + 1], osr(c, qb)[:, 128:129])
                                k.ts("dve", rec[:, 4:8], rec[:, 4:8], nlam[:, l, :], None, ALU.mult)
                                os_ = []
                                for qb in range(nqb):
                                    t1 = k.ring("at1", 2, [128], F32)
                                    k.ts("dve", t1.v(), osr(0, qb)[:, 0:128], rec[:, qb:qb + 1], None, ALU.mult)
                                    o_ = k.ring("ao", 12, [128], F32)
                                    k.stt("dve", o_.v(), osr(1, qb)[:, 0:128], rec[:, 4 + qb:5 + qb], t1.v(),
                                          ALU.mult, ALU.add)
                                    os_.append(o_)
                                st_["os"] = os_

                            def stageB():
                                os_ = st_["os"]
                                ssq = k.ring("ssq", 3, [4], F32)
                                k.memset("dve", ssq.v(), 0.0)
                                for qb in range(nqb):
                                    junk = k.ring("ajunk", 2, [128], F32)
                                    k.act(junk.v(), os_[qb].v(), AF.Square, accum=ssq[:, qb:qb + 1])
                                rs = k.ring("ars", 3, [4], F32)
                                k.ts("dve", rs.v(), ssq.v(), 1.0 / 128, 1e-6, ALU.mult, ALU.add)
                                k.act(rs.v(), rs.v(), AF.Ln)
                                k.act(rs.v(), rs.v(), AF.Exp, scale=-0.5)
                                on = k.ring("aon", 3, [4, 128], BF16)
                                for qb in range(nqb):
                                    k.stt("dve", on[:, qb, :], os_[qb].v(), rs[:, qb:qb + 1], subg[:, l, :],
                                          ALU.mult, ALU.mult)
                                st_["on"] = on

                            def fin():
                                on = st_["on"]
                                pst = k.psum[7]
                                pv_ = V(pst, pst.ap.bitcast(BF16))
                                for qb in range(nqb):
                                    k.tr(pv_[:, qb * 128:(qb + 1) * 128], on[:, qb, :], IDN)
                                ob = k.ring("ob", 3, [512], BF16)
                                k.cp("act", ob[:, :n], pv_[:, :n])
                                k.dma(attT[h * 128:(h + 1) * 128, off:off + n], ob[:, :n])
                            return stageA, stageB, fin
                        sA, sB, sF = mk()
                        pending.append([2, sA])
                        pending.append([6, sB])
                        pending.append([10, sF])
            run_pending(10 ** 6)
            k.end_phase()

            d0, _ = PV["ssdd"]
            LA = 2
            for b in range(NB):
                for dr in range(2):
                    Ms, Um, Vm, Dm = (GT, LE, LE, GT) if dr == 0 else (LT, GE, GE, LT)
                    lastcol = 127 if dr == 0 else 0
                    STp = [k.ring("STp%d" % g, 1, [512], F32) for g in range(4)]
                    STbp = [k.ring("STbp%d" % g, 1, [8, 128], BF16) for g in range(4)]
                    for g in range(4):
                        k.memset("pool", STp[g].v(), 0.0)
                        k.memset("pool", STbp[g].v(), 0.0)
                    cl = [(1, ci) for ci in range(NCX // 128)] + [(0, ci) for ci in range(NL // 128)]
                    if dr == 1:
                        cl = [(1, ci) for ci in reversed(range(NCX // 128))] + \
                             [(0, ci) for ci in reversed(range(NL // 128))]

                    def chunk_pre(idx, b=b, dr=dr, cl=cl, Dm=Dm, Um=Um):
                        ic, ci = cl[idx]
                        C = {}
                        tok0 = b * TS + (NL if ic else 0) + ci * 128
                        C["tok0"] = tok0
                        xs_tok = k.ring("xs_tok", 2, [32, 64], BF16)
                        k.dma(xs_tok.v(), xbtok[tok0:tok0 + 128, 0:2048].rearrange("p (h e) -> p h e", e=64))
                        B_tok = k.ring("B_tok", 2, [1024], BF16)
                        k.dma(B_tok.v(), xbtok[tok0:tok0 + 128, 2048:3072])
                        BT = k.ring("BT", 2, [8, 128], BF16)
                        k.dma(BT.v(), fm(xbcT[2048:3072, :])[:, :, tok0:tok0 + 128])
                        CT = k.ring("CT", 2, [8, 128], BF16)
                        k.dma(CT.v(), fm(xbcT[3072:4096, :])[:, :, tok0:tok0 + 128])
                        dt = k.ring("dt", 2, [32], F32)
                        k.dma(dt.v(), dtD[tok0:tok0 + 128, dr * 32:(dr + 1) * 32])
                        C.update(B_tok=B_tok, BT=BT, CT=CT)
                        if dr == 1:
                            yf = k.ring("yf", 2, [16, 128], F32)
                            k.dma(yf.v(), fm(yT)[:, :, tok0:tok0 + 128])
                            xsT = k.ring("xsT", 2, [16, 128], BF16)
                            k.dma(xsT.v(), fm(xbcT[0:2048, :])[:, :, tok0:tok0 + 128])
                            szt = k.ring("szt", 2, [16, 128], BF16)
                            k.dma(szt.v(), fm(szT)[:, :, tok0:tok0 + 128])
                            xsD = k.ring("xsD", 1, [16, 128], F32)
                            k.tt("pool", xsD.v(), xsT.v(), pv[:, l, d0:d0 + 16].un(2).bc([128, 16, 128]), ALU.mult)
                            k.tt("pool", yf.v(), yf.v(), xsD.v(), ALU.add)
                            C.update(yf=yf, szt=szt)
                        dta = k.ring("dta", 2, [32], F32)
                        k.tt("dve", dta.v(), dt.v(), aneg[:, l, dr * 32:(dr + 1) * 32], ALU.mult)
                        dhi = k.ring("dhi", 2, [32], BF16)
                        k.cp("dve", dhi.v(), dta.v())
                        dlf = k.ring("dlf", 2, [32], F32)
                        k.tt("dve", dlf.v(), dta.v(), dhi.v(), ALU.subtract)
                        dlo = k.ring("dlo", 2, [32], BF16)
                        k.cp("dve", dlo.v(), dlf.v())
                        psd = k.psum[4]
                        k.mm(psd[:, 256:288], Dm, dhi.v(), start=True, stop=False)
                        k.mm(psd[:, 256:288], Dm, dlo.v(), start=False, stop=True)
                        dte = k.ring("dte", 2, [32], F32)
                        k.act(dte.v(), psd[:, 256:288], AF.Exp)
                        xcp = k.ring("xcp", 2, [32, 128], BF16)
                        if k.rot["xcp"][1] <= 2:
                            k.memset("pool", xcp.v(), 0.0)
                        xcv = xcp.v().re("p (a two) c -> p a two c", two=2)
                        xsv = xs_tok.v().re("p (a two) e -> p a two e", two=2)
                        dtv = dt.v().re("p (a two) -> p a two", two=2)
                        for par in range(2):
                            k.tt("pool", xcv[:, :, par, par * 64:par * 64 + 64], xsv[:, :, par, :],
                                 dtv[:, :, par].un(2).bc([128, 16, 64]), ALU.mult)
                        xcd = k.ring("xcd", 2, [32, 64], BF16)
                        dev = dte.v().re("p (a two) -> p a two", two=2)
                        xdv = xcd.v().re("p (a two) e -> p a two e", two=2)
                        for par in range(2):
                            k.tt("pool", xdv[:, :, par, :], xcv[:, :, par, par * 64:par * 64 + 64],
                                 dev[:, :, par].un(2).bc([128, 16, 64]), ALU.mult)
                        rhi = k.ring("rhi", 2, [32, 128], BF16)
                        k.tt("dve", rhi.v(), dhi.v().un(2).bc([128, 32, 128]), Um.un(1).bc([128, 32, 128]), ALU.mult)
                        C.update(xcp=xcp, xcd=xcd, rhi=rhi)
                        return C

                    def stage1(C, gp, Ms=Ms, Vm=Vm):
                        rhi, BT, CT = C["rhi"], C["BT"], C["CT"]
                        g0 = 2 * gp
                        pseg = k.ps2([0, 1], "segcs")
                        for gi in range(2):
                            k.mm(pseg[:, gi * 512:(gi + 1) * 512], Ms,
                                 rhi[:, 4 * (g0 + gi):4 * (g0 + gi) + 4, :].re("p h l -> p (h l)"))
                        pcs = k.ps2([0, 1], "segcs")
                        for gi in range(2):
                            k.mm(pcs[:, gi * 512:(gi + 1) * 512], ONE,
                                 rhi[:, 4 * (g0 + gi):4 * (g0 + gi) + 4, :].re("p h l -> p (h l)"))
                        E = k.ring("E", LA + 2, [8, 128], BF16)
                        k.act(E.v().re("p h l -> p (h l)"), pseg[:, :], AF.Exp)
                        E0 = k.ring("E0", LA + 2, [8, 128], F32)
                        k.act(E0.v().re("p h l -> p (h l)"), pcs[:, :], AF.Exp)
                        pcb = k.psum[4]
                        for gi in range(2):
                            k.mm(pcb[:, gi * 128:(gi + 1) * 128], BT[:, g0 + gi, :], CT[:, g0 + gi, :])
                        cbm = k.ring("cbm", LA + 2, [2, 128], BF16)
                        k.tt("dve", cbm.v(), pcb[:, 0:256].re("p (a l) -> p a l", a=2),
                             Vm.un(1).bc([128, 2, 128]), ALU.mult)
                        MT = k.ring("MT", LA + 2, [2, 4, 128], BF16)
                        k.tt("dve", MT.v(), E.v().re("p (a h) l -> p a h l", a=2),
                             cbm.v().un(2).bc([128, 2, 4, 128]), ALU.mult)
                        MV = k.ring("MV", LA + 2, [2, 4, 128], BF16)
                        k.tt("pool", MV.v(), E0.v().re("p (a h) l -> p a h l", a=2),
                             CT[:, g0:g0 + 2, :].un(2).bc([128, 2, 4, 128]), ALU.mult)
                        return (E0, MT, MV)

                    def stage2(C, gp, S, dr=dr, lastcol=lastcol, STp=STp, STbp=STbp):
                        E0, MT, MV = S
                        xcp, xcd, B_tok, tok0 = C["xcp"], C["xcd"], C["B_tok"], C["tok0"]
                        g0 = 2 * gp
                        py = k.ps([5, 6, 7], "yst")
                        for gi in range(2):
                            for hp in range(2):
                                col = (gi * 2 + hp) * 128
                                for hh in range(2):
                                    hl = 2 * hp + hh
                                    k.mm(py[:, col:col + 128], xcp[:, 4 * (g0 + gi) + hl, :], MT[:, gi, hl, :],
                                         start=(hh == 0), stop=False)
                                for hh in range(2):
                                    hl = 2 * hp + hh
                                    k.mm(py[:, col:col + 128], STbp[gp][:, 4 * gi + hl, :], MV[:, gi, hl, :],
                                         start=False, stop=(hh == 1))
                        rows = yT[4 * gp * 128:(4 * gp + 4) * 128, tok0:tok0 + 128]
                        if dr == 0:
                            yst = k.ring("yst", 3, [4, 128], F32)
                            k.cp("act", yst.v().re("p a t -> p (a t)"), py[:, 0:512])
                            k.dma(rows.rearrange("(a p) t -> p a t", p=128), yst.v())
                        else:
                            yf, szt = C["yf"], C["szt"]
                            gs1 = k.ring("gs1", 2, [4, 128], F32)
                            k.tt("dve", gs1.v(), py[:, 0:512].re("p (a t) -> p a t", a=4),
                                 yf[:, 4 * gp:4 * gp + 4, :], ALU.add)
                            go = k.ring("go", 3, [4, 128], BF16)
                            k.tt("pool", go.v(), gs1.v(), szt[:, 4 * gp:4 * gp + 4, :], ALU.mult)
                            k.dma(gT[4 * gp * 128:(4 * gp + 4) * 128, tok0:tok0 + 128]
                                  .rearrange("(a p) t -> p a t", p=128), go.v())
                        pst_ = k.ps([5, 6, 7], "yst")
                        for gi in range(2):
                            k.mm(pst_[:, gi * 256:(gi + 1) * 256], B_tok[:, (g0 + gi) * 128:(g0 + gi + 1) * 128],
                                 xcd[:, 4 * (g0 + gi):4 * (g0 + gi) + 4, :].re("p h e -> p (h e)"))
                        tmp = k.ring("sttmp", 2, [8, 64], F32)
                        k.tt("dve", tmp.v(), STp[gp].v().re("p (h e) -> p h e", e=64),
                             E0[:, :, lastcol:lastcol + 1].bc([128, 8, 64]), ALU.mult)
                        k.tt("dve", STp[gp].v(), tmp.v().re("p h e -> p (h e)"), pst_[:, 0:512], ALU.add)
                        sgv = STp[gp].v().re("p (a two e) -> p a two e", two=2, e=64)
                        sbv = STbp[gp].v().re("p (a two) c -> p a two c", two=2)
                        for par in range(2):
                            k.cp("act", sbv[:, :, par, par * 64:par * 64 + 64], sgv[:, :, par, :])

                    Cs = {}

                    def get_C(idx):
                        if idx not in Cs:
                            Cs[idx] = chunk_pre(idx)
                        return Cs[idx]
                    work = [(idx, gp) for idx in range(len(cl)) for gp in range(4)]
                    s1 = {}
                    for i_ in range(len(work) + LA):
                        if i_ < len(work):
                            idx, gp = work[i_]
                            C_ = get_C(idx)
                            if gp == 2 and idx + 1 < len(cl):
                                get_C(idx + 1)
                            s1[i_] = stage1(C_, gp)
                        j_ = i_ - LA
                        if j_ >= 0:
                            idx, gp = work[j_]
                            stage2(get_C(idx), gp, s1.pop(j_))
                            if gp == 3:
                                Cs.pop(idx)
            k.end_phase()

            wA = k.alloc([8, D], BF16, "wA")
            wS = k.alloc([16, D], BF16, "wS")
            wO = k.alloc([8, D], BF16, "wO")
            for kk in range(8):
                k.dma(wA[:, kk, :], w_ao[l, kk * 128:(kk + 1) * 128, :], queue="pool")
            for kk in range(16):
                k.dma(wS[:, kk, :], w_so[l, kk * 128:(kk + 1) * 128, :], queue="pool")
            for kk in range(8):
                k.dma(wO[:, kk, :], w_o[l, kk * 128:(kk + 1) * 128, :], queue="pool")
            g0, _ = PV["ssdg"]
            for kk in range(16):
                k.ts("dve", wS[:, kk, :], wS[:, kk, :], pv[:, l, g0 + kk:g0 + kk + 1], None, ALU.mult)
            for (b, ic, off, n, pos0, si, last) in act_tiles:
                col = NB if ic else b
                aT = k.ring("aT", 1, [8, 512], BF16)
                k.dma(aT[:, :, :n], fm(attT)[:, :, off:off + n])
                gt = k.ring("gTt", 1, [16, 512], BF16)
                k.dma(gt[:, :, :n], fm(gT)[:, :, off:off + n])
                sg = k.ring("sg", 1, [16, 512], BF16)
                k.dma(sg[:, :, :n], fm(sgT)[:, :, off:off + n])
                xt = k.ring("xt", 1, [8, 512], F32)
                k.dma(xt[:, :, :n], fm(xsrc)[:, :, off:off + n])
                sq = k.ring("sq", 1, [16, 512], BF16)
                k.act(sq[:, :, :n], gt[:, :, :n], AF.Square)
                pss = k.psum[0]
                for j in range(16):
                    k.mm(pss[:, :n], ONE, sq[:, j, :n], start=(j == 0), stop=(j == 15))
                rstd = k.ring("rstd", 2, [512], F32)
                k.ts("dve", rstd[:, :n], pss[:, :n], 1.0 / (2 * D), 1e-6, ALU.mult, ALU.add)
                k.act(rstd[:, :n], rstd[:, :n], AF.Sqrt)
                k.recip(rstd[:, :n], rstd[:, :n])
                mT = k.ring("mT", 1, [8, 512], BF16)
                for dj in range(8):
                    psA = k.ps([1, 2], "A")
                    for kk in range(8):
                        k.mm(psA[:, :n], wA[:, kk, dj * 128:(dj + 1) * 128], aT[:, kk, :n],
                             start=(kk == 0), stop=(kk == 7))
                    psS = k.ps([3, 4], "S")
                    for kk in range(16):
                        k.mm(psS[:, :n], wS[:, kk, dj * 128:(dj + 1) * 128], gt[:, kk, :n],
                             start=(kk == 0), stop=(kk == 15))
                    t1 = k.ring("et1", 2, [512], F32)
                    k.tt("dve", t1[:, :n], psA[:, :n], sg[:, dj, :n], ALU.mult)
                    t2 = k.ring("et2", 2, [512], F32)
                    k.tt("dve", t2[:, :n], psS[:, :n], rstd[:, :n], ALU.mult)
                    t3 = k.ring("et3", 2, [512], F32)
                    k.tt("pool", t3[:, :n], t2[:, :n], sg[:, 8 + dj, :n], ALU.mult)
                    k.tt("pool", mT[:, dj, :n], t1[:, :n], t3[:, :n], ALU.add)
                for dj in range(8):
                    psO = k.ps([5, 6], "O")
                    for kk in range(8):
                        k.mm(psO[:, :n], wO[:, kk, dj * 128:(dj + 1) * 128], mT[:, kk, :n],
                             start=(kk == 0), stop=(kk == 7))
                    k.stt("dve", xt[:, dj, :n], psO[:, :n], modT[:, l, 16 + dj, col:col + 1], xt[:, dj, :n],
                          ALU.mult, ALU.add)
                k.dma(fm(xres)[:, :, off:off + n], xt[:, :, :n])
            k.end_phase()

            hts = [k.alloc([8, act_tiles[i][3]], BF16, "h2T%d" % i) for i in range(len(act_tiles))]
            mark = k.off
            modulate(xres, hts, gmF, l, 3, act_tiles)
            k.barrier()
            k.off = mark
            k.rot = {}
            w1fm = w_1[l].rearrange("(k p) c -> p k c", p=128)
            for c0 in range(0, 4 * D, 512):
                wb = k.ring("wb", 3, [8, 512], BF16)
                k.dma(wb.v(), w1fm[:, :, c0:c0 + 512], queue="pool")
                for jc in range(4):
                    r0 = c0 + jc * 128
                    for ti, (b, ic, off, n, pos0, si, last) in enumerate(act_tiles):
                        ps = k.ps([0, 1, 2, 3, 4, 5, 6, 7], "f1")
                        for kk in range(8):
                            k.mm(ps[:, :n], wb[:, kk, jc * 128:(jc + 1) * 128], hts[ti][:, kk, :n],
                                 start=(kk == 0), stop=(kk == 7))
                        r_ = k.ring("relu", 3, [512], BF16)
                        k.act(r_[:, :n], ps[:, :n], AF.Relu)
                        ob = k.ring("ob", 3, [512], BF16)
                        k.tt("pool", ob[:, :n], r_[:, :n], r_[:, :n], ALU.mult)
                        k.dma(uT[r0:r0 + 128, off:off + n], ob[:, :n])
            k.end_phase()

            w2 = k.alloc([32, D], BF16, "w2")
            for kk in range(32):
                k.dma(w2[:, kk, :], w_2[l, kk * 128:(kk + 1) * 128, :], queue="pool")
            for (b, ic, off, n, pos0, si, last) in act_tiles:
                col = NB if ic else b
                ut = k.ring("ut", 2, [32, 512], BF16)
                k.dma(ut[:, :, :n], fm(uT)[:, :, off:off + n])
                xt = k.ring("xt", 2, [8, 512], F32)
                k.dma(xt[:, :, :n], fm(xres)[:, :, off:off + n])
                for dj in range(8):
                    ps = k.ps([0, 1, 2, 3], "f2")
                    for kk in range(32):
                        k.mm(ps[:, :n], w2[:, kk, dj * 128:(dj + 1) * 128], ut[:, kk, :n],
                             start=(kk == 0), stop=(kk == 31))
                    k.stt("dve", xt[:, dj, :n], ps[:, :n], modT[:, l, 40 + dj, col:col + 1], xt[:, dj, :n],
                          ALU.mult, ALU.add)
                k.dma(fm(xres)[:, :, off:off + n], xt[:, :, :n])
            k.end_phase()

        for (b, ic, off, n, pos0, si, last) in tiles:
            if ic:
                continue
            xt = k.ring("xt", 2, [8, 512], F32)
            k.dma(xt[:, :, :n], fm(xres)[:, :, off:off + n])
            sq = k.ring("sq", 2, [8, 512], BF16)
            k.act(sq[:, :, :n], xt[:, :, :n], AF.Square)
            ps = k.ps([0, 1], "fin")
            for j in range(8):
                k.mm(ps[:, :n], ONE, sq[:, j, :n], start=(j == 0), stop=(j == 7))
            rstd = k.ring("rstd", 2, [512], F32)
            k.ts("dve", rstd[:, :n], ps[:, :n], 1.0 / D, 1e-6, ALU.mult, ALU.add)
            k.act(rstd[:, :n], rstd[:, :n], AF.Sqrt)
            k.recip(rstd[:, :n], rstd[:, :n])
            xo = k.ring("xo", 2, [8, 512], F32)
            k.tt("dve", xo[:, :, :n], xt[:, :, :n], rstd[:, :n].un(1).bc([128, 8, n]), ALU.mult)
            for j in range(8):
                k.ts("dve", xo[:, j, :n], xo[:, j, :n], fg[:, j:j + 1], None, ALU.mult)
            o0 = b * NL + pos0
            k.dma(fm(outT)[:, :, o0:o0 + n], xo[:, :, :n])
        k.end_phase()
        k.emit()
    return nc


def _fmaj(v, nchunk):
    return np.ascontiguousarray(np.asarray(v, np.float32).reshape(nchunk, 128).T)


def _consts():
    r = np.arange(128)[:, None]
    c = np.arange(128)[None, :]
    mats = [(r == c), (r <= c), (r > c), (r >= c), (r < c), np.ones((128, 128), bool)]
    return np.ascontiguousarray(np.concatenate([m.astype(np.float32) for m in mats], axis=1))


def _rope(NL, grid_w=64):
    rows = NL // grid_w
    row = np.broadcast_to(np.arange(rows)[:, None], (rows, grid_w)).reshape(-1).astype(np.float32)
    col = np.broadcast_to(np.arange(grid_w)[None, :], (rows, grid_w)).reshape(-1).astype(np.float32)
    inv = (np.float32(10000.0) ** (-np.arange(16, dtype=np.float32) / np.float32(16))).astype(np.float32)
    ang = np.concatenate([row[:, None] * inv, col[:, None] * inv], axis=-1).astype(np.float32)
    cos = np.cos(ang).astype(np.float32).T
    sin = np.sin(ang).astype(np.float32).T
    p = np.arange(128)
    cosT = cos[p % 32]
    sgn = np.where((p % 64) < 32, 1.0, -1.0).astype(np.float32)[:, None]
    sinS = sin[p % 32] * sgn
    return np.ascontiguousarray(np.concatenate([cosT, sinS], axis=1).astype(np.float32))


def host_inputs(inp, NB, NL, NCX, DEPTH, ncores):
    f = lambda a: np.asarray(a, np.float32)
    x, c, ctx, c_ctx = f(inp["x"]), f(inp["c"]), f(inp["ctx"]), f(inp["c_ctx"])
    pvec = np.zeros((DEPTH, 128, NPV), np.float32)

    def put(l, name, arr):
        a, b = PV[name]
        pvec[l, :, a:b] = arr
    for l in range(DEPTH):
        put(l, "gmix", _fmaj(inp["norm_mix_g"][l], 8))
        put(l, "gmlp", _fmaj(inp["norm_mlp_g"][l], 8))
        cw = f(inp["conv_w"][l])
        put(l, "convw", cw.T.reshape(32, 128, 5).transpose(1, 0, 2).reshape(128, 160))
        put(l, "convb", _fmaj(inp["conv_b"][l], 32))
        put(l, "dtb", np.broadcast_to(np.concatenate([f(inp["dt_bias_f"][l]), f(inp["dt_bias_b"][l])])[None], (128, 64)))
        put(l, "alog", np.broadcast_to(np.concatenate([f(inp["a_log_f"][l]), f(inp["a_log_b"][l])])[None], (128, 64)))
        put(l, "ssdd", _fmaj(np.repeat(f(inp["ssd_d"][l]), 64), 16))
        put(l, "ssdg", _fmaj(inp["ssd_norm_g"][l], 16))
        put(l, "subg", np.broadcast_to(f(inp["attn_subln_g"][l])[None], (128, 128)))
        for nm, key in (("lq1", "lambda_q1"), ("lk1", "lambda_k1"), ("lq2", "lambda_q2"), ("lk2", "lambda_k2")):
            put(l, nm, np.broadcast_to(f(inp[key][l])[None], (128, 64)))
        put(l, "adab", _fmaj(inp["ada_b"][l], 48))
    shared = {
        "pvec": pvec, "gvec": _fmaj(inp["final_norm_g"], 8), "consts": _consts(), "rope": _rope(NL),
        "ada_w": f(inp["ada_w"]), "w_in": f(inp["w_in"]), "w_attn_o": f(inp["w_attn_o"]),
        "w_ssd_o": f(inp["w_ssd_o"]), "w_out": f(inp["w_out"]), "w_mlp1": f(inp["w_mlp1"]),
        "w_mlp2": f(inp["w_mlp2"]),
    }
    maps = []
    for ci in range(ncores):
        bs = list(range(ci * NB, (ci + 1) * NB))
        toks = np.concatenate([np.concatenate([x[b], ctx[b]], axis=0) for b in bs], axis=0)
        cv = np.stack([c[b] for b in bs] + [c_ctx], axis=1)
        cvec = cv.reshape(8, 128, NB + 1).transpose(1, 0, 2).reshape(128, 8 * (NB + 1))
        m = dict(shared)
        m["xT"] = np.ascontiguousarray(toks.T)
        m["cvec"] = np.ascontiguousarray(cvec)
        maps.append(m)
    return maps


_NC_CACHE = {}


def kernel(**inputs):
    x = np.asarray(inputs["x"])
    B, NL, _ = x.shape
    NCX = np.asarray(inputs["ctx"]).shape[1]
    DEPTH = np.asarray(inputs["w_in"]).shape[0]
    ncores = 8
    NB = B // ncores
    key = (NB, NL, NCX, DEPTH)
    if key not in _NC_CACHE:
        _NC_CACHE[key] = build(NB, NL, NCX, DEPTH)
    nc = _NC_CACHE[key]
    maps = host_inputs(inputs, NB, NL, NCX, DEPTH, ncores)
    res = run_bass_kernel_spmd(nc, maps, core_ids=list(range(ncores)))
    out = np.empty((B, NL, D), np.float32)
    for ci in range(ncores):
        oT = np.asarray(res.results[ci]["outT"])
        for j in range(NB):
            out[ci * NB + j] = oT[:, j * NL:(j + 1) * NL].T
    return out
```
